# Optimizing a Trainium2 kernel written in Bass

```python
import jax, jax.numpy as jnp
from jax import lax
import numpy as np

D_MODEL = 2048
BATCH = 8
SEQ = 2048
DEPTH = 1

CHUNK = 64
Q_BLOCK = 128
N_MEM = 256
EPS = 1e-6

MLA_HEADS = 8
MLA_NOPE = 128
MLA_ROPE = 64
MLA_QK = MLA_NOPE + MLA_ROPE
MLA_V = 128
MLA_Q_RANK = 512
MLA_KV_RANK = 256
ROPE_THETA = 10000.0

GLA_HEADS = 4
GLA_DK = 128
GLA_DV = 256
GLA_GATE_RANK = 16
GLA_TAU = 16.0

MIX_WIDTH = MLA_HEADS * MLA_V + GLA_HEADS * GLA_DV

MEM_HEADS = 4
MEM_HEAD_DIM = 128
MEM_WIDTH = MEM_HEADS * MEM_HEAD_DIM

D_FF = 5632

IN_SIZES = [
    MLA_Q_RANK,
    MLA_KV_RANK,
    MLA_ROPE,
    GLA_HEADS * GLA_DK,
    GLA_HEADS * GLA_DK,
    GLA_HEADS * GLA_DV,
    GLA_GATE_RANK,
    GLA_HEADS * GLA_DV,
]
IN_WIDTH = int(sum(IN_SIZES))
IN_SPLITS = [int(s) for s in np.cumsum(IN_SIZES)[:-1]]

kernel_name = "hybrid_mla_gla_macaron_memory_block"


def rmsnorm(x, g):
    xf = x.astype(jnp.float32)
    y = xf * lax.rsqrt(jnp.mean(xf * xf, axis=-1, keepdims=True) + EPS)
    return (y * g.astype(jnp.float32)).astype(x.dtype)


def swiglu(h, w_gate, w_up, w_down):
    return (jax.nn.silu(h @ w_gate) * (h @ w_up)) @ w_down


def rope(x, positions):
    half = x.shape[-1] // 2
    inv_freq = ROPE_THETA ** (-jnp.arange(half, dtype=jnp.float32) / half)
    ang = positions.astype(jnp.float32)[..., None] * inv_freq
    cos = jnp.cos(ang)[:, :, None, :]
    sin = jnp.sin(ang)[:, :, None, :]
    xf = x.astype(jnp.float32)
    x1, x2 = xf[..., :half], xf[..., half:]
    return jnp.concatenate([x1 * cos - x2 * sin, x2 * cos + x1 * sin], axis=-1).astype(x.dtype)


def chunk_causal_attention(q, k, v):
    B, S, H, Dk = q.shape
    Dv = v.shape[-1]
    n_blk = S // Q_BLOCK
    scale = Dk ** -0.5
    k_chunk = jnp.arange(S) // CHUNK
    q_blocks = q.reshape(B, n_blk, Q_BLOCK, H, Dk).transpose(1, 0, 2, 3, 4)

    def one_block(args):
        q_blk, blk = args
        s = jnp.einsum('bqhd,bkhd->bhqk', q_blk, k).astype(jnp.float32) * scale
        q_chunk = (blk * Q_BLOCK + jnp.arange(Q_BLOCK)) // CHUNK
        mask = k_chunk[None, :] <= q_chunk[:, None]
        s = jnp.where(mask[None, None], s, -jnp.inf)
        p = jax.nn.softmax(s, axis=-1).astype(v.dtype)
        return jnp.einsum('bhqk,bkhd->bqhd', p, v)

    out = lax.map(one_block, (q_blocks, jnp.arange(n_blk)))
    return out.transpose(1, 0, 2, 3, 4).reshape(B, S, H, Dv)


def gla_chunked(q, k, v, log_a):
    B, S, H, K = q.shape
    V = v.shape[-1]
    n_chunk = S // CHUNK
    f32 = jnp.float32
    qc = q.astype(f32).reshape(B, n_chunk, CHUNK, H, K) * (K ** -0.5)
    kc = k.astype(f32).reshape(B, n_chunk, CHUNK, H, K)
    vc = v.astype(f32).reshape(B, n_chunk, CHUNK, H, V)
    g = log_a.astype(f32).reshape(B, n_chunk, CHUNK, H, K)
    b = jnp.cumsum(g, axis=2)
    b_end = b[:, :, -1]
    k_dec = kc * jnp.exp(b_end[:, :, None] - b)
    u = jnp.einsum('bnchk,bnchv->bnhkv', k_dec, vc)
    decay = jnp.exp(b_end)

    def step(state, inp):
        d, uc = inp
        state = d[..., None] * state + uc
        return state, state

    s0 = jnp.zeros((B, H, K, V), f32)
    _, states = lax.scan(step, s0, (decay.transpose(1, 0, 2, 3), u.transpose(1, 0, 2, 3, 4)))
    states = states.transpose(1, 0, 2, 3, 4)
    o = jnp.einsum('bnchk,bnhkv->bnchv', qc, states)
    return o.reshape(B, S, H, V).astype(v.dtype)


def memory_cross_attention(h, m, w_q, w_k, w_v, w_o, g_q, g_k):
    B, S, _ = h.shape
    M = m.shape[1]
    q = rmsnorm((h @ w_q).reshape(B, S, MEM_HEADS, MEM_HEAD_DIM), g_q)
    k = rmsnorm((m @ w_k).reshape(B, M, MEM_HEADS, MEM_HEAD_DIM), g_k)
    v = (m @ w_v).reshape(B, M, MEM_HEADS, MEM_HEAD_DIM)
    s = jnp.einsum('bqhd,bkhd->bhqk', q, k).astype(jnp.float32) * (MEM_HEAD_DIM ** -0.5)
    p = jax.nn.softmax(s, axis=-1).astype(v.dtype)
    o = jnp.einsum('bhqk,bkhd->bqhd', p, v).reshape(B, S, MEM_WIDTH)
    return o @ w_o


def setup_inputs(seed: int = 0) -> dict:
    key = jax.random.key(seed)
    keys = iter(jax.random.split(key, 40))
    f32 = jnp.float32

    def w(fan_in, fan_out):
        return jax.random.normal(next(keys), (DEPTH, fan_in, fan_out), f32) * fan_in ** -0.5

    def g(n):
        return 1.0 + 0.02 * jax.random.normal(next(keys), (DEPTH, n), f32)

    x = jax.random.normal(next(keys), (BATCH, SEQ, D_MODEL), f32)
    mem = jax.random.normal(next(keys), (BATCH, N_MEM, D_MODEL), f32)
    offset = jax.random.randint(next(keys), (BATCH, 1), 0, 64, dtype=jnp.int32) * CHUNK
    positions = (offset + jnp.arange(SEQ, dtype=jnp.int32)[None, :]).astype(jnp.int32)

    return {
        "x": x,
        "mem": mem,
        "positions": positions,
        "ffn1_norm": g(D_MODEL),
        "ffn1_w_gate": w(D_MODEL, D_FF),
        "ffn1_w_up": w(D_MODEL, D_FF),
        "ffn1_w_down": w(D_FF, D_MODEL),
        "mix_norm": g(D_MODEL),
        "w_in": w(D_MODEL, IN_WIDTH),
        "q_a_norm": g(MLA_Q_RANK),
        "w_q_up": w(MLA_Q_RANK, MLA_HEADS * MLA_QK),
        "kv_a_norm": g(MLA_KV_RANK),
        "w_kv_up": w(MLA_KV_RANK, MLA_HEADS * (MLA_NOPE + MLA_V)),
        "mla_q_norm": g(MLA_QK),
        "mla_k_norm": g(MLA_QK),
        "gla_w_gate2": w(GLA_GATE_RANK, GLA_HEADS * GLA_DK),
        "gla_b_gate": 0.1 * jax.random.normal(next(keys), (DEPTH, GLA_HEADS * GLA_DK), f32),
        "gla_out_norm": g(GLA_DV),
        "w_out": w(MIX_WIDTH, D_MODEL),
        "mem_attn_norm": g(D_MODEL),
        "mem_norm": g(D_MODEL),
        "mem_w_q": w(D_MODEL, MEM_WIDTH),
        "mem_w_k": w(D_MODEL, MEM_WIDTH),
        "mem_w_v": w(D_MODEL, MEM_WIDTH),
        "mem_w_o": w(MEM_WIDTH, D_MODEL),
        "mem_q_norm": g(MEM_HEAD_DIM),
        "mem_k_norm": g(MEM_HEAD_DIM),
        "ffn2_norm": g(D_MODEL),
        "ffn2_w_gate": w(D_MODEL, D_FF),
        "ffn2_w_up": w(D_MODEL, D_FF),
        "ffn2_w_down": w(D_FF, D_MODEL),
    }


def reference(x, mem, positions, ffn1_norm, ffn1_w_gate, ffn1_w_up, ffn1_w_down,
              mix_norm, w_in, q_a_norm, w_q_up, kv_a_norm, w_kv_up, mla_q_norm,
              mla_k_norm, gla_w_gate2, gla_b_gate, gla_out_norm, w_out,
              mem_attn_norm, mem_norm, mem_w_q, mem_w_k, mem_w_v, mem_w_o,
              mem_q_norm, mem_k_norm, ffn2_norm, ffn2_w_gate, ffn2_w_up, ffn2_w_down):
    B, S, _ = x.shape
    for l in range(DEPTH):
        x = x + 0.5 * swiglu(rmsnorm(x, ffn1_norm[l]), ffn1_w_gate[l], ffn1_w_up[l], ffn1_w_down[l])

        h = rmsnorm(x, mix_norm[l])
        z = h @ w_in[l]
        zq, zkv, zkr, gq, gk, gv, zg, zr = jnp.split(z, IN_SPLITS, axis=-1)

        q = (rmsnorm(zq, q_a_norm[l]) @ w_q_up[l]).reshape(B, S, MLA_HEADS, MLA_QK)
        kv = (rmsnorm(zkv, kv_a_norm[l]) @ w_kv_up[l]).reshape(B, S, MLA_HEADS, MLA_NOPE + MLA_V)
        k_nope, v = kv[..., :MLA_NOPE], kv[..., MLA_NOPE:]
        k_rope = jnp.broadcast_to(zkr[:, :, None, :], (B, S, MLA_HEADS, MLA_ROPE))
        k = jnp.concatenate([k_nope, k_rope], axis=-1)
        q = rmsnorm(q, mla_q_norm[l])
        k = rmsnorm(k, mla_k_norm[l])
        q = jnp.concatenate([q[..., :MLA_NOPE], rope(q[..., MLA_NOPE:], positions)], axis=-1)
        k = jnp.concatenate([k[..., :MLA_NOPE], rope(k[..., MLA_NOPE:], positions)], axis=-1)
        o_mla = chunk_causal_attention(q, k, v).reshape(B, S, MLA_HEADS * MLA_V)

        log_a = jax.nn.log_sigmoid((zg @ gla_w_gate2[l] + gla_b_gate[l]).astype(jnp.float32)) / GLA_TAU
        o_gla = gla_chunked(gq.reshape(B, S, GLA_HEADS, GLA_DK),
                            gk.reshape(B, S, GLA_HEADS, GLA_DK),
                            gv.reshape(B, S, GLA_HEADS, GLA_DV),
                            log_a.reshape(B, S, GLA_HEADS, GLA_DK))
        o_gla = rmsnorm(o_gla, gla_out_norm[l]).reshape(B, S, GLA_HEADS * GLA_DV) * jax.nn.silu(zr)

        x = x + jnp.concatenate([o_mla, o_gla], axis=-1) @ w_out[l]

        x = x + memory_cross_attention(rmsnorm(x, mem_attn_norm[l]), rmsnorm(mem, mem_norm[l]),
                                       mem_w_q[l], mem_w_k[l], mem_w_v[l], mem_w_o[l],
                                       mem_q_norm[l], mem_k_norm[l])

        x = x + 0.5 * swiglu(rmsnorm(x, ffn2_norm[l]), ffn2_w_gate[l], ffn2_w_up[l], ffn2_w_down[l])
    return x
```

```python
import numpy as np
import concourse.bass as bass
import concourse.mybir as mybir
from concourse.bass_utils import run_bass_kernel_spmd
from contextlib import ExitStack

F32 = mybir.dt.float32
BF16 = mybir.dt.bfloat16
I32 = mybir.dt.int32
AF = mybir.ActivationFunctionType
ALU = mybir.AluOpType

ENGINES = ("pe", "act", "dve", "pool", "sp")

D = 2048
S = 2048
DFF = 5632
NJ = DFF // 128
NQ = 4
JQ = NJ // NQ
GT = 512
NGRP = S // GT
EPS = 1e-6
O_ZQ, O_ZKV, O_ZKR, O_GQ, O_GK, O_GV, O_ZG, O_ZR = 0, 512, 768, 832, 1344, 1856, 2880, 2896
VW = 130
C1_2PI = 6.28125
C2_2PI = 2.0 * np.pi - 6.28125
PI = float(np.pi)


class Tile:
    __slots__ = ("name", "writers", "readers", "sem", "sem_count", "gen_deps", "psum")

    def __init__(self, name):
        self.name = name
        self.psum = False
        self.writers = []
        self.readers = []
        self.gen_deps = []
        self.sem = None
        self.sem_count = 0


def tiles(prefix, n):
    return [Tile(f"{prefix}{i}") for i in range(n)]


class Op:
    __slots__ = ("eng", "kind", "fn", "deps", "signals", "seq", "stage", "dma_sem", "dma_val")

    def __init__(self, eng, kind, fn, stage):
        self.eng = eng
        self.kind = kind
        self.fn = fn
        self.deps = []
        self.signals = False
        self.seq = None
        self.stage = stage
        self.dma_sem = None
        self.dma_val = None


class Prog:
    def __init__(self, nc):
        self.nc = nc
        self.es = ExitStack()
        self.stage = 0
        self.ops = {e: [] for e in ENGINES}
        self.esem = {}
        self.ecount = {e: 0 for e in ENGINES}
        for e in ("pe", "act", "dve", "pool"):
            self.esem[e] = self.es.enter_context(nc.semaphore("sem_" + e))
        self.seen = {e: {} for e in ENGINES}
        self.stage_dmas = {e: [] for e in ENGINES}
        self.n_ops = 0
        self.sem_pool = {"sw": [], "hw": []}
        self.stage_sem_tiles = []
        self.n_sem = 0

    def sbuf(self, name, shape, dt):
        return self.es.enter_context(self.nc.sbuf_tensor("sb_" + name, list(shape), dt))

    def psum(self, name, shape, dt):
        return self.es.enter_context(self.nc.psum_tensor(name, list(shape), dt))

    def new_sem(self, name):
        return self.es.enter_context(self.nc.semaphore(name))

    def _add_deps(self, op, reads, writes, join):
        cur = self.stage
        for t in reads:
            for w in t.writers:
                if w.stage == cur:
                    op.deps.append((w, True))
            if t.psum:
                for r in t.readers:
                    if r.stage == cur and r.eng != op.eng:
                        op.deps.append((r, False))
        for t in writes:
            if not join:
                gd = [w for w in t.writers if w.stage == cur] + [r for r in t.readers if r.stage == cur]
                t.gen_deps = gd
            else:
                gd = [d for d in t.gen_deps if d.stage == cur] + [r for r in t.readers if r.stage == cur]
            for d in gd:
                if d is not op:
                    op.deps.append((d, False))
        for t in reads:
            t.readers.append(op)
        for t in writes:
            if join:
                t.writers.append(op)
            else:
                t.writers = [op]
                t.readers = []

    def op(self, eng, fn, reads=(), writes=(), join=False):
        o = Op(eng, "compute", fn, self.stage)
        self._add_deps(o, reads, writes, join)
        self.ops[eng].append(o)
        self.n_ops += 1
        return o

    def dma(self, eng, out, in_, reads=(), writes=(), join=False, sem_tile=None):
        def fn(e, out=out, in_=in_):
            return e.dma_start(out=out, in_=in_)
        o = Op(eng, "dma", fn, self.stage)
        self._add_deps(o, reads, writes, join)
        st = sem_tile if sem_tile is not None else writes[0]
        if st.sem is None:
            pool = self.sem_pool["sw" if eng == "pool" else "hw"]
            if pool:
                st.sem, st.sem_count = pool.pop()
            else:
                self.n_sem += 1
                st.sem = self.new_sem(f"dsem{self.n_sem}")
                st.sem_count = 0
            self.stage_sem_tiles.append((st, "sw" if eng == "pool" else "hw"))
        st.sem_count += 16
        o.dma_sem = st.sem
        o.dma_val = st.sem_count
        self.ops[eng].append(o)
        self.stage_dmas[eng].append(o)
        self.n_ops += 1
        return o

    def end_stage(self):
        nc = self.nc
        for e in ENGINES:
            for o in self.ops[e]:
                for (p, raw) in o.deps:
                    if p.kind == "compute":
                        if p.eng != o.eng or p.eng != "pe" or o.kind == "dma":
                            p.signals = True
        for e in ("pe", "act", "dve", "pool"):
            c = self.ecount[e]
            for o in self.ops[e]:
                if o.kind == "compute" and o.signals:
                    c += 1
                    o.seq = c
            self.ecount[e] = c

        def emit_engine(ename, eng):
            seen = self.seen[ename]

            def wait(sem, val):
                key = id(sem)
                if seen.get(key, 0) >= val:
                    return
                seen[key] = val
                eng.wait_ge(sem, val)

            for o in self.ops[ename]:
                need = {}
                for (p, raw) in o.deps:
                    if p.kind == "dma":
                        sm, v = p.dma_sem, p.dma_val
                    elif p.eng != ename or p.eng != "pe" or o.kind == "dma":
                        sm, v = self.esem[p.eng], p.seq
                    else:
                        continue
                    k = id(sm)
                    if k not in need or need[k][1] < v:
                        need[k] = (sm, v)
                for (sm, v) in need.values():
                    wait(sm, v)
                ins = o.fn(eng)
                if o.kind == "dma":
                    ins.then_inc(o.dma_sem, 16)
                elif o.signals:
                    ins.then_inc(self.esem[ename], 1)
            need = {}
            for o in self.stage_dmas[ename]:
                k = id(o.dma_sem)
                if k not in need or need[k][1] < o.dma_val:
                    need[k] = (o.dma_sem, o.dma_val)
            for (sm, v) in need.values():
                wait(sm, v)

        with nc.Block(no_gpsimd_drain=True) as blk:
            if self.ops["sp"]:
                @blk.sync
                def _(e):
                    emit_engine("sp", e)
            if self.ops["pe"]:
                @blk.tensor
                def _(e):
                    emit_engine("pe", e)
            if self.ops["act"]:
                @blk.scalar
                def _(e):
                    emit_engine("act", e)
            if self.ops["dve"]:
                @blk.vector
                def _(e):
                    emit_engine("dve", e)
            if self.ops["pool"]:
                @blk.gpsimd
                def _(e):
                    emit_engine("pool", e)
        self.ops = {e: [] for e in ENGINES}
        self.stage_dmas = {e: [] for e in ENGINES}
        for st, kind in self.stage_sem_tiles:
            self.sem_pool[kind].append((st.sem, st.sem_count))
            st.sem = None
        self.stage_sem_tiles = []
        self.stage += 1

    def close(self):
        self.es.close()


class K:
    def __init__(self, nc, ngroups=NGRP, upto="ffn2"):
        self.nc = nc
        self.ngroups = ngroups
        self.upto = upto
        P = self.P = Prog(nc)
        self.din = {}
        self._decl_inputs()
        self.y = nc.dram_tensor("y", [S, D], F32, kind="ExternalOutput").ap()
        sb = P.sbuf
        self.xT = sb("xT", [128, 16, GT], F32)
        self.KTn = sb("KTn", [128, 8, S], BF16)
        self.KTr = sb("KTr", [128, S], BF16)
        self.Vaug = sb("Vaug", [128, 16, 8 * VW], BF16)
        self.rk = sb("rk", [128, 16, 8], F32)
        self.Sst = sb("Sst", [128, 4, 256], F32)
        self.mKT = sb("mKT", [128, 4, 256], BF16)
        self.mV = sb("mV", [128, 2, 4 * VW], BF16)
        self.cosT = sb("cosT", [128, 16, 32], F32)
        self.sinT = sb("sinT", [128, 16, 32], F32)
        self.ident_b = sb("ident_b", [128, 128], BF16)
        self.ident_f = sb("ident_f", [128, 128], F32)
        self.ones_b = sb("ones_b", [128, 128], BF16)
        self.tri_f = sb("tri_f", [128, 128], F32)
        self.ind_f = sb("ind_f", [128, 2], F32)
        self.tri_b = sb("tri_b", [128, 128], BF16)
        self.ind_b = sb("ind_b", [128, 2], BF16)
        self.g_f1 = sb("g_f1", [128, 16], F32)
        self.g_f2 = sb("g_f2", [128, 16], F32)
        self.g_mix = sb("g_mix", [128, 16], F32)
        self.g_ma = sb("g_ma", [128, 16], F32)
        self.g_mn = sb("g_mn", [128, 16], F32)
        self.g_qa = sb("g_qa", [128, 4], F32)
        self.g_kva = sb("g_kva", [128, 2], F32)
        self.gqk_b = sb("gqk_b", [128, 192], F32)
        self.gk_b = sb("gk_b", [128, 192], F32)
        self.gout_b = sb("gout_b", [128, 256], F32)
        self.g_mq = sb("g_mq", [128, 1], F32)
        self.g_mk = sb("g_mk", [128, 1], F32)
        self.wg2 = sb("wg2", [32, 512], BF16)
        self.AW = 86 * 256
        self.arena = sb("arena", [128, self.AW], F32)
        self.psum = [P.psum(f"psb{i}", [128, 512], F32) for i in range(8)]
        self.t_ps = tiles("ps", 8)
        for t in self.t_ps:
            t.psum = True
        self.t_x = tiles("x", 16)
        self.t_const = Tile("const")
        self.t_K = Tile("K")
        self.t_S = Tile("S")
        self.t_Sh = tiles("Sh", 4)
        self.t_mem = Tile("mem")
        self.t_y = tiles("yout", 2)

    def _decl(self, name, shape, dt=F32):
        self.din[name] = self.nc.dram_tensor(name, list(shape), dt, kind="ExternalInput").ap()

    def _decl_inputs(self):
        d = self._decl
        d("x", [S, D]); d("mem", [256, D]); d("pos", [128, 16], I32)
        for k in (1, 2):
            d(f"f{k}_gu", [NJ, 128, 2, 16, 128]); d(f"f{k}_dn", [NQ, 16, 128, JQ, 128]); d(f"f{k}_g", [128, 16])
        d("win_fm", [10, 128, 16, 128]); d("win_zg", [128, 16, 16]); d("win_tm", [10, 128, 16, 256])
        d("win_kr", [128, 16, 64]); d("mix_g", [128, 16]); d("qa_g", [128, 4]); d("kva_g", [128, 2])
        d("wq_up", [128, 4, 1536]); d("wkv_up", [128, 2, 2048]); d("gq_row", [1, 192]); d("gk_row", [1, 192])
        d("wg2", [17, 512]); d("gout_row", [1, 256]); d("w_out", [16, 128, 16, 128])
        d("ma_g", [128, 16]); d("mn_g", [128, 16]); d("mwq", [4, 128, 16, 128]); d("mwk", [4, 128, 16, 128])
        d("mwv", [128, 16, 512]); d("mwo", [16, 128, 4, 128]); d("mq_g", [128, 1]); d("mk_g", [128, 1])
        d("invf", [1, 32])

    def af32(self, off_kib, shape):
        n = int(np.prod(shape))
        o = int(off_kib * 256)
        assert o + n <= self.AW, (off_kib, shape)
        ap = self.arena[:, o:o + n]
        return self._shape(ap, shape)

    def abf(self, off_kib, shape):
        n = int(np.prod(shape))
        w = (n + 1) // 2
        o = int(off_kib * 256)
        assert o + w <= self.AW, (off_kib, shape)
        ap = self.arena[:, o:o + w].bitcast(BF16)[:, 0:n]
        return self._shape(ap, shape)

    @staticmethod
    def _shape(ap, shape):
        if len(shape) == 1:
            return ap
        if len(shape) == 2:
            return ap.rearrange("p (a b) -> p a b", b=shape[1])
        if len(shape) == 3:
            return ap.rearrange("p (a b c) -> p a b c", b=shape[1], c=shape[2])
        raise ValueError

    def psbf(self, b):
        return self.psum[b].bitcast(BF16)

    def mm(self, out, lhsT, rhs, start, stop, reads, writes, join=False):
        self.P.op("pe", lambda e: e.matmul(out, lhsT=lhsT, rhs=rhs, start=start, stop=stop),
                  reads=reads, writes=writes, join=join)

    def tr(self, out, in_, ident, reads, writes, join=False):
        self.P.op("pe", lambda e: e.transpose(out, in_, ident), reads=reads, writes=writes, join=join)

    def act(self, out, in_, func, reads, writes, join=False, scale=1.0, bias=0.0, accum_out=None):
        if accum_out is None:
            self.P.op("act", lambda e: e.activation(out=out, in_=in_, func=func, scale=scale, bias=bias),
                      reads=reads, writes=writes, join=join)
        else:
            self.P.op("act", lambda e: e.activation(out=out, in_=in_, func=func, scale=scale, bias=bias,
                                                   accum_out=accum_out),
                      reads=reads, writes=writes, join=join)

    def copy(self, eng, out, in_, reads, writes, join=False):
        if eng == "act":
            self.act(out, in_, AF.Copy, reads, writes, join)
        else:
            self.P.op(eng, lambda e: e.tensor_copy(out, in_), reads=reads, writes=writes, join=join)

    def tt(self, out, in0, in1, op, reads, writes, join=False, eng="dve"):
        self.P.op(eng, lambda e: e.tensor_tensor(out, in0, in1, op), reads=reads, writes=writes, join=join)

    def ts(self, out, in0, s1, s2, op0, op1, reads, writes, join=False, eng="dve"):
        if op1 is None:
            self.P.op(eng, lambda e: e.tensor_scalar(out, in0, s1, None, op0), reads=reads, writes=writes, join=join)
        else:
            self.P.op(eng, lambda e: e.tensor_scalar(out, in0, s1, s2, op0, op1), reads=reads, writes=writes,
                      join=join)

    def stt(self, out, in0, scalar, in1, op0, op1, reads, writes, join=False):
        self.P.op("dve", lambda e: e.scalar_tensor_tensor(out, in0, scalar, in1, op0, op1),
                  reads=reads, writes=writes, join=join)

    def memset(self, eng, ap, val, reads, writes, join=False):
        self.P.op(eng, lambda e: e.memset(ap, val), reads=reads, writes=writes, join=join)

    def sq_acc(self, junk, src, acc, reads, t_acc):
        self.act(junk, src, AF.Square, list(reads) + [t_acc], [t_acc], join=True, accum_out=acc)

    def rsqrt_inplace(self, ap, t, mult, add):
        self.ts(ap, ap, mult, add, ALU.mult, ALU.add, [t], [t])
        self.act(ap, ap, AF.Ln, [t], [t])
        self.act(ap, ap, AF.Exp, [t], [t], scale=-0.5)

    def prologue(self):
        P, din = self.P, self.din
        tc = self.t_const
        ld = lambda dst, src: P.dma("sp", dst, src, writes=[tc], join=True)
        ld(self.g_f1[:], din["f1_g"]); ld(self.g_f2[:], din["f2_g"]); ld(self.g_mix[:], din["mix_g"])
        ld(self.g_ma[:], din["ma_g"]); ld(self.g_mn[:], din["mn_g"]); ld(self.g_qa[:], din["qa_g"])
        ld(self.g_kva[:], din["kva_g"]); ld(self.g_mq[:], din["mq_g"]); ld(self.g_mk[:], din["mk_g"])
        ld(self.gqk_b[:], din["gq_row"].partition_broadcast(128))
        ld(self.gk_b[:], din["gk_row"].partition_broadcast(128))
        ld(self.gout_b[:], din["gout_row"].partition_broadcast(128))
        P.dma("pool", self.wg2[0:17, :], din["wg2"], writes=[tc], join=True, sem_tile=Tile("const_sw"))
        invf = self.af32(0, [32])
        posi = self.arena[:, 64:80].bitcast(I32)
        posf = self.af32(0.5, [16])
        ld(invf, din["invf"].partition_broadcast(128))
        ld(posi, din["pos"])
        t_c2 = Tile("c2")
        self.memset("pool", self.ident_f[:], 0.0, [], [t_c2])
        P.op("pool", lambda e: e.affine_select(out=self.ident_f[:], in_=self.ident_f[:], pattern=[[-1, 128]],
                                               compare_op=ALU.not_equal, fill=1.0, base=0, channel_multiplier=1),
             reads=[t_c2], writes=[t_c2])
        self.copy("pool", self.ident_b[:], self.ident_f[:], [t_c2], [t_c2], join=True)
        self.memset("pool", self.ones_b[:], 1.0, [], [t_c2], join=True)
        t_tri = Tile("tri")
        self.memset("pool", self.tri_f[:], 1.0, [], [t_tri])
        P.op("pool", lambda e: e.affine_select(out=self.tri_f[:], in_=self.tri_f[:], pattern=[[-1, 128]],
                                               compare_op=ALU.is_gt, fill=0.0, base=0, channel_multiplier=1),
             reads=[t_tri], writes=[t_tri])
        self.memset("pool", self.tri_f[64:128, 0:64], 0.0, [t_tri], [t_tri])
        t_ind = Tile("ind")
        self.memset("pool", self.ind_f[:], 0.0, [], [t_ind])
        self.memset("pool", self.ind_f[0:64, 0:1], 1.0, [t_ind], [t_ind])
        self.memset("pool", self.ind_f[64:128, 1:2], 1.0, [t_ind], [t_ind])
        self.copy("pool", self.ind_b[:], self.ind_f[:], [t_ind], [Tile("indb")])
        self.copy("pool", self.tri_b[:], self.tri_f[:], [t_tri], [Tile("trib")])
        t_v = Tile("vones")
        va = self.Vaug[:].rearrange("p t (h w) -> p t h w", w=VW)
        self.memset("pool", va[:, :, :, 128:129], 1.0, [], [t_v])
        mv = self.mV[:].rearrange("p t (h w) -> p t h w", w=VW)
        self.memset("pool", mv[:, :, :, 128:129], 1.0, [], [t_v], join=True)
        self.memset("pool", self.Sst[:], 0.0, [], [self.t_S])
        self.tt(self.gqk_b[:, 0:128], self.gqk_b[:, 0:128], self.gk_b[:, 0:128], ALU.mult, [tc], [tc])
        t_r = Tile("rope")
        self.copy("dve", posf, posi, [tc], [t_r])
        ang = self.af32(1, [16, 32])
        kf = self.af32(3, [16, 32])
        ki = self.arena[:, 5 * 256:5 * 256 + 512].bitcast(I32).rearrange("p (a b) -> p a b", b=32)
        r = self.af32(7, [16, 32])
        rc = self.af32(9, [16, 32])
        msk = self.af32(11, [16, 32])
        t_a, t_k, t_rr, t_rc, t_m = Tile("ang"), Tile("kf"), Tile("r"), Tile("rc"), Tile("msk")
        self.tt(ang, posf.unsqueeze(2).to_broadcast([128, 16, 32]), invf.unsqueeze(1).to_broadcast([128, 16, 32]),
                ALU.mult, [t_r, tc], [t_a])
        self.ts(kf, ang, 1.0 / (2 * PI), None, ALU.mult, None, [t_a], [t_k])
        self.copy("dve", ki, kf, [t_k], [t_k])
        self.copy("dve", kf, ki, [t_k], [t_k])
        self.stt(r, kf, -C1_2PI, ang, ALU.mult, ALU.add, [t_k, t_a], [t_rr])
        self.stt(r, kf, -C2_2PI, r, ALU.mult, ALU.add, [t_k, t_rr], [t_rr])
        self.ts(r, r, -PI, PI, ALU.max, ALU.min, [t_rr], [t_rr])
        self.act(self.sinT[:], r, AF.Sin, [t_rr], [tc], join=True)
        self.ts(rc, r, PI / 2, None, ALU.add, None, [t_rr], [t_rc])
        self.ts(msk, rc, PI, -2 * PI, ALU.is_gt, ALU.mult, [t_rc], [t_m])
        self.tt(rc, rc, msk, ALU.add, [t_rc, t_m], [t_rc])
        self.ts(rc, rc, -PI, PI, ALU.max, ALU.min, [t_rc], [t_rc])
        self.act(self.cosT[:], rc, AF.Sin, [t_rc], [tc], join=True)
        P.end_stage()
        if self.upto in ("cross", "ffn2"):
            self.mem_kv()

    def mem_kv(self):
        P, din = self.P, self.din
        mem_f = [self.af32(0, [2048]), self.af32(8, [2048])]
        mem_b = [self.abf(16, [2048]), self.abf(20, [2048])]
        junk = self.af32(24, [2048])
        ssm = self.af32(32, [2])
        mnT = self.abf(33, [16, 256])
        wk = [self.abf(41 + 4 * i, [16, 128]) for i in range(3)]
        wv = self.abf(53, [16, 512])
        kmz = self.af32(69, [256])
        sqb = self.abf(70, [256])
        rs = self.af32(71, [256])
        t_mf, t_mb = tiles("mf", 2), tiles("mb", 2)
        t_ss, t_mnT, t_wk, t_wv, t_kmz, t_sqb, t_rs = Tile("ssm"), tiles("mnT", 16), tiles("wk", 3), Tile("wv"), \
            Tile("kmz"), Tile("sqbm"), Tile("rsm")
        P.dma("pool", wv, din["mwv"], writes=[t_wv])
        self.memset("dve", ssm, 0.0, [], [t_ss])
        for mt in range(2):
            P.dma("sp", mem_f[mt], din["mem"][mt * 128:(mt + 1) * 128, :], writes=[t_mf[mt]])
            self.sq_acc(junk, mem_f[mt], ssm[:, mt:mt + 1], [t_mf[mt]], t_ss)
        self.ts(ssm, ssm, 1.0 / D, EPS, ALU.mult, ALU.add, [t_ss], [t_ss])
        self.act(ssm, ssm, AF.Ln, [t_ss], [t_ss])
        self.act(ssm, ssm, AF.Exp, [t_ss], [t_ss], scale=-0.5)
        for mt in range(2):
            self.ts(mem_b[mt], mem_f[mt], ssm[:, mt:mt + 1], None, ALU.mult, None, [t_mf[mt], t_ss], [t_mb[mt]])
            for q in range(4):
                b = (mt * 4 + q) % 4
                pb = self.psbf(b)
                for k in range(4):
                    c = q * 4 + k
                    self.tr(pb[:, k * 128:(k + 1) * 128], mem_b[mt][:, c * 128:(c + 1) * 128], self.ident_b[:],
                            [t_mb[mt]], [self.t_ps[b]], join=(k > 0))
                for k in range(4):
                    c = q * 4 + k
                    self.act(mnT[:, c, mt * 128:(mt + 1) * 128], pb[:, k * 128:(k + 1) * 128], AF.Copy,
                             [self.t_ps[b], self.t_const], [t_mnT[c]], join=True, scale=self.g_mn[:, c:c + 1])
        for h in range(4):
            sl = h % 3
            P.dma("pool", wk[sl], din["mwk"][h], writes=[t_wk[sl]])
            b = 4 + h % 2
            for c in range(16):
                self.mm(self.psum[b][:, 0:256], wk[sl][:, c, :], mnT[:, c, :], c == 0, c == 15,
                        [t_wk[sl], t_mnT[c]], [self.t_ps[b]], join=(c > 0))
            self.copy("act", kmz, self.psum[b][:, 0:256], [self.t_ps[b]], [t_kmz])
            self.act(sqb, self.psum[b][:, 0:256], AF.Square, [self.t_ps[b]], [t_sqb])
            self.mm(self.psum[6][:, 0:256], self.ones_b[:], sqb, True, True, [t_sqb], [self.t_ps[6]])
            self.ts(rs, self.psum[6][:, 0:256], 1.0 / 128, EPS, ALU.mult, ALU.add, [self.t_ps[6]], [t_rs])
            self.act(rs, rs, AF.Ln, [t_rs], [t_rs])
            self.act(rs, rs, AF.Exp, [t_rs], [t_rs], scale=-0.5)
            self.stt(self.mKT[:, h, :], kmz, self.g_mk[:, 0:1], rs, ALU.mult, ALU.mult, [t_kmz, t_rs], [self.t_mem],
                     join=True)
        mv = self.mV[:].rearrange("p t (h w) -> p t h w", w=VW)
        for mt in range(2):
            b = 2 + mt
            for c in range(16):
                self.mm(self.psum[b][:], mnT[:, c, mt * 128:(mt + 1) * 128], wv[:, c, :], c == 0, c == 15,
                        [t_wv, t_mnT[c]], [self.t_ps[b]], join=(c > 0))
            self.copy("dve", mv[:, mt, :, 0:128], self.psum[b][:].rearrange("p (h w) -> p h w", w=128),
                      [self.t_ps[b]], [self.t_mem], join=True)
        P.end_stage()

    def load_x(self, g):
        P = self.P
        xin = [self.af32(0, [2048]), self.af32(8, [2048])]
        t_xin = tiles("xin", 2)
        n = 0
        for tt in range(4):
            sl = tt % 2
            r0 = g * GT + tt * 128
            P.dma("sp", xin[sl], self.din["x"][r0:r0 + 128, :], writes=[t_xin[sl]])
            for q in range(4):
                b = n % 8
                n += 1
                for k in range(4):
                    c = q * 4 + k
                    self.tr(self.psum[b][:, k * 128:(k + 1) * 128], xin[sl][:, c * 128:(c + 1) * 128], self.ident_f[:],
                            [t_xin[sl]], [self.t_ps[b]], join=(k > 0))
                self.copy("act" if q % 2 else "dve", self.xT[:, q * 4:(q + 1) * 4, tt * 128:(tt + 1) * 128],
                          self.psum[b][:].rearrange("p (k t) -> p k t", t=128),
                          [self.t_ps[b]], [self.t_x[c] for c in range(q * 4, q * 4 + 4)], join=True)
        P.end_stage()

    def store_x(self, g):
        P = self.P
        yo = [self.af32(0, [2048]), self.af32(8, [2048])]
        t_yo = tiles("yo", 2)
        n = 0
        for tt in range(4):
            sl = tt % 2
            for q in range(4):
                b = n % 8
                n += 1
                for k in range(4):
                    c = q * 4 + k
                    self.tr(self.psum[b][:, k * 128:(k + 1) * 128], self.xT[:, c, tt * 128:(tt + 1) * 128],
                            self.ident_f[:], [self.t_x[c]], [self.t_ps[b]], join=(k > 0))
                self.copy("act" if q % 2 else "dve", yo[sl][:, q * 512:(q + 1) * 512], self.psum[b][:],
                          [self.t_ps[b]], [t_yo[sl]], join=(q > 0))
            r0 = g * GT + tt * 128
            P.dma("sp", self.y[r0:r0 + 128, :], yo[sl], reads=[t_yo[sl]], writes=[self.t_y[sl]])
        P.end_stage()

    def norm(self, gcol, hT, t_hT, sq_off, rs_off):
        sqb = [self.abf(sq_off, [GT]), self.abf(sq_off + 1, [GT])]
        rs = self.af32(rs_off, [GT])
        t_sq, t_rs = tiles("nsq", 2), Tile("nrs")
        for c in range(16):
            if c % 2 == 0:
                self.act(sqb[0], self.xT[:, c, :], AF.Square, [self.t_x[c]], [t_sq[0]])
            else:
                self.tt(sqb[1], self.xT[:, c, :], self.xT[:, c, :], ALU.mult, [self.t_x[c]], [t_sq[1]])
            self.mm(self.psum[7][:], self.ones_b[:], sqb[c % 2], c == 0, c == 15, [t_sq[c % 2]], [self.t_ps[7]],
                    join=(c > 0))
        self.ts(rs, self.psum[7][:], 1.0 / D, EPS, ALU.mult, ALU.add, [self.t_ps[7]], [t_rs])
        self.act(rs, rs, AF.Ln, [t_rs], [t_rs])
        self.act(rs, rs, AF.Exp, [t_rs], [t_rs], scale=-0.5)
        for c in range(16):
            self.stt(hT[:, c, :], self.xT[:, c, :], gcol[:, c:c + 1], rs, ALU.mult, ALU.mult,
                     [self.t_x[c], t_rs, self.t_const], [t_hT[c]])

    def ffn(self, k):
        P, din = self.P, self.din
        gu_d, dn_d = din[f"f{k}_gu"], din[f"f{k}_dn"]
        gcol = self.g_f1 if k == 1 else self.g_f2
        hT = self.abf(0, [16, GT]); t_hT = tiles("hT", 16)
        aT = [self.abf(16, [JQ, GT]), self.abf(27, [JQ, GT])]
        t_aT = [tiles("aTa", JQ), tiles("aTb", JQ)]
        wgu = [self.abf(38 + 8 * i, [2 * 16, 128]) for i in range(3)]; t_wgu = tiles("wgu", 3)
        wdn = [self.abf(62 + 2.75 * i, [JQ, 128]) for i in range(4)]; t_wdn = tiles("wdn", 4)
        sg = [self.af32(73, [GT]), self.af32(75, [GT])]; t_sg = tiles("sg", 2)
        self.norm(gcol, hT, t_hT, 77, 79)
        def dma_gu(n):
            P.dma("pool", wgu[n % 3], gu_d[n].rearrange("p a c f -> p (a c) f"), writes=[t_wgu[n % 3]])

        def dma_dn(m):
            P.dma("pool", wdn[m % 4], dn_d[m // 16, m % 16], writes=[t_wdn[m % 4]])

        for n in range(3):
            dma_gu(n)
        for s in range(NQ):
            a = aT[s % 2]
            ta = t_aT[s % 2]
            for m in range(4):
                dma_dn(s * 16 + m)
            for jj in range(JQ):
                n_gu = s * JQ + jj
                sl = n_gu % 3
                bg, bu = n_gu % 2, 2 + n_gu % 2
                for half, b in ((0, bg), (1, bu)):
                    for c in range(16):
                        self.mm(self.psum[b][:], wgu[sl][:, half * 16 + c, :], hT[:, c, :], c == 0, c == 15,
                                [t_wgu[sl], t_hT[c]], [self.t_ps[b]], join=(c > 0))
                self.act(sg[n_gu % 2], self.psum[bg][:], AF.Silu, [self.t_ps[bg]], [t_sg[n_gu % 2]])
                self.tt(a[:, jj, :], sg[n_gu % 2], self.psum[bu][:], ALU.mult, [t_sg[n_gu % 2], self.t_ps[bu]],
                        [ta[jj]])
                if n_gu + 3 < NJ:
                    dma_gu(n_gu + 3)
            for c in range(16):
                n_dn = s * 16 + c
                sl = n_dn % 4
                b = 4 + n_dn % 2
                for jj in range(JQ):
                    self.mm(self.psum[b][:], wdn[sl][:, jj, :], a[:, jj, :], jj == 0, jj == JQ - 1,
                            [t_wdn[sl], ta[jj]], [self.t_ps[b]], join=(jj > 0))
                self.stt(self.xT[:, c, :], self.psum[b][:], 0.5, self.xT[:, c, :], ALU.mult, ALU.add,
                         [self.t_ps[b], self.t_x[c]], [self.t_x[c]])
                if c + 4 < 16:
                    dma_dn(n_dn + 4)
        P.end_stage()

    def carried(self):
        c = {}
        c["gqlo"] = self.abf(0, [4, GT]); c["gqhi"] = self.abf(4, [4, GT])
        c["zgT"] = self.abf(8, [GT])
        c["gk"] = self.af32(9, [4, 512]); c["gv"] = self.abf(17, [4, 1024]); c["szr"] = self.abf(25, [4, 1024])
        c["zkr"] = self.af32(33, [4, 64])
        c["cT"] = self.abf(34, [6, GT])
        c["qTn"] = self.abf(40, [8, GT]); c["qTr"] = self.abf(48, [8, GT])
        c["mixT"] = self.abf(56, [16, GT])
        return c

    def mix_p1(self, g, C, T):
        P, din = self.P, self.din
        hT = self.abf(40, [16, GT]); t_hT = T["hT"]
        wfm = [self.abf(56 + 4 * i, [16, 128]) for i in range(3)]; t_w = tiles("wfm", 3)
        zT = self.af32(68, [6, GT]); t_z = tiles("zT", 6)
        self.norm(self.g_mix, hT, t_hT, 80, 82)
        sqb = [self.abf(80, [GT]), self.abf(81, [GT])]; t_sq = tiles("p1sq", 2)
        rsq, rskv = self.af32(82, [GT]), self.af32(84, [GT]); t_rq, t_rkv = Tile("rsq"), Tile("rskv")
        self.memset("pool", C["gqlo"], 0.0, [], [T["gq0"]])
        self.memset("pool", C["gqhi"], 0.0, [], [T["gq0"]], join=True)
        self.memset("pool", C["zgT"][0:32, :], 1.0, [], [T["zgT"]])
        n = 0
        pending = None
        for ck in range(10):
            sl = n % 3
            P.dma("pool", wfm[sl], din["win_fm"][ck], writes=[t_w[sl]])
            b = n % 2
            n += 1
            for c in range(16):
                self.mm(self.psum[b][:], wfm[sl][:, c, :], hT[:, c, :], c == 0, c == 15, [t_w[sl], t_hT[c]],
                        [self.t_ps[b]], join=(c > 0))
            if pending is not None:
                pending()
                pending = None
            if ck < 6:
                self.copy("act", zT[:, ck, :], self.psum[b][:], [self.t_ps[b]], [t_z[ck]])
                self.act(sqb[ck % 2], self.psum[b][:], AF.Square, [self.t_ps[b]], [t_sq[ck % 2]])
                def _post(ck=ck):
                    if ck < 4:
                        self.mm(self.psum[2][:], self.ones_b[:], sqb[ck % 2], ck == 0, ck == 3, [t_sq[ck % 2]],
                                [self.t_ps[2]], join=(ck > 0))
                    else:
                        self.mm(self.psum[3][:], self.ones_b[:], sqb[ck % 2], ck == 4, ck == 5, [t_sq[ck % 2]],
                                [self.t_ps[3]], join=(ck > 4))
                    if ck == 3:
                        self.ts(rsq, self.psum[2][:], 1.0 / 512, EPS, ALU.mult, ALU.add, [self.t_ps[2]], [t_rq])
                        self.act(rsq, rsq, AF.Ln, [t_rq], [t_rq])
                        self.act(rsq, rsq, AF.Exp, [t_rq], [t_rq], scale=-0.5)
                        for c4 in range(4):
                            self.stt(C["cT"][:, c4, :], zT[:, c4, :], self.g_qa[:, c4:c4 + 1], rsq, ALU.mult, ALU.mult,
                                     [t_z[c4], t_rq, self.t_const], [T["cT"][c4]])
                    if ck == 5:
                        self.ts(rskv, self.psum[3][:], 1.0 / 256, EPS, ALU.mult, ALU.add, [self.t_ps[3]], [t_rkv])
                        self.act(rskv, rskv, AF.Ln, [t_rkv], [t_rkv])
                        self.act(rskv, rskv, AF.Exp, [t_rkv], [t_rkv], scale=-0.5)
                        for c2 in range(2):
                            self.stt(C["cT"][:, 4 + c2, :], zT[:, 4 + c2, :], self.g_kva[:, c2:c2 + 1], rskv, ALU.mult,
                                     ALU.mult, [t_z[4 + c2], t_rkv, self.t_const], [T["cT"][4 + c2]])
                pending = _post
            else:
                h = ck - 6
                pv = self.psum[b][:].rearrange("p (t a w) -> p t a w", a=2, w=64)
                lo = C["gqlo"][:, h, :].rearrange("p (t a w) -> p t a w", a=2, w=64)
                hi = C["gqhi"][:, h, :].rearrange("p (t a w) -> p t a w", a=2, w=64)
                self.act(lo[:, :, 0, :], pv[:, :, 0, :], AF.Copy, [self.t_ps[b], T["gq0"]], [T["gq"]], join=True,
                         scale=128 ** -0.5)
                self.act(hi[:, :, 1, :], pv[:, :, 1, :], AF.Copy, [self.t_ps[b], T["gq0"]], [T["gq"]], join=True,
                         scale=128 ** -0.5)
        wzg = self.abf(56, [16, 16]); t_wzg = t_w[0]
        P.dma("pool", wzg, din["win_zg"], writes=[t_wzg])
        for c in range(16):
            self.mm(self.psum[4][0:16, :], wzg[:, c, :], hT[:, c, :], c == 0, c == 15, [t_wzg, t_hT[c]],
                    [self.t_ps[4]], join=(c > 0))
        self.copy("act", C["zgT"][0:16, :], self.psum[4][0:16, :], [self.t_ps[4], T["zgT"]], [T["zgT"]])
        P.end_stage()

    def mix_p2(self, g, C, T):
        P, din = self.P, self.din
        hT = self.abf(40, [16, GT]); t_hT = T["hT"]
        wtm = [self.abf(56, [16, 256]), self.abf(64, [16, 256])]; t_w = tiles("wtm", 2)
        n = 0
        for cg in range(10):
            sl = cg % 2
            P.dma("pool", wtm[sl], din["win_tm"][cg], writes=[t_w[sl]])
            for tt in range(4):
                b = n % 4
                n += 1
                for c in range(16):
                    self.mm(self.psum[b][:, 0:256], hT[:, c, tt * 128:(tt + 1) * 128], wtm[sl][:, c, :], c == 0,
                            c == 15, [t_w[sl], t_hT[c]], [self.t_ps[b]], join=(c > 0))
                src = self.psum[b][:, 0:256]
                if cg < 2:
                    self.copy("act" if n % 2 else "dve", C["gk"][:, tt, cg * 256:(cg + 1) * 256], src,
                              [self.t_ps[b]], [T["gk"][tt]], join=True)
                elif cg < 6:
                    o = (cg - 2) * 256
                    self.copy("act" if n % 2 else "dve", C["gv"][:, tt, o:o + 256], src, [self.t_ps[b]],
                              [T["gv"][tt]], join=True)
                else:
                    o = (cg - 6) * 256
                    self.act(C["szr"][:, tt, o:o + 256], src, AF.Silu, [self.t_ps[b]], [T["szr"][tt]], join=True)
        wkr = self.abf(72, [16, 64]); t_wkr = Tile("wkr")
        P.dma("pool", wkr, din["win_kr"], writes=[t_wkr])
        for tt in range(4):
            b = 4 + tt % 2
            for c in range(16):
                self.mm(self.psum[b][:, 0:64], hT[:, c, tt * 128:(tt + 1) * 128], wkr[:, c, :], c == 0, c == 15,
                        [t_wkr, t_hT[c]], [self.t_ps[b]], join=(c > 0))
            self.copy("dve", C["zkr"][:, tt, :], self.psum[b][:, 0:64], [self.t_ps[b]], [T["zkr"]], join=True)
        P.end_stage()

    def mix_p3a(self, g, C, T):
        P, din = self.P, self.din
        wkv = self.abf(40, [2, 2048]); t_wkv = Tile("wkv")
        P.dma("pool", wkv, din["wkv_up"], writes=[t_wkv])
        junk = self.af32(48, [128]); t_junk = Tile("junk")
        ssk = self.af32(48.5, [4, 8]); sskr = self.af32(48.75, [4]); t_ssk = Tile("ssk")
        kr = self.af32(49, [4, 64]); t_kr = Tile("kr")
        tmp = [self.af32(50 + 0.5 * i, [4, 32]) for i in range(4)]; t_tmp = tiles("rtmp", 4)
        krb = self.abf(52, [4, 64]); t_krb = Tile("krb")
        va = self.Vaug[:].rearrange("p t (h w) -> p t h w", w=VW)
        cT = C["cT"]
        n = 0
        self.memset("dve", ssk, 0.0, [], [t_ssk])
        self.memset("dve", sskr, 0.0, [], [t_ssk], join=True)
        for tt in range(4):
            Tg = g * 4 + tt
            for cgi in range(4):
                b = n % 4
                n += 1
                for c in range(2):
                    self.mm(self.psum[b][:], cT[:, 4 + c, tt * 128:(tt + 1) * 128], wkv[:, c, cgi * 512:(cgi + 1) * 512],
                            c == 0, c == 1, [t_wkv, T["cT"][4 + c]], [self.t_ps[b]], join=(c > 0))
                for hh in range(2):
                    h = cgi * 2 + hh
                    self.sq_acc(junk, self.psum[b][:, hh * 256:hh * 256 + 128], ssk[:, tt, h:h + 1], [self.t_ps[b]],
                                t_ssk)
                self.copy("act", va[:, Tg, cgi * 2:cgi * 2 + 2, 0:128],
                          self.psum[b][:].rearrange("p (h w) -> p h w", w=256)[:, :, 128:256],
                          [self.t_ps[b]], [self.t_K], join=True)
            self.sq_acc(junk[:, 0:64], C["zkr"][:, tt, :], sskr[:, tt:tt + 1], [T["zkr"]], t_ssk)
        self.tt(ssk, ssk, sskr.unsqueeze(2).to_broadcast([128, 4, 8]), ALU.add, [t_ssk], [t_ssk])
        self.rsqrt_inplace(ssk, t_ssk, 1.0 / 192, EPS)
        self.ts(self.rk[:, g * 4:(g + 1) * 4, :], ssk, 192 ** -0.5, None, ALU.mult, None, [t_ssk], [self.t_K],
                join=True)
        self.tt(kr, C["zkr"], self.gk_b[:, 128:192].unsqueeze(1).to_broadcast([128, 4, 64]), ALU.mult,
                [T["zkr"], self.t_const], [t_kr])
        cs = self.cosT[:, g * 4:(g + 1) * 4, :]
        sn = self.sinT[:, g * 4:(g + 1) * 4, :]
        x1, x2 = kr[:, :, 0:32], kr[:, :, 32:64]
        self.tt(tmp[0], x1, cs, ALU.mult, [t_kr], [t_tmp[0]])
        self.tt(tmp[1], x2, sn, ALU.mult, [t_kr], [t_tmp[1]])
        self.tt(tmp[2], x2, cs, ALU.mult, [t_kr], [t_tmp[2]])
        self.tt(tmp[3], x1, sn, ALU.mult, [t_kr], [t_tmp[3]])
        self.tt(krb[:, :, 0:32], tmp[0], tmp[1], ALU.subtract, [t_tmp[0], t_tmp[1]], [t_krb])
        self.tt(krb[:, :, 32:64], tmp[2], tmp[3], ALU.add, [t_tmp[2], t_tmp[3]], [t_krb], join=True)
        pb = self.psbf(4)
        for tt in range(4):
            self.tr(pb[0:64, tt * 128:(tt + 1) * 128], krb[:, tt, :], self.ident_b[:], [t_krb], [self.t_ps[4]],
                    join=(tt > 0))
        self.copy("act", self.KTr[0:64, g * GT:(g + 1) * GT], pb[0:64, 0:512], [self.t_ps[4]], [self.t_K], join=True)
        for h in range(8):
            b = 5 + h % 3
            for c in range(2):
                self.mm(self.psum[b][:], wkv[:, c, h * 256:h * 256 + 128], cT[:, 4 + c, :], c == 0, c == 1,
                        [t_wkv, T["cT"][4 + c]], [self.t_ps[b]], join=(c > 0))
            self.copy("act" if h % 2 else "dve", self.KTn[:, h, g * GT:(g + 1) * GT], self.psum[b][:],
                      [self.t_ps[b]], [self.t_K], join=True)
        P.end_stage()

    def mix_p3b(self, g, C, T):
        P, din = self.P, self.din
        wq = self.abf(56, [4, 1536]); t_wq = Tile("wq")
        P.dma("pool", wq, din["wq_up"], writes=[t_wq])
        qf = [self.af32(68, [8, 192]), self.af32(74, [8, 192])]; t_qf = tiles("qf", 2)
        qb = self.abf(80, [8, 192]); t_qb = Tile("qb")
        tmp = [self.af32(83, [8, 32]), self.af32(84, [8, 32])]; t_tmp = tiles("qtmp", 2)
        junk = self.af32(33, [192])
        ssq = [self.af32(33.75, [8]), self.af32(33.78125, [8])]; t_ssq = tiles("ssq", 2)
        cT = C["cT"]

        def A_pe(tt):
            for cg in range(3):
                for c in range(4):
                    self.mm(self.psum[cg][:], cT[:, c, tt * 128:(tt + 1) * 128], wq[:, c, cg * 512:(cg + 1) * 512],
                            c == 0, c == 3, [t_wq, T["cT"][c]], [self.t_ps[cg]], join=(c > 0))

        def A_post(tt):
            q, tq, sq, tsq = qf[tt % 2], t_qf[tt % 2], ssq[tt % 2], t_ssq[tt % 2]
            qff = q.rearrange("p h w -> p (h w)")
            for cg in range(3):
                self.copy("act" if cg == 1 else "dve", qff[:, cg * 512:(cg + 1) * 512], self.psum[cg][:],
                          [self.t_ps[cg]], [tq], join=(cg > 0))
            self.memset("dve", sq, 0.0, [], [tsq])
            for h in range(8):
                self.sq_acc(junk, q[:, h, :], sq[:, h:h + 1], [tq], tsq)
            self.ts(sq, sq, 1.0 / 192, EPS, ALU.mult, ALU.add, [tsq], [tsq])
            self.act(sq, sq, AF.Ln, [tsq], [tsq])
            self.act(sq, sq, AF.Exp, [tsq], [tsq], scale=-0.5)

        def B(tt):
            Tg = g * 4 + tt
            q, tq, sq, tsq = qf[tt % 2], t_qf[tt % 2], ssq[tt % 2], t_ssq[tt % 2]
            self.tt(q, q, sq.unsqueeze(2).to_broadcast([128, 8, 192]), ALU.mult, [tq, tsq], [tq])
            self.tt(q, q, self.gqk_b[:].unsqueeze(1).to_broadcast([128, 8, 192]), ALU.mult, [tq, self.t_const], [tq])
            cs = self.cosT[:, Tg, :].unsqueeze(1).to_broadcast([128, 8, 32])
            sn = self.sinT[:, Tg, :].unsqueeze(1).to_broadcast([128, 8, 32])
            x1, x2 = q[:, :, 128:160], q[:, :, 160:192]
            self.copy("act", qb[:, :, 0:128], q[:, :, 0:128], [tq], [t_qb])
            self.tt(tmp[0], x1, cs, ALU.mult, [tq], [t_tmp[0]])
            self.tt(tmp[1], x2, sn, ALU.mult, [tq], [t_tmp[1]])
            self.tt(qb[:, :, 128:160], tmp[0], tmp[1], ALU.subtract, [t_tmp[0], t_tmp[1]], [t_qb], join=True)
            self.tt(tmp[0], x2, cs, ALU.mult, [tq], [t_tmp[0]])
            self.tt(tmp[1], x1, sn, ALU.mult, [tq], [t_tmp[1]])
            self.tt(qb[:, :, 160:192], tmp[0], tmp[1], ALU.add, [t_tmp[0], t_tmp[1]], [t_qb], join=True)

        def Ct(tt):
            for hq in range(2):
                b = 3 + hq
                pb = self.psbf(b)
                for k in range(4):
                    h = hq * 4 + k
                    self.tr(pb[:, k * 128:(k + 1) * 128], qb[:, h, 0:128], self.ident_b[:], [t_qb], [self.t_ps[b]],
                            join=(k > 0))
                self.copy("act" if hq else "dve", C["qTn"][:, hq * 4:(hq + 1) * 4, tt * 128:(tt + 1) * 128],
                          pb[:, 0:512].rearrange("p (k t) -> p k t", t=128), [self.t_ps[b]], [T["qT"]], join=True)
            pb = self.psbf(5)
            for h in range(8):
                self.tr(pb[0:64, h * 128:(h + 1) * 128], qb[:, h, 128:192], self.ident_b[:], [t_qb], [self.t_ps[5]],
                        join=(h > 0))
            self.copy("dve", C["qTr"][0:64, :, tt * 128:(tt + 1) * 128],
                      pb[0:64, :].rearrange("p (k t) -> p k t", t=128), [self.t_ps[5]], [T["qT"]], join=True)

        A_pe(0)
        A_post(0)
        for tt in range(4):
            if tt < 3:
                A_pe(tt + 1)
            B(tt)
            if tt < 3:
                A_post(tt + 1)
            Ct(tt)
        P.end_stage()

    def mla(self, g, C, T):
        P = self.P
        pT = [self.abf(72 + i, [GT]) for i in range(4)]; t_pT = tiles("pT", 4)
        rec = [self.af32(76, [GT]), self.af32(78, [GT])]; t_rec = tiles("rec", 2)
        qTn, qTr = C["qTn"], C["qTr"]
        nkt = 4 * g + 4
        its = [(h, kt) for h in range(8) for kt in range(nkt)]

        def S(i):
            h, kt = its[i]
            off = max(0, kt - 4 * g) * 128
            bs, sl = i % 3, i % 4
            self.mm(self.psum[bs][:, off:512], self.KTn[:, h, kt * 128:(kt + 1) * 128], qTn[:, h, off:512],
                    True, False, [self.t_K, T["qT"]], [self.t_ps[bs]])
            self.mm(self.psum[bs][:, off:512], self.KTr[0:64, kt * 128:(kt + 1) * 128], qTr[0:64, h, off:512],
                    False, True, [self.t_K, T["qT"]], [self.t_ps[bs]], join=True)
            self.act(pT[sl][:, off:512], self.psum[bs][:, off:512], AF.Exp, [self.t_ps[bs], self.t_K],
                     [t_pT[sl]], scale=self.rk[:, kt, h:h + 1])
            if kt >= 4 * g:
                self.memset("pool", pT[sl][64:128, off:off + 64], 0.0, [t_pT[sl]], [t_pT[sl]])

        def V(i):
            h, kt = its[i]
            off = max(0, kt - 4 * g) * 128
            sl = i % 4
            bo, bsum = 4 + 2 * (h % 2), 5 + 2 * (h % 2)
            first, last = kt == 0, kt == nkt - 1
            self.mm(self.psum[bo][:, off:512], self.Vaug[:, kt, h * VW:h * VW + 128], pT[sl][:, off:512],
                    first, last, [t_pT[sl], self.t_K], [self.t_ps[bo]], join=not first)
            self.mm(self.psum[bsum][:, off:512], self.ones_b[:], pT[sl][:, off:512],
                    first, last, [t_pT[sl]], [self.t_ps[bsum]], join=not first)
            if last:
                r = rec[h % 2]
                P.op("dve", lambda e, o=r, i_=self.psum[bsum][:]: e.reciprocal(o, i_),
                     reads=[self.t_ps[bsum]], writes=[t_rec[h % 2]])
                self.tt(C["mixT"][:, h, :], self.psum[bo][:], r, ALU.mult, [self.t_ps[bo], t_rec[h % 2]],
                        [T["mixT"]], join=True)

        n = len(its)
        for i in range(n + 1):
            if i < n:
                S(i)
            if i >= 1:
                V(i - 1)
        P.end_stage()

    def gla(self, g, C, T):
        P = self.P
        la = self.af32(40, [4, 512]); t_la = tiles("la", 4)
        kdec = self.abf(48, [4, 512]); t_kd = tiles("kdec", 4)
        eb = [self.af32(52, [512]), self.af32(54, [512])]; t_eb = tiles("eb", 2)
        Sb = [self.abf(72 + i, [2, 256]) for i in range(4)]; t_Sb = tiles("Sb", 4)
        og = self.af32(76, [4, 256]); t_og = Tile("og")
        ogb = [self.abf(80, [1024]), self.abf(82, [1024])]; t_ogb = tiles("ogb", 2)
        dec = self.af32(84, [4, 4, 2]); t_dec = Tile("dec")
        sso = self.af32(84.25, [4]); t_sso = Tile("sso"); t_sso_h = tiles("ssoh", 4)
        junk = self.af32(85, [256]); t_junk = Tile("junkg")
        gqlo, gqhi, gk, gv, szr, zgT = C["gqlo"], C["gqhi"], C["gk"], C["gv"], C["szr"], C["zgT"]
        for tt in range(4):
            b = tt % 2
            self.mm(self.psum[b][:], zgT[0:17, tt * 128:(tt + 1) * 128], self.wg2[0:17, :], True, True,
                    [T["zgT"], self.t_const], [self.t_ps[b]])
            self.act(eb[b], self.psum[b][:], AF.Exp, [self.t_ps[b]], [t_eb[b]], scale=-1.0)
            self.ts(eb[b], eb[b], 1.0, None, ALU.add, None, [t_eb[b]], [t_eb[b]])
            self.act(la[:, tt, :], eb[b], AF.Ln, [t_eb[b]], [t_la[tt]])
            self.ts(la[:, tt, :], la[:, tt, :], -1.0 / 16, None, ALU.mult, None, [t_la[tt]], [t_la[tt]])
        lhi = [self.abf(34, [512]), self.abf(35, [512])]; llo = [self.abf(36, [512]), self.abf(37, [512])]
        lres = self.af32(38, [512])
        t_lh, t_ll, t_lr = tiles("lhi", 2), tiles("llo", 2), Tile("lres")
        for tt in range(4):
            b = tt % 2
            self.copy("act", lhi[b], la[:, tt, :], [t_la[tt]], [t_lh[b]])
            self.tt(lres, la[:, tt, :], lhi[b], ALU.subtract, [t_la[tt], t_lh[b]], [t_lr])
            self.copy("dve", llo[b], lres, [t_lr], [t_ll[b]])
            self.mm(self.psum[b][:], self.tri_b[:], lhi[b], True, False, [t_lh[b]], [self.t_ps[b]])
            self.mm(self.psum[b][:], self.tri_b[:], llo[b], False, True, [t_ll[b]], [self.t_ps[b]], join=True)
            self.act(eb[b], self.psum[b][:], AF.Exp, [self.t_ps[b]], [t_eb[b]])
            self.tt(kdec[:, tt, :], gk[:, tt, :], eb[b], ALU.mult, [T["gk"][tt], t_eb[b]], [t_kd[tt]])
            for h in range(4):
                o = (tt * 4 + h) * 2
                self.mm(self.psum[2][:, o:o + 2], lhi[b][:, h * 128:(h + 1) * 128], self.ind_b[:], True, False,
                        [t_lh[b]], [self.t_ps[2]], join=(o > 0))
                self.mm(self.psum[2][:, o:o + 2], llo[b][:, h * 128:(h + 1) * 128], self.ind_b[:], False, True,
                        [t_ll[b]], [self.t_ps[2]], join=True)
        self.act(dec.rearrange("p a b c -> p (a b c)"), self.psum[2][:, 0:32], AF.Exp, [self.t_ps[2]], [t_dec])
        Stmp = self.af32(34, [4, 256]); t_St = tiles("Stmp", 4)
        ob_ = [5, 6]

        def U(tt):
            for h in range(4):
                cols = (h % 2) * 256
                for half in range(2):
                    p0 = half * 64
                    bank = (3 if half == 0 else 1) + h // 2
                    self.mm(self.psum[bank][:, cols:cols + 256], kdec[p0:p0 + 64, tt, h * 128:(h + 1) * 128],
                            gv[p0:p0 + 64, tt, h * 256:(h + 1) * 256], True, True, [t_kd[tt], T["gv"][tt]],
                            [self.t_ps[bank]], join=(h % 2 > 0))

        def CH(tt):
            for h in range(4):
                cols = (h % 2) * 256
                b0, b1 = 3 + h // 2, 1 + h // 2
                self.stt(Stmp[:, h, :], self.Sst[:, h, :], dec[:, tt, h, 0:1], self.psum[b0][:, cols:cols + 256],
                         ALU.mult, ALU.add, [self.t_Sh[h], t_dec, self.t_ps[b0]], [t_St[h]])
                self.copy("act", Sb[h][:, 0, :], Stmp[:, h, :], [t_St[h]], [t_Sb[h]])
                self.stt(self.Sst[:, h, :], Stmp[:, h, :], dec[:, tt, h, 1:2], self.psum[b1][:, cols:cols + 256],
                         ALU.mult, ALU.add, [t_St[h], t_dec, self.t_ps[b1]], [self.t_Sh[h]])
                self.copy("act", Sb[h][:, 1, :], self.Sst[:, h, :], [self.t_Sh[h]], [t_Sb[h]], join=True)

        def O(tt):
            for h in range(4):
                bo = ob_[h // 2]
                co = (h % 2) * 256
                self.mm(self.psum[bo][:, co:co + 256], gqlo[:, h, tt * 128:(tt + 1) * 128], Sb[h][:, 0, :], True,
                        False, [T["gq"], t_Sb[h]], [self.t_ps[bo]], join=(h % 2 > 0))
                self.mm(self.psum[bo][:, co:co + 256], gqhi[:, h, tt * 128:(tt + 1) * 128], Sb[h][:, 1, :], False,
                        True, [T["gq"], t_Sb[h]], [self.t_ps[bo]], join=True)

        U(0)
        CH(0)
        dtr_pending = []
        for tt in range(4):
            if tt < 3:
                U(tt + 1)
            O(tt)
            while dtr_pending:
                dtr_pending.pop(0)()
            if tt < 3:
                CH(tt + 1)
            self.memset("dve", sso, 0.0, [], [t_sso])
            for h in range(4):
                bo = ob_[h // 2]
                co = (h % 2) * 256
                self.sq_acc(junk, self.psum[bo][:, co:co + 256], sso[:, h:h + 1], [self.t_ps[bo]], t_sso)
            self.ts(sso, sso, 1.0 / 256, EPS, ALU.mult, ALU.add, [t_sso], [t_sso])
            self.act(sso, sso, AF.Ln, [t_sso], [t_sso])
            self.act(sso, sso, AF.Exp, [t_sso], [t_sso], scale=-0.5)
            for hp in range(2):
                self.tt(og[:, hp * 2:hp * 2 + 2, :], self.psum[ob_[hp]][:].rearrange("p (h w) -> p h w", w=256),
                        sso[:, hp * 2:hp * 2 + 2].unsqueeze(2).to_broadcast([128, 2, 256]), ALU.mult,
                        [self.t_ps[ob_[hp]], t_sso], [t_og], join=(hp > 0))
            self.tt(og, og, self.gout_b[:].unsqueeze(1).to_broadcast([128, 4, 256]), ALU.mult, [t_og, self.t_const],
                    [t_og])
            o_b = ogb[tt % 2]
            self.tt(o_b, og.rearrange("p h w -> p (h w)"), szr[:, tt, :], ALU.mult, [t_og, T["szr"][tt]],
                    [t_ogb[tt % 2]])
            def Dtr(tt=tt, o_b=o_b):
                b = 7 if tt % 2 else 0
                pb = self.psbf(b)
                for c in range(8):
                    self.tr(pb[:, c * 128:(c + 1) * 128], o_b[:, c * 128:(c + 1) * 128], self.ident_b[:],
                            [t_ogb[tt % 2]], [self.t_ps[b]], join=(c > 0))
                self.copy("act", C["mixT"][:, 8:16, tt * 128:(tt + 1) * 128],
                          pb[:, :].rearrange("p (k t) -> p k t", t=128), [self.t_ps[b]], [T["mixT"]], join=True)
            dtr_pending.append(Dtr)
        for f_ in dtr_pending:
            f_()
        P.end_stage()

    def wout(self, g, C, T):
        P, din = self.P, self.din
        wo = [self.abf(4 * i, [16, 128]) for i in range(3)]; t_wo = tiles("wo", 3)
        mixT = C["mixT"]
        for co in range(16):
            sl = co % 3
            P.dma("pool", wo[sl], din["w_out"][co], writes=[t_wo[sl]])
            b = co % 2
            for ci in range(16):
                self.mm(self.psum[b][:], wo[sl][:, ci, :], mixT[:, ci, :], ci == 0, ci == 15, [t_wo[sl], T["mixT"]],
                        [self.t_ps[b]], join=(ci > 0))
            self.tt(self.xT[:, co, :], self.psum[b][:], self.xT[:, co, :], ALU.add, [self.t_ps[b], self.t_x[co]],
                    [self.t_x[co]])
        P.end_stage()

    def cross(self, g):
        P, din = self.P, self.din
        hT = self.abf(0, [16, GT]); t_hT = tiles("hTc", 16)
        wq = [self.abf(16 + 4 * i, [16, 128]) for i in range(3)]; t_wq = tiles("mwq", 3)
        qmz = [self.af32(28, [GT]), self.af32(30, [GT])]; t_qmz = tiles("qmz", 2)
        qmT = self.abf(36, [4, GT]); t_qm = tiles("qmT", 4)
        pT = [self.abf(40 + i, [GT]) for i in range(4)]; t_pT = tiles("pTc", 4)
        rec = [self.af32(44, [GT]), self.af32(46, [GT])]; t_rec = tiles("recc", 2)
        omT = self.abf(48, [4, GT]); t_omT = tiles("omT", 4)
        wmo = [self.abf(52 + i, [4, 128]) for i in range(3)]; t_wmo = tiles("wmo", 3)
        sqb = [self.abf(56, [GT]), self.abf(57, [GT])]; t_sq = tiles("sqc", 2)
        rs = [self.af32(58, [GT]), self.af32(60, [GT])]; t_rs = tiles("rsc", 2)
        for h in range(3):
            P.dma("pool", wq[h], din["mwq"][h], writes=[t_wq[h]])
        for co in range(3):
            P.dma("pool", wmo[co], din["mwo"][co], writes=[t_wmo[co]])
        self.norm(self.g_ma, hT, t_hT, 62, 64)

        def proj(h):
            sl, b = h % 3, h % 2
            for c in range(16):
                self.mm(self.psum[b][:], wq[sl][:, c, :], hT[:, c, :], c == 0, c == 15, [t_wq[sl], t_hT[c]],
                        [self.t_ps[b]], join=(c > 0))
            self.copy("act", qmz[b], self.psum[b][:], [self.t_ps[b]], [t_qmz[b]])
            self.act(sqb[b], self.psum[b][:], AF.Square, [self.t_ps[b]], [t_sq[b]])
            if h == 0:
                P.dma("pool", wq[0], din["mwq"][3], writes=[t_wq[0]])

        def qnorm(h):
            b = h % 2
            self.mm(self.psum[2 + b][:], self.ones_b[:], sqb[b], True, True, [t_sq[b]], [self.t_ps[2 + b]])
            self.ts(rs[b], self.psum[2 + b][:], 1.0 / 128, EPS, ALU.mult, ALU.add, [self.t_ps[2 + b]], [t_rs[b]])
            self.act(rs[b], rs[b], AF.Ln, [t_rs[b]], [t_rs[b]])
            self.act(rs[b], rs[b], AF.Exp, [t_rs[b]], [t_rs[b]], scale=-0.5)
            self.stt(qmT[:, h, :], qmz[b], self.g_mq[:, 0:1], rs[b], ALU.mult, ALU.mult,
                     [t_qmz[b], t_rs[b], self.t_const], [t_qm[h]])

        for h in range(5):
            if h < 4:
                proj(h)
            if h >= 1:
                qnorm(h - 1)

        its = [(h, mt) for h in range(4) for mt in range(2)]

        def S(i):
            h, mt = its[i]
            bs, sl = i % 2, i % 4
            self.mm(self.psum[bs][:], self.mKT[:, h, mt * 128:(mt + 1) * 128], qmT[:, h, :], True, True,
                    [self.t_mem, t_qm[h]], [self.t_ps[bs]])
            self.act(pT[sl], self.psum[bs][:], AF.Exp, [self.t_ps[bs]], [t_pT[sl]], scale=128 ** -0.5)

        def V(i):
            h, mt = its[i]
            sl = i % 4
            bo, bsum = 4 + 2 * (h % 2), 5 + 2 * (h % 2)
            self.mm(self.psum[bo][:], self.mV[:, mt, h * VW:h * VW + 128], pT[sl], mt == 0, mt == 1,
                    [t_pT[sl], self.t_mem], [self.t_ps[bo]], join=(mt > 0))
            self.mm(self.psum[bsum][:], self.ones_b[:], pT[sl], mt == 0, mt == 1, [t_pT[sl]], [self.t_ps[bsum]],
                    join=(mt > 0))
            if mt == 1:
                r = rec[h % 2]
                P.op("dve", lambda e, o=r, i_=self.psum[bsum][:]: e.reciprocal(o, i_),
                     reads=[self.t_ps[bsum]], writes=[t_rec[h % 2]])
                self.tt(omT[:, h, :], self.psum[bo][:], r, ALU.mult, [self.t_ps[bo], t_rec[h % 2]], [t_omT[h]])

        for i in range(len(its) + 1):
            if i < len(its):
                S(i)
            if i >= 1:
                V(i - 1)
        for co in range(16):
            sl = co % 3
            b = 2 + co % 2
            for ci in range(4):
                self.mm(self.psum[b][:], wmo[sl][:, ci, :], omT[:, ci, :], ci == 0, ci == 3, [t_wmo[sl], t_omT[ci]],
                        [self.t_ps[b]], join=(ci > 0))
            self.tt(self.xT[:, co, :], self.psum[b][:], self.xT[:, co, :], ALU.add, [self.t_ps[b], self.t_x[co]],
                    [self.t_x[co]])
            if co + 3 < 16:
                P.dma("pool", wmo[sl], din["mwo"][co + 3], writes=[t_wmo[sl]])
        P.end_stage()

    def build(self):
        self.prologue()
        order = ["x", "ffn1", "mix", "cross", "ffn2"]
        lvl = order.index(self.upto) - 1
        for g in range(self.ngroups):
            self.load_x(g)
            if lvl >= 0:
                self.ffn(1)
            if lvl >= 1:
                C = self.carried()
                T = {"hT": tiles("hTm", 16), "gq": Tile("gq"), "gq0": Tile("gq0"), "zgT": Tile("zgT"), "gk": tiles("gk", 4),
                     "gv": tiles("gv", 4), "szr": tiles("szr", 4), "zkr": Tile("zkr"), "cT": tiles("cT", 6),
                     "qT": Tile("qT"), "mixT": Tile("mixT")}
                sub = getattr(self, "sub", 99)
                self.mix_p1(g, C, T)
                if sub >= 2:
                    self.mix_p2(g, C, T)
                if sub >= 3:
                    self.mix_p3a(g, C, T)
                if sub >= 4:
                    self.mix_p3b(g, C, T)
                if sub >= 5:
                    self.mla(g, C, T)
                if sub >= 6:
                    self.gla(g, C, T)
                if sub >= 7:
                    self.wout(g, C, T)
            if lvl >= 2:
                self.cross(g)
            if lvl >= 3:
                self.ffn(2)
            self.store_x(g)
        self.P.close()
        return self.nc


def _chunk_rows(w, ncols_chunk):
    Kd, N = w.shape
    return np.ascontiguousarray(w.reshape(Kd // 128, 128, N // ncols_chunk, ncols_chunk).transpose(2, 1, 0, 3))


def _rows(w):
    Kd, N = w.shape
    return np.ascontiguousarray(w.reshape(Kd // 128, 128, N).transpose(1, 0, 2))


def _col(g):
    return np.ascontiguousarray(g.reshape(-1, 128).T)


def prep_shared(inp):
    f32 = np.float32
    sh = {}
    for k in (1, 2):
        wg = _chunk_rows(np.asarray(inp[f"ffn{k}_w_gate"][0], f32), 128)
        wu = _chunk_rows(np.asarray(inp[f"ffn{k}_w_up"][0], f32), 128)
        sh[f"f{k}_gu"] = np.ascontiguousarray(np.stack([wg, wu], axis=2))
        wd = np.asarray(inp[f"ffn{k}_w_down"][0], f32)
        sh[f"f{k}_dn"] = np.ascontiguousarray(wd.reshape(NQ, JQ, 128, 16, 128).transpose(0, 3, 2, 1, 4))
        sh[f"f{k}_g"] = _col(np.asarray(inp[f"ffn{k}_norm"][0], f32))
    win = np.asarray(inp["w_in"][0], f32)
    fm_cols = np.concatenate([win[:, O_ZQ:O_ZQ + 512], win[:, O_ZKV:O_ZKV + 256], win[:, O_GQ:O_GQ + 512]], axis=1)
    sh["win_fm"] = _chunk_rows(fm_cols, 128)
    sh["win_zg"] = _rows(win[:, O_ZG:O_ZG + 16])
    tm_cols = np.concatenate([win[:, O_GK:O_GK + 512], win[:, O_GV:O_GV + 1024], win[:, O_ZR:O_ZR + 1024]], axis=1)
    sh["win_tm"] = _chunk_rows(tm_cols, 256)
    sh["win_kr"] = _rows(win[:, O_ZKR:O_ZKR + 64])
    sh["mix_g"] = _col(np.asarray(inp["mix_norm"][0], f32))
    sh["qa_g"] = _col(np.asarray(inp["q_a_norm"][0], f32))
    sh["kva_g"] = _col(np.asarray(inp["kv_a_norm"][0], f32))
    sh["wq_up"] = _rows(np.asarray(inp["w_q_up"][0], f32))
    sh["wkv_up"] = _rows(np.asarray(inp["w_kv_up"][0], f32))
    sh["gq_row"] = np.asarray(inp["mla_q_norm"][0], f32).reshape(1, 192).copy()
    sh["gk_row"] = np.asarray(inp["mla_k_norm"][0], f32).reshape(1, 192).copy()
    sh["wg2"] = np.ascontiguousarray(np.concatenate([np.asarray(inp["gla_w_gate2"][0], f32),
                                                     np.asarray(inp["gla_b_gate"][0], f32).reshape(1, 512)], axis=0))
    sh["gout_row"] = np.asarray(inp["gla_out_norm"][0], f32).reshape(1, 256).copy()
    sh["w_out"] = _chunk_rows(np.asarray(inp["w_out"][0], f32), 128)
    sh["ma_g"] = _col(np.asarray(inp["mem_attn_norm"][0], f32))
    sh["mn_g"] = _col(np.asarray(inp["mem_norm"][0], f32))
    sh["mwq"] = _chunk_rows(np.asarray(inp["mem_w_q"][0], f32), 128)
    sh["mwk"] = _chunk_rows(np.asarray(inp["mem_w_k"][0], f32), 128)
    sh["mwv"] = _rows(np.asarray(inp["mem_w_v"][0], f32))
    sh["mwo"] = _chunk_rows(np.asarray(inp["mem_w_o"][0], f32), 128)
    sh["mq_g"] = np.asarray(inp["mem_q_norm"][0], f32).reshape(128, 1).copy()
    sh["mk_g"] = np.asarray(inp["mem_k_norm"][0], f32).reshape(128, 1).copy()
    half = 32
    sh["invf"] = (np.float32(10000.0) ** (-np.arange(half, dtype=np.float32) / np.float32(half))).astype(f32).reshape(1, 32)
    return sh


def core_inputs(inp, b, sh):
    m = dict(sh)
    m["x"] = np.ascontiguousarray(np.asarray(inp["x"][b], np.float32))
    m["mem"] = np.ascontiguousarray(np.asarray(inp["mem"][b], np.float32))
    m["pos"] = np.ascontiguousarray(np.asarray(inp["positions"][b], np.int32).reshape(16, 128).T)
    return m


def kernel(**inputs):
    B = inputs["x"].shape[0]
    sh = prep_shared(inputs)
    nc = bass.Bass("TRN2", target_bir_lowering=False)
    K(nc).build()
    in_maps = [core_inputs(inputs, b, sh) for b in range(B)]
    res = run_bass_kernel_spmd(nc, in_maps, core_ids=list(range(B)))
    return np.stack([r["y"] for r in res.results], axis=0).astype(np.float32)
```

```python
import numpy as np
import concourse.bass as bass
import concourse.mybir as mybir
from concourse.bass_utils import run_bass_kernel_spmd
from contextlib import ExitStack

F32 = mybir.dt.float32
BF16 = mybir.dt.bfloat16
I32 = mybir.dt.int32
AF = mybir.ActivationFunctionType
ALU = mybir.AluOpType

ENGINES = ("pe", "act", "dve", "pool", "sp")

D = 2048
S = 2048
DFF = 5632
NJ = DFF // 128
NQ = 4
JQ = NJ // NQ
GT = 512
NGRP = S // GT
EPS = 1e-6
O_ZQ, O_ZKV, O_ZKR, O_GQ, O_GK, O_GV, O_ZG, O_ZR = 0, 512, 768, 832, 1344, 1856, 2880, 2896
VW = 130
C1_2PI = 6.28125
C2_2PI = 2.0 * np.pi - 6.28125
PI = float(np.pi)


class Tile:
    __slots__ = ("name", "writers", "readers", "sem", "sem_count", "gen_deps", "psum")

    def __init__(self, name):
        self.name = name
        self.psum = False
        self.writers = []
        self.readers = []
        self.gen_deps = []
        self.sem = None
        self.sem_count = 0


def tiles(prefix, n):
    return [Tile(f"{prefix}{i}") for i in range(n)]


class Op:
    __slots__ = ("eng", "kind", "fn", "deps", "signals", "seq", "stage", "dma_sem", "dma_val")

    def __init__(self, eng, kind, fn, stage):
        self.eng = eng
        self.kind = kind
        self.fn = fn
        self.deps = []
        self.signals = False
        self.seq = None
        self.stage = stage
        self.dma_sem = None
        self.dma_val = None


class Prog:
    def __init__(self, nc):
        self.nc = nc
        self.es = ExitStack()
        self.stage = 0
        self.ops = {e: [] for e in ENGINES}
        self.esem = {}
        self.ecount = {e: 0 for e in ENGINES}
        for e in ("pe", "act", "dve", "pool"):
            self.esem[e] = self.es.enter_context(nc.semaphore("sem_" + e))
        self.seen = {e: {} for e in ENGINES}
        self.stage_dmas = {e: [] for e in ENGINES}
        self.n_ops = 0
        self.sem_pool = {"sw": [], "hw": []}
        self.stage_sem_tiles = []
        self.n_sem = 0

    def sbuf(self, name, shape, dt):
        return self.es.enter_context(self.nc.sbuf_tensor("sb_" + name, list(shape), dt))

    def psum(self, name, shape, dt):
        return self.es.enter_context(self.nc.psum_tensor(name, list(shape), dt))

    def new_sem(self, name):
        return self.es.enter_context(self.nc.semaphore(name))

    def _add_deps(self, op, reads, writes, join):
        cur = self.stage
        for t in reads:
            for w in t.writers:
                if w.stage == cur:
                    op.deps.append((w, True))
            if t.psum:
                for r in t.readers:
                    if r.stage == cur and r.eng != op.eng:
                        op.deps.append((r, False))
        for t in writes:
            if not join:
                gd = [w for w in t.writers if w.stage == cur] + [r for r in t.readers if r.stage == cur]
                t.gen_deps = gd
            else:
                gd = [d for d in t.gen_deps if d.stage == cur] + [r for r in t.readers if r.stage == cur]
            for d in gd:
                if d is not op:
                    op.deps.append((d, False))
        for t in reads:
            t.readers.append(op)
        for t in writes:
            if join:
                t.writers.append(op)
            else:
                t.writers = [op]
                t.readers = []

    def op(self, eng, fn, reads=(), writes=(), join=False):
        o = Op(eng, "compute", fn, self.stage)
        self._add_deps(o, reads, writes, join)
        self.ops[eng].append(o)
        self.n_ops += 1
        return o

    def dma(self, eng, out, in_, reads=(), writes=(), join=False, sem_tile=None):
        def fn(e, out=out, in_=in_):
            return e.dma_start(out=out, in_=in_)
        o = Op(eng, "dma", fn, self.stage)
        self._add_deps(o, reads, writes, join)
        st = sem_tile if sem_tile is not None else writes[0]
        if st.sem is None:
            pool = self.sem_pool["sw" if eng == "pool" else "hw"]
            if pool:
                st.sem, st.sem_count = pool.pop()
            else:
                self.n_sem += 1
                st.sem = self.new_sem(f"dsem{self.n_sem}")
                st.sem_count = 0
            self.stage_sem_tiles.append((st, "sw" if eng == "pool" else "hw"))
        st.sem_count += 16
        o.dma_sem = st.sem
        o.dma_val = st.sem_count
        self.ops[eng].append(o)
        self.stage_dmas[eng].append(o)
        self.n_ops += 1
        return o

    def end_stage(self):
        nc = self.nc
        for e in ENGINES:
            for o in self.ops[e]:
                for (p, raw) in o.deps:
                    if p.kind == "compute":
                        if p.eng != o.eng or p.eng != "pe" or o.kind == "dma":
                            p.signals = True
        for e in ("pe", "act", "dve", "pool"):
            c = self.ecount[e]
            for o in self.ops[e]:
                if o.kind == "compute" and o.signals:
                    c += 1
                    o.seq = c
            self.ecount[e] = c

        def emit_engine(ename, eng):
            seen = self.seen[ename]

            def wait(sem, val):
                key = id(sem)
                if seen.get(key, 0) >= val:
                    return
                seen[key] = val
                eng.wait_ge(sem, val)

            for o in self.ops[ename]:
                need = {}
                for (p, raw) in o.deps:
                    if p.kind == "dma":
                        sm, v = p.dma_sem, p.dma_val
                    elif p.eng != ename or p.eng != "pe" or o.kind == "dma":
                        sm, v = self.esem[p.eng], p.seq
                    else:
                        continue
                    k = id(sm)
                    if k not in need or need[k][1] < v:
                        need[k] = (sm, v)
                for (sm, v) in need.values():
                    wait(sm, v)
                ins = o.fn(eng)
                if o.kind == "dma":
                    ins.then_inc(o.dma_sem, 16)
                elif o.signals:
                    ins.then_inc(self.esem[ename], 1)
            need = {}
            for o in self.stage_dmas[ename]:
                k = id(o.dma_sem)
                if k not in need or need[k][1] < o.dma_val:
                    need[k] = (o.dma_sem, o.dma_val)
            for (sm, v) in need.values():
                wait(sm, v)

        with nc.Block(no_gpsimd_drain=True) as blk:
            if self.ops["sp"]:
                @blk.sync
                def _(e):
                    emit_engine("sp", e)
            if self.ops["pe"]:
                @blk.tensor
                def _(e):
                    emit_engine("pe", e)
            if self.ops["act"]:
                @blk.scalar
                def _(e):
                    emit_engine("act", e)
            if self.ops["dve"]:
                @blk.vector
                def _(e):
                    emit_engine("dve", e)
            if self.ops["pool"]:
                @blk.gpsimd
                def _(e):
                    emit_engine("pool", e)
        self.ops = {e: [] for e in ENGINES}
        self.stage_dmas = {e: [] for e in ENGINES}
        for st, kind in self.stage_sem_tiles:
            self.sem_pool[kind].append((st.sem, st.sem_count))
            st.sem = None
        self.stage_sem_tiles = []
        self.stage += 1

    def close(self):
        self.es.close()


class K:
    def __init__(self, nc, ngroups=NGRP, upto="ffn2"):
        self.nc = nc
        self.ngroups = ngroups
        self.upto = upto
        P = self.P = Prog(nc)
        self.din = {}
        self._decl_inputs()
        self.y = nc.dram_tensor("y", [S, D], F32, kind="ExternalOutput").ap()
        sb = P.sbuf
        self.xT = sb("xT", [128, 16, GT], F32)
        self.KTn = sb("KTn", [128, 8, S], BF16)
        self.KTr = sb("KTr", [128, S], BF16)
        self.Vaug = sb("Vaug", [128, 16, 8 * VW], BF16)
        self.rk = sb("rk", [128, 16, 8], F32)
        self.Sst = sb("Sst", [128, 4, 256], F32)
        self.mKT = sb("mKT", [128, 4, 256], BF16)
        self.mV = sb("mV", [128, 2, 4 * VW], BF16)
        self.cosT = sb("cosT", [128, 16, 32], F32)
        self.sinT = sb("sinT", [128, 16, 32], F32)
        self.ident_b = sb("ident_b", [128, 128], BF16)
        self.ident_f = sb("ident_f", [128, 128], F32)
        self.ones_b = sb("ones_b", [128, 128], BF16)
        self.tri_f = sb("tri_f", [128, 128], F32)
        self.ind_f = sb("ind_f", [128, 2], F32)
        self.tri_b = sb("tri_b", [128, 128], BF16)
        self.ind_b = sb("ind_b", [128, 2], BF16)
        self.g_f1 = sb("g_f1", [128, 16], F32)
        self.g_f2 = sb("g_f2", [128, 16], F32)
        self.g_mix = sb("g_mix", [128, 16], F32)
        self.g_ma = sb("g_ma", [128, 16], F32)
        self.g_mn = sb("g_mn", [128, 16], F32)
        self.g_qa = sb("g_qa", [128, 4], F32)
        self.g_kva = sb("g_kva", [128, 2], F32)
        self.gqk_b = sb("gqk_b", [128, 192], F32)
        self.gk_b = sb("gk_b", [128, 192], F32)
        self.gout_b = sb("gout_b", [128, 256], F32)
        self.g_mq = sb("g_mq", [128, 1], F32)
        self.g_mk = sb("g_mk", [128, 1], F32)
        self.wg2 = sb("wg2", [32, 512], BF16)
        self.AW = 86 * 256
        self.arena = sb("arena", [128, self.AW], F32)
        self.psum = [P.psum(f"psb{i}", [128, 512], F32) for i in range(8)]
        self.t_ps = tiles("ps", 8)
        for t in self.t_ps:
            t.psum = True
        self.t_x = tiles("x", 16)
        self.t_const = Tile("const")
        self.t_K = Tile("K")
        self.t_S = Tile("S")
        self.t_Sh = tiles("Sh", 4)
        self.t_mem = Tile("mem")
        self.t_y = tiles("yout", 2)

    def _decl(self, name, shape, dt=F32):
        self.din[name] = self.nc.dram_tensor(name, list(shape), dt, kind="ExternalInput").ap()

    def _decl_inputs(self):
        d = self._decl
        d("x", [S, D]); d("mem", [256, D]); d("pos", [128, 16], I32)
        for k in (1, 2):
            d(f"f{k}_gu", [NJ, 128, 2, 16, 128]); d(f"f{k}_dn", [NQ, 16, 128, JQ, 128]); d(f"f{k}_g", [128, 16])
        d("win_fm", [10, 128, 16, 128]); d("win_zg", [128, 16, 16]); d("win_tm", [10, 128, 16, 256])
        d("win_kr", [128, 16, 64]); d("mix_g", [128, 16]); d("qa_g", [128, 4]); d("kva_g", [128, 2])
        d("wq_up", [128, 4, 1536]); d("wkv_up", [128, 2, 2048]); d("gq_row", [1, 192]); d("gk_row", [1, 192])
        d("wg2", [17, 512]); d("gout_row", [1, 256]); d("w_out", [16, 128, 16, 128])
        d("ma_g", [128, 16]); d("mn_g", [128, 16]); d("mwq", [4, 128, 16, 128]); d("mwk", [4, 128, 16, 128])
        d("mwv", [128, 16, 512]); d("mwo", [16, 128, 4, 128]); d("mq_g", [128, 1]); d("mk_g", [128, 1])
        d("invf", [1, 32])

    def af32(self, off_kib, shape):
        n = int(np.prod(shape))
        o = int(off_kib * 256)
        assert o + n <= self.AW, (off_kib, shape)
        ap = self.arena[:, o:o + n]
        return self._shape(ap, shape)

    def abf(self, off_kib, shape):
        n = int(np.prod(shape))
        w = (n + 1) // 2
        o = int(off_kib * 256)
        assert o + w <= self.AW, (off_kib, shape)
        ap = self.arena[:, o:o + w].bitcast(BF16)[:, 0:n]
        return self._shape(ap, shape)

    @staticmethod
    def _shape(ap, shape):
        if len(shape) == 1:
            return ap
        if len(shape) == 2:
            return ap.rearrange("p (a b) -> p a b", b=shape[1])
        if len(shape) == 3:
            return ap.rearrange("p (a b c) -> p a b c", b=shape[1], c=shape[2])
        raise ValueError

    def psbf(self, b):
        return self.psum[b].bitcast(BF16)

    def mm(self, out, lhsT, rhs, start, stop, reads, writes, join=False):
        self.P.op("pe", lambda e: e.matmul(out, lhsT=lhsT, rhs=rhs, start=start, stop=stop),
                  reads=reads, writes=writes, join=join)

    def tr(self, out, in_, ident, reads, writes, join=False):
        self.P.op("pe", lambda e: e.transpose(out, in_, ident), reads=reads, writes=writes, join=join)

    def act(self, out, in_, func, reads, writes, join=False, scale=1.0, bias=0.0, accum_out=None):
        if accum_out is None:
            self.P.op("act", lambda e: e.activation(out=out, in_=in_, func=func, scale=scale, bias=bias),
                      reads=reads, writes=writes, join=join)
        else:
            self.P.op("act", lambda e: e.activation(out=out, in_=in_, func=func, scale=scale, bias=bias,
                                                   accum_out=accum_out),
                      reads=reads, writes=writes, join=join)

    def copy(self, eng, out, in_, reads, writes, join=False):
        if eng == "act":
            self.act(out, in_, AF.Copy, reads, writes, join)
        else:
            self.P.op(eng, lambda e: e.tensor_copy(out, in_), reads=reads, writes=writes, join=join)

    def tt(self, out, in0, in1, op, reads, writes, join=False, eng="dve"):
        self.P.op(eng, lambda e: e.tensor_tensor(out, in0, in1, op), reads=reads, writes=writes, join=join)

    def ts(self, out, in0, s1, s2, op0, op1, reads, writes, join=False, eng="dve"):
        if op1 is None:
            self.P.op(eng, lambda e: e.tensor_scalar(out, in0, s1, None, op0), reads=reads, writes=writes, join=join)
        else:
            self.P.op(eng, lambda e: e.tensor_scalar(out, in0, s1, s2, op0, op1), reads=reads, writes=writes,
                      join=join)

    def stt(self, out, in0, scalar, in1, op0, op1, reads, writes, join=False):
        self.P.op("dve", lambda e: e.scalar_tensor_tensor(out, in0, scalar, in1, op0, op1),
                  reads=reads, writes=writes, join=join)

    def memset(self, eng, ap, val, reads, writes, join=False):
        self.P.op(eng, lambda e: e.memset(ap, val), reads=reads, writes=writes, join=join)

    def sq_acc(self, junk, src, acc, reads, t_acc):
        self.act(junk, src, AF.Square, list(reads) + [t_acc], [t_acc], join=True, accum_out=acc)

    def rsqrt_inplace(self, ap, t, mult, add):
        self.ts(ap, ap, mult, add, ALU.mult, ALU.add, [t], [t])
        self.act(ap, ap, AF.Ln, [t], [t])
        self.act(ap, ap, AF.Exp, [t], [t], scale=-0.5)

    def prologue(self):
        P, din = self.P, self.din
        tc = self.t_const
        ld = lambda dst, src: P.dma("sp", dst, src, writes=[tc], join=True)
        ld(self.g_f1[:], din["f1_g"]); ld(self.g_f2[:], din["f2_g"]); ld(self.g_mix[:], din["mix_g"])
        ld(self.g_ma[:], din["ma_g"]); ld(self.g_mn[:], din["mn_g"]); ld(self.g_qa[:], din["qa_g"])
        ld(self.g_kva[:], din["kva_g"]); ld(self.g_mq[:], din["mq_g"]); ld(self.g_mk[:], din["mk_g"])
        ld(self.gqk_b[:], din["gq_row"].partition_broadcast(128))
        ld(self.gk_b[:], din["gk_row"].partition_broadcast(128))
        ld(self.gout_b[:], din["gout_row"].partition_broadcast(128))
        P.dma("pool", self.wg2[0:17, :], din["wg2"], writes=[tc], join=True, sem_tile=Tile("const_sw"))
        invf = self.af32(0, [32])
        posi = self.arena[:, 64:80].bitcast(I32)
        posf = self.af32(0.5, [16])
        ld(invf, din["invf"].partition_broadcast(128))
        ld(posi, din["pos"])
        t_c2 = Tile("c2")
        self.memset("pool", self.ident_f[:], 0.0, [], [t_c2])
        P.op("pool", lambda e: e.affine_select(out=self.ident_f[:], in_=self.ident_f[:], pattern=[[-1, 128]],
                                               compare_op=ALU.not_equal, fill=1.0, base=0, channel_multiplier=1),
             reads=[t_c2], writes=[t_c2])
        self.copy("pool", self.ident_b[:], self.ident_f[:], [t_c2], [t_c2], join=True)
        self.memset("pool", self.ones_b[:], 1.0, [], [t_c2], join=True)
        t_tri = Tile("tri")
        self.memset("pool", self.tri_f[:], 1.0, [], [t_tri])
        P.op("pool", lambda e: e.affine_select(out=self.tri_f[:], in_=self.tri_f[:], pattern=[[-1, 128]],
                                               compare_op=ALU.is_gt, fill=0.0, base=0, channel_multiplier=1),
             reads=[t_tri], writes=[t_tri])
        self.memset("pool", self.tri_f[64:128, 0:64], 0.0, [t_tri], [t_tri])
        t_ind = Tile("ind")
        self.memset("pool", self.ind_f[:], 0.0, [], [t_ind])
        self.memset("pool", self.ind_f[0:64, 0:1], 1.0, [t_ind], [t_ind])
        self.memset("pool", self.ind_f[64:128, 1:2], 1.0, [t_ind], [t_ind])
        self.copy("pool", self.ind_b[:], self.ind_f[:], [t_ind], [Tile("indb")])
        self.copy("pool", self.tri_b[:], self.tri_f[:], [t_tri], [Tile("trib")])
        t_v = Tile("vones")
        va = self.Vaug[:].rearrange("p t (h w) -> p t h w", w=VW)
        self.memset("pool", va[:, :, :, 128:129], 1.0, [], [t_v])
        mv = self.mV[:].rearrange("p t (h w) -> p t h w", w=VW)
        self.memset("pool", mv[:, :, :, 128:129], 1.0, [], [t_v], join=True)
        self.memset("pool", self.Sst[:], 0.0, [], [self.t_S])
        self.tt(self.gqk_b[:, 0:128], self.gqk_b[:, 0:128], self.gk_b[:, 0:128], ALU.mult, [tc], [tc])
        t_r = Tile("rope")
        self.copy("dve", posf, posi, [tc], [t_r])
        ang = self.af32(1, [16, 32])
        kf = self.af32(3, [16, 32])
        ki = self.arena[:, 5 * 256:5 * 256 + 512].bitcast(I32).rearrange("p (a b) -> p a b", b=32)
        r = self.af32(7, [16, 32])
        rc = self.af32(9, [16, 32])
        msk = self.af32(11, [16, 32])
        t_a, t_k, t_rr, t_rc, t_m = Tile("ang"), Tile("kf"), Tile("r"), Tile("rc"), Tile("msk")
        self.tt(ang, posf.unsqueeze(2).to_broadcast([128, 16, 32]), invf.unsqueeze(1).to_broadcast([128, 16, 32]),
                ALU.mult, [t_r, tc], [t_a])
        self.ts(kf, ang, 1.0 / (2 * PI), None, ALU.mult, None, [t_a], [t_k])
        self.copy("dve", ki, kf, [t_k], [t_k])
        self.copy("dve", kf, ki, [t_k], [t_k])
        self.stt(r, kf, -C1_2PI, ang, ALU.mult, ALU.add, [t_k, t_a], [t_rr])
        self.stt(r, kf, -C2_2PI, r, ALU.mult, ALU.add, [t_k, t_rr], [t_rr])
        self.ts(r, r, -PI, PI, ALU.max, ALU.min, [t_rr], [t_rr])
        self.act(self.sinT[:], r, AF.Sin, [t_rr], [tc], join=True)
        self.ts(rc, r, PI / 2, None, ALU.add, None, [t_rr], [t_rc])
        self.ts(msk, rc, PI, -2 * PI, ALU.is_gt, ALU.mult, [t_rc], [t_m])
        self.tt(rc, rc, msk, ALU.add, [t_rc, t_m], [t_rc])
        self.ts(rc, rc, -PI, PI, ALU.max, ALU.min, [t_rc], [t_rc])
        self.act(self.cosT[:], rc, AF.Sin, [t_rc], [tc], join=True)
        P.end_stage()
        if self.upto in ("cross", "ffn2"):
            self.mem_kv()

    def mem_kv(self):
        P, din = self.P, self.din
        mem_f = [self.af32(0, [2048]), self.af32(8, [2048])]
        mem_b = [self.abf(16, [2048]), self.abf(20, [2048])]
        junk = self.af32(24, [2048])
        ssm = self.af32(32, [2])
        mnT = self.abf(33, [16, 256])
        wk = [self.abf(41 + 4 * i, [16, 128]) for i in range(3)]
        wv = self.abf(53, [16, 512])
        kmz = self.af32(69, [256])
        sqb = self.abf(70, [256])
        rs = self.af32(71, [256])
        t_mf, t_mb = tiles("mf", 2), tiles("mb", 2)
        t_ss, t_mnT, t_wk, t_wv, t_kmz, t_sqb, t_rs = Tile("ssm"), tiles("mnT", 16), tiles("wk", 3), Tile("wv"), \
            Tile("kmz"), Tile("sqbm"), Tile("rsm")
        P.dma("pool", wv, din["mwv"], writes=[t_wv])
        self.memset("dve", ssm, 0.0, [], [t_ss])
        for mt in range(2):
            P.dma("sp", mem_f[mt], din["mem"][mt * 128:(mt + 1) * 128, :], writes=[t_mf[mt]])
            self.sq_acc(junk, mem_f[mt], ssm[:, mt:mt + 1], [t_mf[mt]], t_ss)
        self.ts(ssm, ssm, 1.0 / D, EPS, ALU.mult, ALU.add, [t_ss], [t_ss])
        self.act(ssm, ssm, AF.Ln, [t_ss], [t_ss])
        self.act(ssm, ssm, AF.Exp, [t_ss], [t_ss], scale=-0.5)
        for mt in range(2):
            self.ts(mem_b[mt], mem_f[mt], ssm[:, mt:mt + 1], None, ALU.mult, None, [t_mf[mt], t_ss], [t_mb[mt]])
            for q in range(4):
                b = (mt * 4 + q) % 4
                pb = self.psbf(b)
                for k in range(4):
                    c = q * 4 + k
                    self.tr(pb[:, k * 128:(k + 1) * 128], mem_b[mt][:, c * 128:(c + 1) * 128], self.ident_b[:],
                            [t_mb[mt]], [self.t_ps[b]], join=(k > 0))
                for k in range(4):
                    c = q * 4 + k
                    self.act(mnT[:, c, mt * 128:(mt + 1) * 128], pb[:, k * 128:(k + 1) * 128], AF.Copy,
                             [self.t_ps[b], self.t_const], [t_mnT[c]], join=True, scale=self.g_mn[:, c:c + 1])
        for h in range(4):
            sl = h % 3
            P.dma("pool", wk[sl], din["mwk"][h], writes=[t_wk[sl]])
            b = 4 + h % 2
            for c in range(16):
                self.mm(self.psum[b][:, 0:256], wk[sl][:, c, :], mnT[:, c, :], c == 0, c == 15,
                        [t_wk[sl], t_mnT[c]], [self.t_ps[b]], join=(c > 0))
            self.copy("act", kmz, self.psum[b][:, 0:256], [self.t_ps[b]], [t_kmz])
            self.act(sqb, self.psum[b][:, 0:256], AF.Square, [self.t_ps[b]], [t_sqb])
            self.mm(self.psum[6][:, 0:256], self.ones_b[:], sqb, True, True, [t_sqb], [self.t_ps[6]])
            self.ts(rs, self.psum[6][:, 0:256], 1.0 / 128, EPS, ALU.mult, ALU.add, [self.t_ps[6]], [t_rs])
            self.act(rs, rs, AF.Ln, [t_rs], [t_rs])
            self.act(rs, rs, AF.Exp, [t_rs], [t_rs], scale=-0.5)
            self.stt(self.mKT[:, h, :], kmz, self.g_mk[:, 0:1], rs, ALU.mult, ALU.mult, [t_kmz, t_rs], [self.t_mem],
                     join=True)
        mv = self.mV[:].rearrange("p t (h w) -> p t h w", w=VW)
        for mt in range(2):
            b = 2 + mt
            for c in range(16):
                self.mm(self.psum[b][:], mnT[:, c, mt * 128:(mt + 1) * 128], wv[:, c, :], c == 0, c == 15,
                        [t_wv, t_mnT[c]], [self.t_ps[b]], join=(c > 0))
            self.copy("dve", mv[:, mt, :, 0:128], self.psum[b][:].rearrange("p (h w) -> p h w", w=128),
                      [self.t_ps[b]], [self.t_mem], join=True)
        P.end_stage()

    def load_x(self, g, base=0, end=True):
        P = self.P
        xin = [self.af32(base, [2048]), self.af32(base + 8, [2048])]
        t_xin = tiles("xin", 2)
        n = 0
        for tt in range(4):
            sl = tt % 2
            r0 = g * GT + tt * 128
            P.dma("sp", xin[sl], self.din["x"][r0:r0 + 128, :], writes=[t_xin[sl]])
            for q in range(4):
                b = n % 8
                n += 1
                for k in range(4):
                    c = q * 4 + k
                    self.tr(self.psum[b][:, k * 128:(k + 1) * 128], xin[sl][:, c * 128:(c + 1) * 128], self.ident_f[:],
                            [t_xin[sl]], [self.t_ps[b]], join=(k > 0))
                self.copy("act" if q % 2 else "dve", self.xT[:, q * 4:(q + 1) * 4, tt * 128:(tt + 1) * 128],
                          self.psum[b][:].rearrange("p (k t) -> p k t", t=128),
                          [self.t_ps[b]], [self.t_x[c] for c in range(q * 4, q * 4 + 4)], join=True)
        if end:
            P.end_stage()

    def store_x(self, g, end=True):
        P = self.P
        yo = [self.af32(0, [2048]), self.af32(8, [2048])]
        t_yo = tiles("yo", 2)
        n = 0
        for tt in range(4):
            sl = tt % 2
            for q in range(4):
                b = n % 8
                n += 1
                for k in range(4):
                    c = q * 4 + k
                    self.tr(self.psum[b][:, k * 128:(k + 1) * 128], self.xT[:, c, tt * 128:(tt + 1) * 128],
                            self.ident_f[:], [self.t_x[c]], [self.t_ps[b]], join=(k > 0))
                self.copy("act" if q % 2 else "dve", yo[sl][:, q * 512:(q + 1) * 512], self.psum[b][:],
                          [self.t_ps[b]], [t_yo[sl]], join=(q > 0))
            r0 = g * GT + tt * 128
            P.dma("sp", self.y[r0:r0 + 128, :], yo[sl], reads=[t_yo[sl]], writes=[self.t_y[sl]])
        if end:
            P.end_stage()

    def norm(self, gcol, hT, t_hT, sq_off, rs_off):
        sqb = [self.abf(sq_off, [GT]), self.abf(sq_off + 1, [GT])]
        rs = self.af32(rs_off, [GT])
        t_sq, t_rs = tiles("nsq", 2), Tile("nrs")
        for c in range(16):
            if c % 2 == 0:
                self.act(sqb[0], self.xT[:, c, :], AF.Square, [self.t_x[c]], [t_sq[0]])
            else:
                self.tt(sqb[1], self.xT[:, c, :], self.xT[:, c, :], ALU.mult, [self.t_x[c]], [t_sq[1]])
            self.mm(self.psum[7][:], self.ones_b[:], sqb[c % 2], c == 0, c == 15, [t_sq[c % 2]], [self.t_ps[7]],
                    join=(c > 0))
        self.ts(rs, self.psum[7][:], 1.0 / D, EPS, ALU.mult, ALU.add, [self.t_ps[7]], [t_rs])
        self.act(rs, rs, AF.Ln, [t_rs], [t_rs])
        self.act(rs, rs, AF.Exp, [t_rs], [t_rs], scale=-0.5)
        for c in range(16):
            self.stt(hT[:, c, :], self.xT[:, c, :], gcol[:, c:c + 1], rs, ALU.mult, ALU.mult,
                     [self.t_x[c], t_rs, self.t_const], [t_hT[c]])

    def ffn(self, k):
        P, din = self.P, self.din
        gu_d, dn_d = din[f"f{k}_gu"], din[f"f{k}_dn"]
        gcol = self.g_f1 if k == 1 else self.g_f2
        hT = self.abf(0, [16, GT]); t_hT = tiles("hT", 16)
        aT = [self.abf(16, [JQ, GT]), self.abf(27, [JQ, GT])]
        t_aT = [tiles("aTa", JQ), tiles("aTb", JQ)]
        wgu = [self.abf(38 + 8 * i, [2 * 16, 128]) for i in range(3)]; t_wgu = tiles("wgu", 3)
        wdn = [self.abf(62 + 2.75 * i, [JQ, 128]) for i in range(4)]; t_wdn = tiles("wdn", 4)
        sg = [self.af32(73, [GT]), self.af32(75, [GT])]; t_sg = tiles("sg", 2)
        self.norm(gcol, hT, t_hT, 77, 79)
        def dma_gu(n):
            P.dma("pool", wgu[n % 3], gu_d[n].rearrange("p a c f -> p (a c) f"), writes=[t_wgu[n % 3]])

        def dma_dn(m):
            P.dma("pool", wdn[m % 4], dn_d[m // 16, m % 16], writes=[t_wdn[m % 4]])

        for n in range(3):
            dma_gu(n)
        for s in range(NQ):
            a = aT[s % 2]
            ta = t_aT[s % 2]
            for m in range(4):
                dma_dn(s * 16 + m)
            for jj in range(JQ):
                n_gu = s * JQ + jj
                sl = n_gu % 3
                bg, bu = n_gu % 2, 2 + n_gu % 2
                for half, b in ((0, bg), (1, bu)):
                    for c in range(16):
                        self.mm(self.psum[b][:], wgu[sl][:, half * 16 + c, :], hT[:, c, :], c == 0, c == 15,
                                [t_wgu[sl], t_hT[c]], [self.t_ps[b]], join=(c > 0))
                self.act(sg[n_gu % 2], self.psum[bg][:], AF.Silu, [self.t_ps[bg]], [t_sg[n_gu % 2]])
                self.tt(a[:, jj, :], sg[n_gu % 2], self.psum[bu][:], ALU.mult, [t_sg[n_gu % 2], self.t_ps[bu]],
                        [ta[jj]])
                if n_gu + 3 < NJ:
                    dma_gu(n_gu + 3)
            for c in range(16):
                n_dn = s * 16 + c
                sl = n_dn % 4
                b = 4 + n_dn % 2
                for jj in range(JQ):
                    self.mm(self.psum[b][:], wdn[sl][:, jj, :], a[:, jj, :], jj == 0, jj == JQ - 1,
                            [t_wdn[sl], ta[jj]], [self.t_ps[b]], join=(jj > 0))
                self.stt(self.xT[:, c, :], self.psum[b][:], 0.5, self.xT[:, c, :], ALU.mult, ALU.add,
                         [self.t_ps[b], self.t_x[c]], [self.t_x[c]])
                if c + 4 < 16:
                    dma_dn(n_dn + 4)
        P.end_stage()

    def carried(self):
        c = {}
        c["gqlo"] = self.abf(0, [4, GT]); c["gqhi"] = self.abf(4, [4, GT])
        c["zgT"] = self.abf(8, [GT])
        c["gk"] = self.af32(9, [4, 512]); c["gv"] = self.abf(17, [4, 1024]); c["szr"] = self.abf(25, [4, 1024])
        c["zkr"] = self.af32(33, [4, 64])
        c["cT"] = self.abf(34, [6, GT])
        c["qTn"] = self.abf(40, [8, GT]); c["qTr"] = self.abf(48, [8, GT])
        c["mixT"] = self.abf(56, [16, GT])
        return c

    def mix_p1(self, g, C, T):
        P, din = self.P, self.din
        hT = self.abf(40, [16, GT]); t_hT = T["hT"]
        wfm = [self.abf(56 + 4 * i, [16, 128]) for i in range(3)]; t_w = tiles("wfm", 3)
        zT = self.af32(68, [6, GT]); t_z = tiles("zT", 6)
        self.norm(self.g_mix, hT, t_hT, 80, 82)
        sqb = [self.abf(80, [GT]), self.abf(81, [GT])]; t_sq = tiles("p1sq", 2)
        rsq, rskv = self.af32(82, [GT]), self.af32(84, [GT]); t_rq, t_rkv = Tile("rsq"), Tile("rskv")
        self.memset("pool", C["gqlo"], 0.0, [], [T["gq0"]])
        self.memset("pool", C["gqhi"], 0.0, [], [T["gq0"]], join=True)
        self.memset("pool", C["zgT"][0:32, :], 1.0, [], [T["zgT"]])
        n = 0
        pending = None
        for ck in range(10):
            sl = n % 3
            P.dma("pool", wfm[sl], din["win_fm"][ck], writes=[t_w[sl]])
            b = n % 2
            n += 1
            for c in range(16):
                self.mm(self.psum[b][:], wfm[sl][:, c, :], hT[:, c, :], c == 0, c == 15, [t_w[sl], t_hT[c]],
                        [self.t_ps[b]], join=(c > 0))
            if pending is not None:
                pending()
                pending = None
            if ck < 6:
                self.copy("act", zT[:, ck, :], self.psum[b][:], [self.t_ps[b]], [t_z[ck]])
                self.act(sqb[ck % 2], self.psum[b][:], AF.Square, [self.t_ps[b]], [t_sq[ck % 2]])
                def _post(ck=ck):
                    if ck < 4:
                        self.mm(self.psum[2][:], self.ones_b[:], sqb[ck % 2], ck == 0, ck == 3, [t_sq[ck % 2]],
                                [self.t_ps[2]], join=(ck > 0))
                    else:
                        self.mm(self.psum[3][:], self.ones_b[:], sqb[ck % 2], ck == 4, ck == 5, [t_sq[ck % 2]],
                                [self.t_ps[3]], join=(ck > 4))
                    if ck == 3:
                        self.ts(rsq, self.psum[2][:], 1.0 / 512, EPS, ALU.mult, ALU.add, [self.t_ps[2]], [t_rq])
                        self.act(rsq, rsq, AF.Ln, [t_rq], [t_rq])
                        self.act(rsq, rsq, AF.Exp, [t_rq], [t_rq], scale=-0.5)
                        for c4 in range(4):
                            self.stt(C["cT"][:, c4, :], zT[:, c4, :], self.g_qa[:, c4:c4 + 1], rsq, ALU.mult, ALU.mult,
                                     [t_z[c4], t_rq, self.t_const], [T["cT"][c4]])
                    if ck == 5:
                        self.ts(rskv, self.psum[3][:], 1.0 / 256, EPS, ALU.mult, ALU.add, [self.t_ps[3]], [t_rkv])
                        self.act(rskv, rskv, AF.Ln, [t_rkv], [t_rkv])
                        self.act(rskv, rskv, AF.Exp, [t_rkv], [t_rkv], scale=-0.5)
                        for c2 in range(2):
                            self.stt(C["cT"][:, 4 + c2, :], zT[:, 4 + c2, :], self.g_kva[:, c2:c2 + 1], rskv, ALU.mult,
                                     ALU.mult, [t_z[4 + c2], t_rkv, self.t_const], [T["cT"][4 + c2]])
                pending = _post
            else:
                h = ck - 6
                pv = self.psum[b][:].rearrange("p (t a w) -> p t a w", a=2, w=64)
                lo = C["gqlo"][:, h, :].rearrange("p (t a w) -> p t a w", a=2, w=64)
                hi = C["gqhi"][:, h, :].rearrange("p (t a w) -> p t a w", a=2, w=64)
                self.act(lo[:, :, 0, :], pv[:, :, 0, :], AF.Copy, [self.t_ps[b], T["gq0"]], [T["gq"]], join=True,
                         scale=128 ** -0.5)
                self.act(hi[:, :, 1, :], pv[:, :, 1, :], AF.Copy, [self.t_ps[b], T["gq0"]], [T["gq"]], join=True,
                         scale=128 ** -0.5)
        wzg = self.abf(56, [16, 16]); t_wzg = t_w[0]
        P.dma("pool", wzg, din["win_zg"], writes=[t_wzg])
        for c in range(16):
            self.mm(self.psum[4][0:16, :], wzg[:, c, :], hT[:, c, :], c == 0, c == 15, [t_wzg, t_hT[c]],
                    [self.t_ps[4]], join=(c > 0))
        self.copy("act", C["zgT"][0:16, :], self.psum[4][0:16, :], [self.t_ps[4], T["zgT"]], [T["zgT"]])
        P.end_stage()

    def mix_p2(self, g, C, T):
        P, din = self.P, self.din
        hT = self.abf(40, [16, GT]); t_hT = T["hT"]
        wtm = [self.abf(56, [16, 256]), self.abf(64, [16, 256])]; t_w = tiles("wtm", 2)
        n = 0
        for cg in range(10):
            sl = cg % 2
            P.dma("pool", wtm[sl], din["win_tm"][cg], writes=[t_w[sl]])
            for tt in range(4):
                b = n % 4
                n += 1
                for c in range(16):
                    self.mm(self.psum[b][:, 0:256], hT[:, c, tt * 128:(tt + 1) * 128], wtm[sl][:, c, :], c == 0,
                            c == 15, [t_w[sl], t_hT[c]], [self.t_ps[b]], join=(c > 0))
                src = self.psum[b][:, 0:256]
                if cg < 2:
                    self.copy("act" if n % 2 else "dve", C["gk"][:, tt, cg * 256:(cg + 1) * 256], src,
                              [self.t_ps[b]], [T["gk"][tt]], join=True)
                elif cg < 6:
                    o = (cg - 2) * 256
                    self.copy("act" if n % 2 else "dve", C["gv"][:, tt, o:o + 256], src, [self.t_ps[b]],
                              [T["gv"][tt]], join=True)
                else:
                    o = (cg - 6) * 256
                    self.act(C["szr"][:, tt, o:o + 256], src, AF.Silu, [self.t_ps[b]], [T["szr"][tt]], join=True)
        wkr = self.abf(72, [16, 64]); t_wkr = Tile("wkr")
        P.dma("pool", wkr, din["win_kr"], writes=[t_wkr])
        for tt in range(4):
            b = 4 + tt % 2
            for c in range(16):
                self.mm(self.psum[b][:, 0:64], hT[:, c, tt * 128:(tt + 1) * 128], wkr[:, c, :], c == 0, c == 15,
                        [t_wkr, t_hT[c]], [self.t_ps[b]], join=(c > 0))
            self.copy("dve", C["zkr"][:, tt, :], self.psum[b][:, 0:64], [self.t_ps[b]], [T["zkr"]], join=True)

    def mix_p3a(self, g, C, T):
        P, din = self.P, self.din
        wkv = self.abf(74, [2, 2048]); t_wkv = Tile("wkv")
        P.dma("pool", wkv, din["wkv_up"], writes=[t_wkv])
        junk = self.af32(82, [128]); t_junk = Tile("junk")
        ssk = self.af32(82.5, [4, 8]); sskr = self.af32(82.625, [4]); t_ssk = Tile("ssk")
        kr = self.af32(82.75, [4, 64]); t_kr = Tile("kr")
        tmp = [self.af32(83.75 + 0.5 * i, [4, 32]) for i in range(2)]; t_tmp = tiles("rtmp", 2)
        krb = self.abf(84.75, [4, 64]); t_krb = Tile("krb")
        va = self.Vaug[:].rearrange("p t (h w) -> p t h w", w=VW)
        cT = C["cT"]
        n = 0
        self.memset("dve", ssk, 0.0, [], [t_ssk])
        self.memset("dve", sskr, 0.0, [], [t_ssk], join=True)
        for tt in range(4):
            Tg = g * 4 + tt
            for cgi in range(4):
                b = n % 4
                n += 1
                for c in range(2):
                    self.mm(self.psum[b][:], cT[:, 4 + c, tt * 128:(tt + 1) * 128], wkv[:, c, cgi * 512:(cgi + 1) * 512],
                            c == 0, c == 1, [t_wkv, T["cT"][4 + c]], [self.t_ps[b]], join=(c > 0))
                for hh in range(2):
                    h = cgi * 2 + hh
                    self.sq_acc(junk, self.psum[b][:, hh * 256:hh * 256 + 128], ssk[:, tt, h:h + 1], [self.t_ps[b]],
                                t_ssk)
                self.copy("act", va[:, Tg, cgi * 2:cgi * 2 + 2, 0:128],
                          self.psum[b][:].rearrange("p (h w) -> p h w", w=256)[:, :, 128:256],
                          [self.t_ps[b]], [self.t_K], join=True)
            self.sq_acc(junk[:, 0:64], C["zkr"][:, tt, :], sskr[:, tt:tt + 1], [T["zkr"]], t_ssk)
        self.tt(ssk, ssk, sskr.unsqueeze(2).to_broadcast([128, 4, 8]), ALU.add, [t_ssk], [t_ssk])
        self.rsqrt_inplace(ssk, t_ssk, 1.0 / 192, EPS)
        self.ts(self.rk[:, g * 4:(g + 1) * 4, :], ssk, 192 ** -0.5, None, ALU.mult, None, [t_ssk], [self.t_K],
                join=True)
        self.tt(kr, C["zkr"], self.gk_b[:, 128:192].unsqueeze(1).to_broadcast([128, 4, 64]), ALU.mult,
                [T["zkr"], self.t_const], [t_kr])
        cs = self.cosT[:, g * 4:(g + 1) * 4, :]
        sn = self.sinT[:, g * 4:(g + 1) * 4, :]
        x1, x2 = kr[:, :, 0:32], kr[:, :, 32:64]
        self.tt(tmp[0], x1, cs, ALU.mult, [t_kr], [t_tmp[0]])
        self.tt(tmp[1], x2, sn, ALU.mult, [t_kr], [t_tmp[1]])
        self.tt(krb[:, :, 0:32], tmp[0], tmp[1], ALU.subtract, [t_tmp[0], t_tmp[1]], [t_krb])
        self.tt(tmp[0], x2, cs, ALU.mult, [t_kr], [t_tmp[0]])
        self.tt(tmp[1], x1, sn, ALU.mult, [t_kr], [t_tmp[1]])
        self.tt(krb[:, :, 32:64], tmp[0], tmp[1], ALU.add, [t_tmp[0], t_tmp[1]], [t_krb], join=True)
        pb = self.psbf(4)
        for tt in range(4):
            self.tr(pb[0:64, tt * 128:(tt + 1) * 128], krb[:, tt, :], self.ident_b[:], [t_krb], [self.t_ps[4]],
                    join=(tt > 0))
        self.copy("act", self.KTr[0:64, g * GT:(g + 1) * GT], pb[0:64, 0:512], [self.t_ps[4]], [self.t_K], join=True)
        for h in range(8):
            b = 5 + h % 3
            for c in range(2):
                self.mm(self.psum[b][:], wkv[:, c, h * 256:h * 256 + 128], cT[:, 4 + c, :], c == 0, c == 1,
                        [t_wkv, T["cT"][4 + c]], [self.t_ps[b]], join=(c > 0))
            self.copy("act" if h % 2 else "dve", self.KTn[:, h, g * GT:(g + 1) * GT], self.psum[b][:],
                      [self.t_ps[b]], [self.t_K], join=True)
        P.end_stage()

    def mix_p3b(self, g, C, T):
        P, din = self.P, self.din
        wq = self.abf(56, [4, 1536]); t_wq = Tile("wq")
        P.dma("pool", wq, din["wq_up"], writes=[t_wq])
        qf = [self.af32(68, [8, 192]), self.af32(74, [8, 192])]; t_qf = tiles("qf", 2)
        qb = self.abf(80, [8, 192]); t_qb = Tile("qb")
        tmp = [self.af32(83, [8, 32]), self.af32(84, [8, 32])]; t_tmp = tiles("qtmp", 2)
        junk = self.af32(33, [192])
        ssq = [self.af32(33.75, [8]), self.af32(33.78125, [8])]; t_ssq = tiles("ssq", 2)
        cT = C["cT"]

        def A_pe(tt):
            for cg in range(3):
                for c in range(4):
                    self.mm(self.psum[cg][:], cT[:, c, tt * 128:(tt + 1) * 128], wq[:, c, cg * 512:(cg + 1) * 512],
                            c == 0, c == 3, [t_wq, T["cT"][c]], [self.t_ps[cg]], join=(c > 0))

        def A_post(tt):
            q, tq, sq, tsq = qf[tt % 2], t_qf[tt % 2], ssq[tt % 2], t_ssq[tt % 2]
            qff = q.rearrange("p h w -> p (h w)")
            for cg in range(3):
                self.copy("act" if cg == 1 else "dve", qff[:, cg * 512:(cg + 1) * 512], self.psum[cg][:],
                          [self.t_ps[cg]], [tq], join=(cg > 0))
            self.memset("dve", sq, 0.0, [], [tsq])
            for h in range(8):
                self.sq_acc(junk, q[:, h, :], sq[:, h:h + 1], [tq], tsq)
            self.ts(sq, sq, 1.0 / 192, EPS, ALU.mult, ALU.add, [tsq], [tsq])
            self.act(sq, sq, AF.Ln, [tsq], [tsq])
            self.act(sq, sq, AF.Exp, [tsq], [tsq], scale=-0.5)

        def B(tt):
            Tg = g * 4 + tt
            q, tq, sq, tsq = qf[tt % 2], t_qf[tt % 2], ssq[tt % 2], t_ssq[tt % 2]
            self.tt(q, q, sq.unsqueeze(2).to_broadcast([128, 8, 192]), ALU.mult, [tq, tsq], [tq])
            self.tt(q, q, self.gqk_b[:].unsqueeze(1).to_broadcast([128, 8, 192]), ALU.mult, [tq, self.t_const], [tq])
            cs = self.cosT[:, Tg, :].unsqueeze(1).to_broadcast([128, 8, 32])
            sn = self.sinT[:, Tg, :].unsqueeze(1).to_broadcast([128, 8, 32])
            x1, x2 = q[:, :, 128:160], q[:, :, 160:192]
            self.copy("act", qb[:, :, 0:128], q[:, :, 0:128], [tq], [t_qb])
            self.tt(tmp[0], x1, cs, ALU.mult, [tq], [t_tmp[0]])
            self.tt(tmp[1], x2, sn, ALU.mult, [tq], [t_tmp[1]])
            self.tt(qb[:, :, 128:160], tmp[0], tmp[1], ALU.subtract, [t_tmp[0], t_tmp[1]], [t_qb], join=True)
            self.tt(tmp[0], x2, cs, ALU.mult, [tq], [t_tmp[0]])
            self.tt(tmp[1], x1, sn, ALU.mult, [tq], [t_tmp[1]])
            self.tt(qb[:, :, 160:192], tmp[0], tmp[1], ALU.add, [t_tmp[0], t_tmp[1]], [t_qb], join=True)

        def Ct(tt):
            for hq in range(2):
                b = 3 + hq
                pb = self.psbf(b)
                for k in range(4):
                    h = hq * 4 + k
                    self.tr(pb[:, k * 128:(k + 1) * 128], qb[:, h, 0:128], self.ident_b[:], [t_qb], [self.t_ps[b]],
                            join=(k > 0))
                self.copy("act" if hq else "dve", C["qTn"][:, hq * 4:(hq + 1) * 4, tt * 128:(tt + 1) * 128],
                          pb[:, 0:512].rearrange("p (k t) -> p k t", t=128), [self.t_ps[b]], [T["qT"]], join=True)
            pb = self.psbf(5)
            for h in range(8):
                self.tr(pb[0:64, h * 128:(h + 1) * 128], qb[:, h, 128:192], self.ident_b[:], [t_qb], [self.t_ps[5]],
                        join=(h > 0))
            self.copy("dve", C["qTr"][0:64, :, tt * 128:(tt + 1) * 128],
                      pb[0:64, :].rearrange("p (k t) -> p k t", t=128), [self.t_ps[5]], [T["qT"]], join=True)

        A_pe(0)
        A_post(0)
        for tt in range(4):
            if tt < 3:
                A_pe(tt + 1)
            B(tt)
            if tt < 3:
                A_post(tt + 1)
            Ct(tt)
        P.end_stage()

    def mla(self, g, C, T):
        P = self.P
        pT = [self.abf(72 + i, [GT]) for i in range(4)]; t_pT = tiles("pT", 4)
        rec = [self.af32(76, [GT]), self.af32(78, [GT])]; t_rec = tiles("rec", 2)
        qTn, qTr = C["qTn"], C["qTr"]
        nkt = 4 * g + 4
        its = [(h, kt) for h in range(8) for kt in range(nkt)]

        def S(i):
            h, kt = its[i]
            off = max(0, kt - 4 * g) * 128
            bs, sl = i % 3, i % 4
            self.mm(self.psum[bs][:, off:512], self.KTn[:, h, kt * 128:(kt + 1) * 128], qTn[:, h, off:512],
                    True, False, [self.t_K, T["qT"]], [self.t_ps[bs]])
            self.mm(self.psum[bs][:, off:512], self.KTr[0:64, kt * 128:(kt + 1) * 128], qTr[0:64, h, off:512],
                    False, True, [self.t_K, T["qT"]], [self.t_ps[bs]], join=True)
            self.act(pT[sl][:, off:512], self.psum[bs][:, off:512], AF.Exp, [self.t_ps[bs], self.t_K],
                     [t_pT[sl]], scale=self.rk[:, kt, h:h + 1])
            if kt >= 4 * g:
                self.memset("pool", pT[sl][64:128, off:off + 64], 0.0, [t_pT[sl]], [t_pT[sl]])

        def V(i):
            h, kt = its[i]
            off = max(0, kt - 4 * g) * 128
            sl = i % 4
            bo, bsum = 4 + 2 * (h % 2), 5 + 2 * (h % 2)
            first, last = kt == 0, kt == nkt - 1
            self.mm(self.psum[bo][:, off:512], self.Vaug[:, kt, h * VW:h * VW + 128], pT[sl][:, off:512],
                    first, last, [t_pT[sl], self.t_K], [self.t_ps[bo]], join=not first)
            self.mm(self.psum[bsum][:, off:512], self.ones_b[:], pT[sl][:, off:512],
                    first, last, [t_pT[sl]], [self.t_ps[bsum]], join=not first)
            if last:
                r = rec[h % 2]
                P.op("dve", lambda e, o=r, i_=self.psum[bsum][:]: e.reciprocal(o, i_),
                     reads=[self.t_ps[bsum]], writes=[t_rec[h % 2]])
                self.tt(C["mixT"][:, h, :], self.psum[bo][:], r, ALU.mult, [self.t_ps[bo], t_rec[h % 2]],
                        [T["mixT"]], join=True)

        n = len(its)
        for i in range(n + 1):
            if i < n:
                S(i)
            if i >= 1:
                V(i - 1)
        P.end_stage()

    def gla(self, g, C, T):
        P = self.P
        la = self.af32(40, [4, 512]); t_la = tiles("la", 4)
        kdec = self.abf(48, [4, 512]); t_kd = tiles("kdec", 4)
        eb = [self.af32(52, [512]), self.af32(54, [512])]; t_eb = tiles("eb", 2)
        Sb = [self.abf(72 + i, [2, 256]) for i in range(4)]; t_Sb = tiles("Sb", 4)
        og = self.af32(76, [4, 256]); t_og = Tile("og")
        ogb = [self.abf(80, [1024]), self.abf(82, [1024])]; t_ogb = tiles("ogb", 2)
        dec = self.af32(84, [4, 4, 2]); t_dec = Tile("dec")
        sso = self.af32(84.25, [4]); t_sso = Tile("sso"); t_sso_h = tiles("ssoh", 4)
        junk = self.af32(85, [256]); t_junk = Tile("junkg")
        gqlo, gqhi, gk, gv, szr, zgT = C["gqlo"], C["gqhi"], C["gk"], C["gv"], C["szr"], C["zgT"]
        for tt in range(4):
            b = tt % 2
            self.mm(self.psum[b][:], zgT[0:17, tt * 128:(tt + 1) * 128], self.wg2[0:17, :], True, True,
                    [T["zgT"], self.t_const], [self.t_ps[b]])
            self.act(eb[b], self.psum[b][:], AF.Exp, [self.t_ps[b]], [t_eb[b]], scale=-1.0)
            self.ts(eb[b], eb[b], 1.0, None, ALU.add, None, [t_eb[b]], [t_eb[b]])
            self.act(la[:, tt, :], eb[b], AF.Ln, [t_eb[b]], [t_la[tt]])
            self.ts(la[:, tt, :], la[:, tt, :], -1.0 / 16, None, ALU.mult, None, [t_la[tt]], [t_la[tt]])
        lhi = [self.abf(34, [512]), self.abf(35, [512])]; llo = [self.abf(36, [512]), self.abf(37, [512])]
        lres = self.af32(38, [512])
        t_lh, t_ll, t_lr = tiles("lhi", 2), tiles("llo", 2), Tile("lres")
        for tt in range(4):
            b = tt % 2
            self.copy("act", lhi[b], la[:, tt, :], [t_la[tt]], [t_lh[b]])
            self.tt(lres, la[:, tt, :], lhi[b], ALU.subtract, [t_la[tt], t_lh[b]], [t_lr])
            self.copy("dve", llo[b], lres, [t_lr], [t_ll[b]])
            self.mm(self.psum[b][:], self.tri_b[:], lhi[b], True, False, [t_lh[b]], [self.t_ps[b]])
            self.mm(self.psum[b][:], self.tri_b[:], llo[b], False, True, [t_ll[b]], [self.t_ps[b]], join=True)
            self.act(eb[b], self.psum[b][:], AF.Exp, [self.t_ps[b]], [t_eb[b]])
            self.tt(kdec[:, tt, :], gk[:, tt, :], eb[b], ALU.mult, [T["gk"][tt], t_eb[b]], [t_kd[tt]])
            for h in range(4):
                o = (tt * 4 + h) * 2
                self.mm(self.psum[2][:, o:o + 2], lhi[b][:, h * 128:(h + 1) * 128], self.ind_b[:], True, False,
                        [t_lh[b]], [self.t_ps[2]], join=(o > 0))
                self.mm(self.psum[2][:, o:o + 2], llo[b][:, h * 128:(h + 1) * 128], self.ind_b[:], False, True,
                        [t_ll[b]], [self.t_ps[2]], join=True)
        self.act(dec.rearrange("p a b c -> p (a b c)"), self.psum[2][:, 0:32], AF.Exp, [self.t_ps[2]], [t_dec])
        Stmp = self.af32(34, [4, 256]); t_St = tiles("Stmp", 4)
        ob_ = [5, 6]

        def U(tt):
            for h in range(4):
                cols = (h % 2) * 256
                for half in range(2):
                    p0 = half * 64
                    bank = (3 if half == 0 else 1) + h // 2
                    self.mm(self.psum[bank][:, cols:cols + 256], kdec[p0:p0 + 64, tt, h * 128:(h + 1) * 128],
                            gv[p0:p0 + 64, tt, h * 256:(h + 1) * 256], True, True, [t_kd[tt], T["gv"][tt]],
                            [self.t_ps[bank]], join=(h % 2 > 0))

        def CH(tt):
            for h in range(4):
                cols = (h % 2) * 256
                b0, b1 = 3 + h // 2, 1 + h // 2
                self.stt(Stmp[:, h, :], self.Sst[:, h, :], dec[:, tt, h, 0:1], self.psum[b0][:, cols:cols + 256],
                         ALU.mult, ALU.add, [self.t_Sh[h], t_dec, self.t_ps[b0]], [t_St[h]])
                self.copy("act", Sb[h][:, 0, :], Stmp[:, h, :], [t_St[h]], [t_Sb[h]])
                self.stt(self.Sst[:, h, :], Stmp[:, h, :], dec[:, tt, h, 1:2], self.psum[b1][:, cols:cols + 256],
                         ALU.mult, ALU.add, [t_St[h], t_dec, self.t_ps[b1]], [self.t_Sh[h]])
                self.copy("act", Sb[h][:, 1, :], self.Sst[:, h, :], [self.t_Sh[h]], [t_Sb[h]], join=True)

        def O(tt):
            for h in range(4):
                bo = ob_[h // 2]
                co = (h % 2) * 256
                self.mm(self.psum[bo][:, co:co + 256], gqlo[:, h, tt * 128:(tt + 1) * 128], Sb[h][:, 0, :], True,
                        False, [T["gq"], t_Sb[h]], [self.t_ps[bo]], join=(h % 2 > 0))
                self.mm(self.psum[bo][:, co:co + 256], gqhi[:, h, tt * 128:(tt + 1) * 128], Sb[h][:, 1, :], False,
                        True, [T["gq"], t_Sb[h]], [self.t_ps[bo]], join=True)

        U(0)
        CH(0)
        dtr_pending = []
        for tt in range(4):
            if tt < 3:
                U(tt + 1)
            O(tt)
            while dtr_pending:
                dtr_pending.pop(0)()
            if tt < 3:
                CH(tt + 1)
            self.memset("dve", sso, 0.0, [], [t_sso])
            for h in range(4):
                bo = ob_[h // 2]
                co = (h % 2) * 256
                self.sq_acc(junk, self.psum[bo][:, co:co + 256], sso[:, h:h + 1], [self.t_ps[bo]], t_sso)
            self.ts(sso, sso, 1.0 / 256, EPS, ALU.mult, ALU.add, [t_sso], [t_sso])
            self.act(sso, sso, AF.Ln, [t_sso], [t_sso])
            self.act(sso, sso, AF.Exp, [t_sso], [t_sso], scale=-0.5)
            for hp in range(2):
                self.tt(og[:, hp * 2:hp * 2 + 2, :], self.psum[ob_[hp]][:].rearrange("p (h w) -> p h w", w=256),
                        sso[:, hp * 2:hp * 2 + 2].unsqueeze(2).to_broadcast([128, 2, 256]), ALU.mult,
                        [self.t_ps[ob_[hp]], t_sso], [t_og], join=(hp > 0))
            self.tt(og, og, self.gout_b[:].unsqueeze(1).to_broadcast([128, 4, 256]), ALU.mult, [t_og, self.t_const],
                    [t_og])
            o_b = ogb[tt % 2]
            self.tt(o_b, og.rearrange("p h w -> p (h w)"), szr[:, tt, :], ALU.mult, [t_og, T["szr"][tt]],
                    [t_ogb[tt % 2]])
            def Dtr(tt=tt, o_b=o_b):
                b = 7 if tt % 2 else 0
                pb = self.psbf(b)
                for c in range(8):
                    self.tr(pb[:, c * 128:(c + 1) * 128], o_b[:, c * 128:(c + 1) * 128], self.ident_b[:],
                            [t_ogb[tt % 2]], [self.t_ps[b]], join=(c > 0))
                self.copy("act", C["mixT"][:, 8:16, tt * 128:(tt + 1) * 128],
                          pb[:, :].rearrange("p (k t) -> p k t", t=128), [self.t_ps[b]], [T["mixT"]], join=True)
            dtr_pending.append(Dtr)
        for f_ in dtr_pending:
            f_()
        P.end_stage()

    def wout(self, g, C, T):
        P, din = self.P, self.din
        wo = [self.abf(4 * i, [16, 128]) for i in range(3)]; t_wo = tiles("wo", 3)
        mixT = C["mixT"]
        for co in range(16):
            sl = co % 3
            P.dma("pool", wo[sl], din["w_out"][co], writes=[t_wo[sl]])
            b = co % 2
            for ci in range(16):
                self.mm(self.psum[b][:], wo[sl][:, ci, :], mixT[:, ci, :], ci == 0, ci == 15, [t_wo[sl], T["mixT"]],
                        [self.t_ps[b]], join=(ci > 0))
            self.tt(self.xT[:, co, :], self.psum[b][:], self.xT[:, co, :], ALU.add, [self.t_ps[b], self.t_x[co]],
                    [self.t_x[co]])
        P.end_stage()

    def cross(self, g):
        P, din = self.P, self.din
        hT = self.abf(0, [16, GT]); t_hT = tiles("hTc", 16)
        wq = [self.abf(16 + 4 * i, [16, 128]) for i in range(3)]; t_wq = tiles("mwq", 3)
        qmz = [self.af32(28, [GT]), self.af32(30, [GT])]; t_qmz = tiles("qmz", 2)
        qmT = self.abf(36, [4, GT]); t_qm = tiles("qmT", 4)
        pT = [self.abf(40 + i, [GT]) for i in range(4)]; t_pT = tiles("pTc", 4)
        rec = [self.af32(44, [GT]), self.af32(46, [GT])]; t_rec = tiles("recc", 2)
        omT = self.abf(48, [4, GT]); t_omT = tiles("omT", 4)
        wmo = [self.abf(52 + i, [4, 128]) for i in range(3)]; t_wmo = tiles("wmo", 3)
        sqb = [self.abf(56, [GT]), self.abf(57, [GT])]; t_sq = tiles("sqc", 2)
        rs = [self.af32(58, [GT]), self.af32(60, [GT])]; t_rs = tiles("rsc", 2)
        for h in range(3):
            P.dma("pool", wq[h], din["mwq"][h], writes=[t_wq[h]])
        for co in range(3):
            P.dma("pool", wmo[co], din["mwo"][co], writes=[t_wmo[co]])
        self.norm(self.g_ma, hT, t_hT, 62, 64)

        def proj(h):
            sl, b = h % 3, h % 2
            for c in range(16):
                self.mm(self.psum[b][:], wq[sl][:, c, :], hT[:, c, :], c == 0, c == 15, [t_wq[sl], t_hT[c]],
                        [self.t_ps[b]], join=(c > 0))
            self.copy("act", qmz[b], self.psum[b][:], [self.t_ps[b]], [t_qmz[b]])
            self.act(sqb[b], self.psum[b][:], AF.Square, [self.t_ps[b]], [t_sq[b]])
            if h == 0:
                P.dma("pool", wq[0], din["mwq"][3], writes=[t_wq[0]])

        def qnorm(h):
            b = h % 2
            self.mm(self.psum[2 + b][:], self.ones_b[:], sqb[b], True, True, [t_sq[b]], [self.t_ps[2 + b]])
            self.ts(rs[b], self.psum[2 + b][:], 1.0 / 128, EPS, ALU.mult, ALU.add, [self.t_ps[2 + b]], [t_rs[b]])
            self.act(rs[b], rs[b], AF.Ln, [t_rs[b]], [t_rs[b]])
            self.act(rs[b], rs[b], AF.Exp, [t_rs[b]], [t_rs[b]], scale=-0.5)
            self.stt(qmT[:, h, :], qmz[b], self.g_mq[:, 0:1], rs[b], ALU.mult, ALU.mult,
                     [t_qmz[b], t_rs[b], self.t_const], [t_qm[h]])

        for h in range(5):
            if h < 4:
                proj(h)
            if h >= 1:
                qnorm(h - 1)

        its = [(h, mt) for h in range(4) for mt in range(2)]

        def S(i):
            h, mt = its[i]
            bs, sl = i % 2, i % 4
            self.mm(self.psum[bs][:], self.mKT[:, h, mt * 128:(mt + 1) * 128], qmT[:, h, :], True, True,
                    [self.t_mem, t_qm[h]], [self.t_ps[bs]])
            self.act(pT[sl], self.psum[bs][:], AF.Exp, [self.t_ps[bs]], [t_pT[sl]], scale=128 ** -0.5)

        def V(i):
            h, mt = its[i]
            sl = i % 4
            bo, bsum = 4 + 2 * (h % 2), 5 + 2 * (h % 2)
            self.mm(self.psum[bo][:], self.mV[:, mt, h * VW:h * VW + 128], pT[sl], mt == 0, mt == 1,
                    [t_pT[sl], self.t_mem], [self.t_ps[bo]], join=(mt > 0))
            self.mm(self.psum[bsum][:], self.ones_b[:], pT[sl], mt == 0, mt == 1, [t_pT[sl]], [self.t_ps[bsum]],
                    join=(mt > 0))
            if mt == 1:
                r = rec[h % 2]
                P.op("dve", lambda e, o=r, i_=self.psum[bsum][:]: e.reciprocal(o, i_),
                     reads=[self.t_ps[bsum]], writes=[t_rec[h % 2]])
                self.tt(omT[:, h, :], self.psum[bo][:], r, ALU.mult, [self.t_ps[bo], t_rec[h % 2]], [t_omT[h]])

        for i in range(len(its) + 1):
            if i < len(its):
                S(i)
            if i >= 1:
                V(i - 1)
        for co in range(16):
            sl = co % 3
            b = 2 + co % 2
            for ci in range(4):
                self.mm(self.psum[b][:], wmo[sl][:, ci, :], omT[:, ci, :], ci == 0, ci == 3, [t_wmo[sl], t_omT[ci]],
                        [self.t_ps[b]], join=(ci > 0))
            self.tt(self.xT[:, co, :], self.psum[b][:], self.xT[:, co, :], ALU.add, [self.t_ps[b], self.t_x[co]],
                    [self.t_x[co]])
            if co + 3 < 16:
                P.dma("pool", wmo[sl], din["mwo"][co + 3], writes=[t_wmo[sl]])
        P.end_stage()

    def build(self):
        self.prologue()
        order = ["x", "ffn1", "mix", "cross", "ffn2"]
        lvl = order.index(self.upto) - 1
        for g in range(self.ngroups):
            if g == 0:
                self.load_x(g)
            if lvl >= 0:
                self.ffn(1)
            if lvl >= 1:
                C = self.carried()
                T = {"hT": tiles("hTm", 16), "gq": Tile("gq"), "gq0": Tile("gq0"), "zgT": Tile("zgT"), "gk": tiles("gk", 4),
                     "gv": tiles("gv", 4), "szr": tiles("szr", 4), "zkr": Tile("zkr"), "cT": tiles("cT", 6),
                     "qT": Tile("qT"), "mixT": Tile("mixT")}
                sub = getattr(self, "sub", 99)
                self.mix_p1(g, C, T)
                self.mix_p2(g, C, T)
                self.mix_p3a(g, C, T)
                if sub >= 4:
                    self.mix_p3b(g, C, T)
                if sub >= 5:
                    self.mla(g, C, T)
                if sub >= 6:
                    self.gla(g, C, T)
                if sub >= 7:
                    self.wout(g, C, T)
            if lvl >= 2:
                self.cross(g)
            if lvl >= 3:
                self.ffn(2)
            if g + 1 < self.ngroups:
                self.store_x(g, end=False)
                self.load_x(g + 1, base=16, end=True)
            else:
                self.store_x(g)
        self.P.close()
        return self.nc


def _chunk_rows(w, ncols_chunk):
    Kd, N = w.shape
    return np.ascontiguousarray(w.reshape(Kd // 128, 128, N // ncols_chunk, ncols_chunk).transpose(2, 1, 0, 3))


def _rows(w):
    Kd, N = w.shape
    return np.ascontiguousarray(w.reshape(Kd // 128, 128, N).transpose(1, 0, 2))


def _col(g):
    return np.ascontiguousarray(g.reshape(-1, 128).T)


def prep_shared(inp):
    f32 = np.float32
    sh = {}
    for k in (1, 2):
        wg = _chunk_rows(np.asarray(inp[f"ffn{k}_w_gate"][0], f32), 128)
        wu = _chunk_rows(np.asarray(inp[f"ffn{k}_w_up"][0], f32), 128)
        sh[f"f{k}_gu"] = np.ascontiguousarray(np.stack([wg, wu], axis=2))
        wd = np.asarray(inp[f"ffn{k}_w_down"][0], f32)
        sh[f"f{k}_dn"] = np.ascontiguousarray(wd.reshape(NQ, JQ, 128, 16, 128).transpose(0, 3, 2, 1, 4))
        sh[f"f{k}_g"] = _col(np.asarray(inp[f"ffn{k}_norm"][0], f32))
    win = np.asarray(inp["w_in"][0], f32)
    fm_cols = np.concatenate([win[:, O_ZQ:O_ZQ + 512], win[:, O_ZKV:O_ZKV + 256], win[:, O_GQ:O_GQ + 512]], axis=1)
    sh["win_fm"] = _chunk_rows(fm_cols, 128)
    sh["win_zg"] = _rows(win[:, O_ZG:O_ZG + 16])
    tm_cols = np.concatenate([win[:, O_GK:O_GK + 512], win[:, O_GV:O_GV + 1024], win[:, O_ZR:O_ZR + 1024]], axis=1)
    sh["win_tm"] = _chunk_rows(tm_cols, 256)
    sh["win_kr"] = _rows(win[:, O_ZKR:O_ZKR + 64])
    sh["mix_g"] = _col(np.asarray(inp["mix_norm"][0], f32))
    sh["qa_g"] = _col(np.asarray(inp["q_a_norm"][0], f32))
    sh["kva_g"] = _col(np.asarray(inp["kv_a_norm"][0], f32))
    sh["wq_up"] = _rows(np.asarray(inp["w_q_up"][0], f32))
    sh["wkv_up"] = _rows(np.asarray(inp["w_kv_up"][0], f32))
    sh["gq_row"] = np.asarray(inp["mla_q_norm"][0], f32).reshape(1, 192).copy()
    sh["gk_row"] = np.asarray(inp["mla_k_norm"][0], f32).reshape(1, 192).copy()
    sh["wg2"] = np.ascontiguousarray(np.concatenate([np.asarray(inp["gla_w_gate2"][0], f32),
                                                     np.asarray(inp["gla_b_gate"][0], f32).reshape(1, 512)], axis=0))
    sh["gout_row"] = np.asarray(inp["gla_out_norm"][0], f32).reshape(1, 256).copy()
    sh["w_out"] = _chunk_rows(np.asarray(inp["w_out"][0], f32), 128)
    sh["ma_g"] = _col(np.asarray(inp["mem_attn_norm"][0], f32))
    sh["mn_g"] = _col(np.asarray(inp["mem_norm"][0], f32))
    sh["mwq"] = _chunk_rows(np.asarray(inp["mem_w_q"][0], f32), 128)
    sh["mwk"] = _chunk_rows(np.asarray(inp["mem_w_k"][0], f32), 128)
    sh["mwv"] = _rows(np.asarray(inp["mem_w_v"][0], f32))
    sh["mwo"] = _chunk_rows(np.asarray(inp["mem_w_o"][0], f32), 128)
    sh["mq_g"] = np.asarray(inp["mem_q_norm"][0], f32).reshape(128, 1).copy()
    sh["mk_g"] = np.asarray(inp["mem_k_norm"][0], f32).reshape(128, 1).copy()
    half = 32
    sh["invf"] = (np.float32(10000.0) ** (-np.arange(half, dtype=np.float32) / np.float32(half))).astype(f32).reshape(1, 32)
    return sh


def core_inputs(inp, b, sh):
    m = dict(sh)
    m["x"] = np.ascontiguousarray(np.asarray(inp["x"][b], np.float32))
    m["mem"] = np.ascontiguousarray(np.asarray(inp["mem"][b], np.float32))
    m["pos"] = np.ascontiguousarray(np.asarray(inp["positions"][b], np.int32).reshape(16, 128).T)
    return m


def kernel(**inputs):
    B = inputs["x"].shape[0]
    sh = prep_shared(inputs)
    nc = bass.Bass("TRN2", target_bir_lowering=False)
    K(nc).build()
    in_maps = [core_inputs(inputs, b, sh) for b in range(B)]
    res = run_bass_kernel_spmd(nc, in_maps, core_ids=list(range(B)))
    return np.stack([r["y"] for r in res.results], axis=0).astype(np.float32)
```

```python
import numpy as np
import concourse.bass as bass
import concourse.mybir as mybir
from concourse.bass_utils import run_bass_kernel_spmd
from contextlib import ExitStack

F32 = mybir.dt.float32
BF16 = mybir.dt.bfloat16
I32 = mybir.dt.int32
AF = mybir.ActivationFunctionType
ALU = mybir.AluOpType

ENGINES = ("pe", "act", "dve", "pool", "sp")

D = 2048
S = 2048
DFF = 5632
NJ = DFF // 128
NQ = 4
JQ = NJ // NQ
GT = 512
NGRP = S // GT
EPS = 1e-6
O_ZQ, O_ZKV, O_ZKR, O_GQ, O_GK, O_GV, O_ZG, O_ZR = 0, 512, 768, 832, 1344, 1856, 2880, 2896
VW = 130
C1_2PI = 6.28125
C2_2PI = 2.0 * np.pi - 6.28125
PI = float(np.pi)


class Tile:
    __slots__ = ("name", "writers", "readers", "sem", "sem_count", "gen_deps", "psum")

    def __init__(self, name):
        self.name = name
        self.psum = False
        self.writers = []
        self.readers = []
        self.gen_deps = []
        self.sem = None
        self.sem_count = 0


def tiles(prefix, n):
    return [Tile(f"{prefix}{i}") for i in range(n)]


class Op:
    __slots__ = ("eng", "kind", "fn", "deps", "signals", "seq", "stage", "dma_sem", "dma_val")

    def __init__(self, eng, kind, fn, stage):
        self.eng = eng
        self.kind = kind
        self.fn = fn
        self.deps = []
        self.signals = False
        self.seq = None
        self.stage = stage
        self.dma_sem = None
        self.dma_val = None


class Prog:
    def __init__(self, nc):
        self.nc = nc
        self.es = ExitStack()
        self.stage = 0
        self.ops = {e: [] for e in ENGINES}
        self.esem = {}
        self.ecount = {e: 0 for e in ENGINES}
        for e in ("pe", "act", "dve", "pool"):
            self.esem[e] = self.es.enter_context(nc.semaphore("sem_" + e))
        self.seen = {e: {} for e in ENGINES}
        self.stage_dmas = {e: [] for e in ENGINES}
        self.n_ops = 0
        self.sem_pool = {"sw": [], "hw": []}
        self.stage_sem_tiles = []
        self.n_sem = 0

    def sbuf(self, name, shape, dt):
        return self.es.enter_context(self.nc.sbuf_tensor("sb_" + name, list(shape), dt))

    def psum(self, name, shape, dt):
        return self.es.enter_context(self.nc.psum_tensor(name, list(shape), dt))

    def new_sem(self, name):
        return self.es.enter_context(self.nc.semaphore(name))

    def _add_deps(self, op, reads, writes, join):
        cur = self.stage
        for t in reads:
            for w in t.writers:
                if w.stage == cur:
                    op.deps.append((w, True))
            if t.psum:
                for r in t.readers:
                    if r.stage == cur and r.eng != op.eng:
                        op.deps.append((r, False))
        for t in writes:
            if not join:
                gd = [w for w in t.writers if w.stage == cur] + [r for r in t.readers if r.stage == cur]
                t.gen_deps = gd
            else:
                gd = [d for d in t.gen_deps if d.stage == cur] + [r for r in t.readers if r.stage == cur]
            for d in gd:
                if d is not op:
                    op.deps.append((d, False))
        for t in reads:
            t.readers.append(op)
        for t in writes:
            if join:
                t.writers.append(op)
            else:
                t.writers = [op]
                t.readers = []

    def op(self, eng, fn, reads=(), writes=(), join=False):
        o = Op(eng, "compute", fn, self.stage)
        self._add_deps(o, reads, writes, join)
        self.ops[eng].append(o)
        self.n_ops += 1
        return o

    def dma(self, eng, out, in_, reads=(), writes=(), join=False, sem_tile=None):
        def fn(e, out=out, in_=in_):
            return e.dma_start(out=out, in_=in_)
        o = Op(eng, "dma", fn, self.stage)
        self._add_deps(o, reads, writes, join)
        st = sem_tile if sem_tile is not None else writes[0]
        if st.sem is None:
            pool = self.sem_pool["sw" if eng == "pool" else "hw"]
            if pool:
                st.sem, st.sem_count = pool.pop()
            else:
                self.n_sem += 1
                st.sem = self.new_sem(f"dsem{self.n_sem}")
                st.sem_count = 0
            self.stage_sem_tiles.append((st, "sw" if eng == "pool" else "hw"))
        st.sem_count += 16
        o.dma_sem = st.sem
        o.dma_val = st.sem_count
        self.ops[eng].append(o)
        self.stage_dmas[eng].append(o)
        self.n_ops += 1
        return o

    def end_stage(self):
        nc = self.nc
        for e in ENGINES:
            for o in self.ops[e]:
                for (p, raw) in o.deps:
                    if p.kind == "compute":
                        if p.eng != o.eng or p.eng != "pe" or o.kind == "dma":
                            p.signals = True
        for e in ("pe", "act", "dve", "pool"):
            c = self.ecount[e]
            for o in self.ops[e]:
                if o.kind == "compute" and o.signals:
                    c += 1
                    o.seq = c
            self.ecount[e] = c

        def emit_engine(ename, eng):
            seen = self.seen[ename]

            def wait(sem, val):
                key = id(sem)
                if seen.get(key, 0) >= val:
                    return
                seen[key] = val
                eng.wait_ge(sem, val)

            for o in self.ops[ename]:
                need = {}
                for (p, raw) in o.deps:
                    if p.kind == "dma":
                        sm, v = p.dma_sem, p.dma_val
                    elif p.eng != ename or p.eng != "pe" or o.kind == "dma":
                        sm, v = self.esem[p.eng], p.seq
                    else:
                        continue
                    k = id(sm)
                    if k not in need or need[k][1] < v:
                        need[k] = (sm, v)
                for (sm, v) in need.values():
                    wait(sm, v)
                ins = o.fn(eng)
                if o.kind == "dma":
                    ins.then_inc(o.dma_sem, 16)
                elif o.signals:
                    ins.then_inc(self.esem[ename], 1)
            need = {}
            for o in self.stage_dmas[ename]:
                k = id(o.dma_sem)
                if k not in need or need[k][1] < o.dma_val:
                    need[k] = (o.dma_sem, o.dma_val)
            for (sm, v) in need.values():
                wait(sm, v)

        with nc.Block(no_gpsimd_drain=True) as blk:
            if self.ops["sp"]:
                @blk.sync
                def _(e):
                    emit_engine("sp", e)
            if self.ops["pe"]:
                @blk.tensor
                def _(e):
                    emit_engine("pe", e)
            if self.ops["act"]:
                @blk.scalar
                def _(e):
                    emit_engine("act", e)
            if self.ops["dve"]:
                @blk.vector
                def _(e):
                    emit_engine("dve", e)
            if self.ops["pool"]:
                @blk.gpsimd
                def _(e):
                    emit_engine("pool", e)
        self.ops = {e: [] for e in ENGINES}
        self.stage_dmas = {e: [] for e in ENGINES}
        for st, kind in self.stage_sem_tiles:
            self.sem_pool[kind].append((st.sem, st.sem_count))
            st.sem = None
        self.stage_sem_tiles = []
        self.stage += 1

    def close(self):
        self.es.close()


class K:
    def __init__(self, nc, ngroups=NGRP, upto="ffn2"):
        self.nc = nc
        self.ngroups = ngroups
        self.upto = upto
        P = self.P = Prog(nc)
        self.din = {}
        self._decl_inputs()
        self.y = nc.dram_tensor("y", [S, D], F32, kind="ExternalOutput").ap()
        sb = P.sbuf
        self.xT = sb("xT", [128, 16, GT], F32)
        self.KTn = sb("KTn", [128, 8, S], BF16)
        self.KTr = sb("KTr", [128, S], BF16)
        self.Vaug = sb("Vaug", [128, 16, 8 * VW], BF16)
        self.rk = sb("rk", [128, 16, 8], F32)
        self.Sst = sb("Sst", [128, 4, 256], F32)
        self.mKT = sb("mKT", [128, 4, 256], BF16)
        self.mV = sb("mV", [128, 2, 4 * VW], BF16)
        self.cosT = sb("cosT", [128, 16, 32], F32)
        self.sinT = sb("sinT", [128, 16, 32], F32)
        self.ident_b = sb("ident_b", [128, 128], BF16)
        self.ident_f = sb("ident_f", [128, 128], F32)
        self.ones_b = sb("ones_b", [128, 128], BF16)
        self.tri_f = sb("tri_f", [128, 128], F32)
        self.ind_f = sb("ind_f", [128, 2], F32)
        self.tri_b = sb("tri_b", [128, 128], BF16)
        self.ind_b = sb("ind_b", [128, 2], BF16)
        self.g_f1 = sb("g_f1", [128, 16], F32)
        self.g_f2 = sb("g_f2", [128, 16], F32)
        self.g_mix = sb("g_mix", [128, 16], F32)
        self.g_ma = sb("g_ma", [128, 16], F32)
        self.g_mn = sb("g_mn", [128, 16], F32)
        self.g_qa = sb("g_qa", [128, 4], F32)
        self.g_kva = sb("g_kva", [128, 2], F32)
        self.gqk_b = sb("gqk_b", [128, 192], F32)
        self.gk_b = sb("gk_b", [128, 192], F32)
        self.gout_b = sb("gout_b", [128, 256], F32)
        self.g_mq = sb("g_mq", [128, 1], F32)
        self.g_mk = sb("g_mk", [128, 1], F32)
        self.wg2 = sb("wg2", [32, 512], BF16)
        self.AW = 86 * 256
        self.arena = sb("arena", [128, self.AW], F32)
        self.psum = [P.psum(f"psb{i}", [128, 512], F32) for i in range(8)]
        self.t_ps = tiles("ps", 8)
        for t in self.t_ps:
            t.psum = True
        self.t_x = tiles("x", 16)
        self.t_const = Tile("const")
        self.t_K = Tile("K")
        self.t_S = Tile("S")
        self.t_Sh = tiles("Sh", 4)
        self.t_mem = Tile("mem")
        self.t_y = tiles("yout", 2)

    def _decl(self, name, shape, dt=F32):
        self.din[name] = self.nc.dram_tensor(name, list(shape), dt, kind="ExternalInput").ap()

    def _decl_inputs(self):
        d = self._decl
        d("x", [S, D]); d("mem", [256, D]); d("pos", [128, 16], I32)
        for k in (1, 2):
            d(f"f{k}_gu", [NJ, 128, 2, 16, 128]); d(f"f{k}_dn", [NQ, 16, 128, JQ, 128]); d(f"f{k}_g", [128, 16])
        d("win_fm", [10, 128, 16, 128]); d("win_zg", [128, 16, 16]); d("win_tm", [10, 128, 16, 256])
        d("win_kr", [128, 16, 64]); d("mix_g", [128, 16]); d("qa_g", [128, 4]); d("kva_g", [128, 2])
        d("wq_up", [128, 4, 1536]); d("wkv_up", [128, 2, 2048]); d("gq_row", [1, 192]); d("gk_row", [1, 192])
        d("wg2", [17, 512]); d("gout_row", [1, 256]); d("w_out", [16, 128, 16, 128])
        d("ma_g", [128, 16]); d("mn_g", [128, 16]); d("mwq", [4, 128, 16, 128]); d("mwk", [4, 128, 16, 128])
        d("mwv", [128, 16, 512]); d("mwo", [16, 128, 4, 128]); d("mq_g", [128, 1]); d("mk_g", [128, 1])
        d("invf", [1, 32])

    def af32(self, off_kib, shape):
        n = int(np.prod(shape))
        o = int(off_kib * 256)
        assert o + n <= self.AW, (off_kib, shape)
        ap = self.arena[:, o:o + n]
        return self._shape(ap, shape)

    def abf(self, off_kib, shape):
        n = int(np.prod(shape))
        w = (n + 1) // 2
        o = int(off_kib * 256)
        assert o + w <= self.AW, (off_kib, shape)
        ap = self.arena[:, o:o + w].bitcast(BF16)[:, 0:n]
        return self._shape(ap, shape)

    @staticmethod
    def _shape(ap, shape):
        if len(shape) == 1:
            return ap
        if len(shape) == 2:
            return ap.rearrange("p (a b) -> p a b", b=shape[1])
        if len(shape) == 3:
            return ap.rearrange("p (a b c) -> p a b c", b=shape[1], c=shape[2])
        raise ValueError

    def psbf(self, b):
        return self.psum[b].bitcast(BF16)

    def mm(self, out, lhsT, rhs, start, stop, reads, writes, join=False):
        self.P.op("pe", lambda e: e.matmul(out, lhsT=lhsT, rhs=rhs, start=start, stop=stop),
                  reads=reads, writes=writes, join=join)

    def tr(self, out, in_, ident, reads, writes, join=False):
        self.P.op("pe", lambda e: e.transpose(out, in_, ident), reads=reads, writes=writes, join=join)

    def act(self, out, in_, func, reads, writes, join=False, scale=1.0, bias=0.0, accum_out=None):
        if accum_out is None:
            self.P.op("act", lambda e: e.activation(out=out, in_=in_, func=func, scale=scale, bias=bias),
                      reads=reads, writes=writes, join=join)
        else:
            self.P.op("act", lambda e: e.activation(out=out, in_=in_, func=func, scale=scale, bias=bias,
                                                   accum_out=accum_out),
                      reads=reads, writes=writes, join=join)

    def copy(self, eng, out, in_, reads, writes, join=False):
        if eng == "act":
            self.act(out, in_, AF.Copy, reads, writes, join)
        else:
            self.P.op(eng, lambda e: e.tensor_copy(out, in_), reads=reads, writes=writes, join=join)

    def tt(self, out, in0, in1, op, reads, writes, join=False, eng="dve"):
        self.P.op(eng, lambda e: e.tensor_tensor(out, in0, in1, op), reads=reads, writes=writes, join=join)

    def ts(self, out, in0, s1, s2, op0, op1, reads, writes, join=False, eng="dve"):
        if op1 is None:
            self.P.op(eng, lambda e: e.tensor_scalar(out, in0, s1, None, op0), reads=reads, writes=writes, join=join)
        else:
            self.P.op(eng, lambda e: e.tensor_scalar(out, in0, s1, s2, op0, op1), reads=reads, writes=writes,
                      join=join)

    def stt(self, out, in0, scalar, in1, op0, op1, reads, writes, join=False):
        self.P.op("dve", lambda e: e.scalar_tensor_tensor(out, in0, scalar, in1, op0, op1),
                  reads=reads, writes=writes, join=join)

    def memset(self, eng, ap, val, reads, writes, join=False):
        self.P.op(eng, lambda e: e.memset(ap, val), reads=reads, writes=writes, join=join)

    def sq_acc(self, junk, src, acc, reads, t_acc):
        self.act(junk, src, AF.Square, list(reads) + [t_acc], [t_acc], join=True, accum_out=acc)

    def rsqrt_inplace(self, ap, t, mult, add):
        self.ts(ap, ap, mult, add, ALU.mult, ALU.add, [t], [t])
        self.act(ap, ap, AF.Ln, [t], [t])
        self.act(ap, ap, AF.Exp, [t], [t], scale=-0.5)

    def prologue(self):
        P, din = self.P, self.din
        tc = self.t_const
        ld = lambda dst, src: P.dma("sp", dst, src, writes=[tc], join=True)
        ld(self.g_f1[:], din["f1_g"]); ld(self.g_f2[:], din["f2_g"]); ld(self.g_mix[:], din["mix_g"])
        ld(self.g_ma[:], din["ma_g"]); ld(self.g_mn[:], din["mn_g"]); ld(self.g_qa[:], din["qa_g"])
        ld(self.g_kva[:], din["kva_g"]); ld(self.g_mq[:], din["mq_g"]); ld(self.g_mk[:], din["mk_g"])
        ld(self.gqk_b[:], din["gq_row"].partition_broadcast(128))
        ld(self.gk_b[:], din["gk_row"].partition_broadcast(128))
        ld(self.gout_b[:], din["gout_row"].partition_broadcast(128))
        P.dma("pool", self.wg2[0:17, :], din["wg2"], writes=[tc], join=True, sem_tile=Tile("const_sw"))
        invf = self.af32(0, [32])
        posi = self.arena[:, 64:80].bitcast(I32)
        posf = self.af32(0.5, [16])
        ld(invf, din["invf"].partition_broadcast(128))
        ld(posi, din["pos"])
        t_c2 = Tile("c2")
        self.memset("pool", self.ident_f[:], 0.0, [], [t_c2])
        P.op("pool", lambda e: e.affine_select(out=self.ident_f[:], in_=self.ident_f[:], pattern=[[-1, 128]],
                                               compare_op=ALU.not_equal, fill=1.0, base=0, channel_multiplier=1),
             reads=[t_c2], writes=[t_c2])
        self.copy("pool", self.ident_b[:], self.ident_f[:], [t_c2], [t_c2], join=True)
        self.memset("pool", self.ones_b[:], 1.0, [], [t_c2], join=True)
        t_tri = Tile("tri")
        self.memset("pool", self.tri_f[:], 1.0, [], [t_tri])
        P.op("pool", lambda e: e.affine_select(out=self.tri_f[:], in_=self.tri_f[:], pattern=[[-1, 128]],
                                               compare_op=ALU.is_gt, fill=0.0, base=0, channel_multiplier=1),
             reads=[t_tri], writes=[t_tri])
        self.memset("pool", self.tri_f[64:128, 0:64], 0.0, [t_tri], [t_tri])
        t_ind = Tile("ind")
        self.memset("pool", self.ind_f[:], 0.0, [], [t_ind])
        self.memset("pool", self.ind_f[0:64, 0:1], 1.0, [t_ind], [t_ind])
        self.memset("pool", self.ind_f[64:128, 1:2], 1.0, [t_ind], [t_ind])
        self.copy("pool", self.ind_b[:], self.ind_f[:], [t_ind], [Tile("indb")])
        self.copy("pool", self.tri_b[:], self.tri_f[:], [t_tri], [Tile("trib")])
        t_v = Tile("vones")
        va = self.Vaug[:].rearrange("p t (h w) -> p t h w", w=VW)
        self.memset("pool", va[:, :, :, 128:129], 1.0, [], [t_v])
        mv = self.mV[:].rearrange("p t (h w) -> p t h w", w=VW)
        self.memset("pool", mv[:, :, :, 128:129], 1.0, [], [t_v], join=True)
        self.memset("pool", self.Sst[:], 0.0, [], [self.t_S])
        self.tt(self.gqk_b[:, 0:128], self.gqk_b[:, 0:128], self.gk_b[:, 0:128], ALU.mult, [tc], [tc])
        t_r = Tile("rope")
        self.copy("dve", posf, posi, [tc], [t_r])
        ang = self.af32(1, [16, 32])
        kf = self.af32(3, [16, 32])
        ki = self.arena[:, 5 * 256:5 * 256 + 512].bitcast(I32).rearrange("p (a b) -> p a b", b=32)
        r = self.af32(7, [16, 32])
        rc = self.af32(9, [16, 32])
        msk = self.af32(11, [16, 32])
        t_a, t_k, t_rr, t_rc, t_m = Tile("ang"), Tile("kf"), Tile("r"), Tile("rc"), Tile("msk")
        self.tt(ang, posf.unsqueeze(2).to_broadcast([128, 16, 32]), invf.unsqueeze(1).to_broadcast([128, 16, 32]),
                ALU.mult, [t_r, tc], [t_a])
        self.ts(kf, ang, 1.0 / (2 * PI), None, ALU.mult, None, [t_a], [t_k])
        self.copy("dve", ki, kf, [t_k], [t_k])
        self.copy("dve", kf, ki, [t_k], [t_k])
        self.stt(r, kf, -C1_2PI, ang, ALU.mult, ALU.add, [t_k, t_a], [t_rr])
        self.stt(r, kf, -C2_2PI, r, ALU.mult, ALU.add, [t_k, t_rr], [t_rr])
        self.ts(r, r, -PI, PI, ALU.max, ALU.min, [t_rr], [t_rr])
        self.act(self.sinT[:], r, AF.Sin, [t_rr], [tc], join=True)
        self.ts(rc, r, PI / 2, None, ALU.add, None, [t_rr], [t_rc])
        self.ts(msk, rc, PI, -2 * PI, ALU.is_gt, ALU.mult, [t_rc], [t_m])
        self.tt(rc, rc, msk, ALU.add, [t_rc, t_m], [t_rc])
        self.ts(rc, rc, -PI, PI, ALU.max, ALU.min, [t_rc], [t_rc])
        self.act(self.cosT[:], rc, AF.Sin, [t_rc], [tc], join=True)
        P.end_stage()
        if self.upto in ("cross", "ffn2"):
            self.mem_kv()

    def mem_kv(self):
        P, din = self.P, self.din
        mem_f = [self.af32(0, [2048]), self.af32(8, [2048])]
        mem_b = [self.abf(16, [2048]), self.abf(20, [2048])]
        junk = self.af32(24, [2048])
        ssm = self.af32(32, [2])
        mnT = self.abf(33, [16, 256])
        wk = [self.abf(41 + 4 * i, [16, 128]) for i in range(3)]
        wv = self.abf(53, [16, 512])
        kmz = self.af32(69, [256])
        sqb = self.abf(70, [256])
        rs = self.af32(71, [256])
        t_mf, t_mb = tiles("mf", 2), tiles("mb", 2)
        t_ss, t_mnT, t_wk, t_wv, t_kmz, t_sqb, t_rs = Tile("ssm"), tiles("mnT", 16), tiles("wk", 3), Tile("wv"), \
            Tile("kmz"), Tile("sqbm"), Tile("rsm")
        P.dma("pool", wv, din["mwv"], writes=[t_wv])
        self.memset("dve", ssm, 0.0, [], [t_ss])
        for mt in range(2):
            P.dma("sp", mem_f[mt], din["mem"][mt * 128:(mt + 1) * 128, :], writes=[t_mf[mt]])
            self.sq_acc(junk, mem_f[mt], ssm[:, mt:mt + 1], [t_mf[mt]], t_ss)
        self.ts(ssm, ssm, 1.0 / D, EPS, ALU.mult, ALU.add, [t_ss], [t_ss])
        self.act(ssm, ssm, AF.Ln, [t_ss], [t_ss])
        self.act(ssm, ssm, AF.Exp, [t_ss], [t_ss], scale=-0.5)
        for mt in range(2):
            self.ts(mem_b[mt], mem_f[mt], ssm[:, mt:mt + 1], None, ALU.mult, None, [t_mf[mt], t_ss], [t_mb[mt]])
            for q in range(4):
                b = (mt * 4 + q) % 4
                pb = self.psbf(b)
                for k in range(4):
                    c = q * 4 + k
                    self.tr(pb[:, k * 128:(k + 1) * 128], mem_b[mt][:, c * 128:(c + 1) * 128], self.ident_b[:],
                            [t_mb[mt]], [self.t_ps[b]], join=(k > 0))
                for k in range(4):
                    c = q * 4 + k
                    self.act(mnT[:, c, mt * 128:(mt + 1) * 128], pb[:, k * 128:(k + 1) * 128], AF.Copy,
                             [self.t_ps[b], self.t_const], [t_mnT[c]], join=True, scale=self.g_mn[:, c:c + 1])
        for h in range(4):
            sl = h % 3
            P.dma("pool", wk[sl], din["mwk"][h], writes=[t_wk[sl]])
            b = 4 + h % 2
            for c in range(16):
                self.mm(self.psum[b][:, 0:256], wk[sl][:, c, :], mnT[:, c, :], c == 0, c == 15,
                        [t_wk[sl], t_mnT[c]], [self.t_ps[b]], join=(c > 0))
            self.copy("act", kmz, self.psum[b][:, 0:256], [self.t_ps[b]], [t_kmz])
            self.act(sqb, self.psum[b][:, 0:256], AF.Square, [self.t_ps[b]], [t_sqb])
            self.mm(self.psum[6][:, 0:256], self.ones_b[:], sqb, True, True, [t_sqb], [self.t_ps[6]])
            self.ts(rs, self.psum[6][:, 0:256], 1.0 / 128, EPS, ALU.mult, ALU.add, [self.t_ps[6]], [t_rs])
            self.act(rs, rs, AF.Ln, [t_rs], [t_rs])
            self.act(rs, rs, AF.Exp, [t_rs], [t_rs], scale=-0.5)
            self.stt(self.mKT[:, h, :], kmz, self.g_mk[:, 0:1], rs, ALU.mult, ALU.mult, [t_kmz, t_rs], [self.t_mem],
                     join=True)
        mv = self.mV[:].rearrange("p t (h w) -> p t h w", w=VW)
        for mt in range(2):
            b = 2 + mt
            for c in range(16):
                self.mm(self.psum[b][:], mnT[:, c, mt * 128:(mt + 1) * 128], wv[:, c, :], c == 0, c == 15,
                        [t_wv, t_mnT[c]], [self.t_ps[b]], join=(c > 0))
            self.copy("dve", mv[:, mt, :, 0:128], self.psum[b][:].rearrange("p (h w) -> p h w", w=128),
                      [self.t_ps[b]], [self.t_mem], join=True)
        P.end_stage()

    def load_x(self, g, base=0, end=True):
        P = self.P
        xin = [self.af32(base, [2048]), self.af32(base + 8, [2048])]
        t_xin = tiles("xin", 2)
        n = 0
        for tt in range(4):
            sl = tt % 2
            r0 = g * GT + tt * 128
            P.dma("sp", xin[sl], self.din["x"][r0:r0 + 128, :], writes=[t_xin[sl]])
            for q in range(4):
                b = n % 8
                n += 1
                for k in range(4):
                    c = q * 4 + k
                    self.tr(self.psum[b][:, k * 128:(k + 1) * 128], xin[sl][:, c * 128:(c + 1) * 128], self.ident_f[:],
                            [t_xin[sl]], [self.t_ps[b]], join=(k > 0))
                self.copy("act" if q % 2 else "dve", self.xT[:, q * 4:(q + 1) * 4, tt * 128:(tt + 1) * 128],
                          self.psum[b][:].rearrange("p (k t) -> p k t", t=128),
                          [self.t_ps[b]], [self.t_x[c] for c in range(q * 4, q * 4 + 4)], join=True)
        if end:
            P.end_stage()

    def store_x(self, g, end=True):
        P = self.P
        yo = [self.af32(0, [2048]), self.af32(8, [2048])]
        t_yo = tiles("yo", 2)
        n = 0
        for tt in range(4):
            sl = tt % 2
            for q in range(4):
                b = n % 8
                n += 1
                for k in range(4):
                    c = q * 4 + k
                    self.tr(self.psum[b][:, k * 128:(k + 1) * 128], self.xT[:, c, tt * 128:(tt + 1) * 128],
                            self.ident_f[:], [self.t_x[c]], [self.t_ps[b]], join=(k > 0))
                self.copy("act" if q % 2 else "dve", yo[sl][:, q * 512:(q + 1) * 512], self.psum[b][:],
                          [self.t_ps[b]], [t_yo[sl]], join=(q > 0))
            r0 = g * GT + tt * 128
            P.dma("sp", self.y[r0:r0 + 128, :], yo[sl], reads=[t_yo[sl]], writes=[self.t_y[sl]])
        if end:
            P.end_stage()

    def norm(self, gcol, hT, t_hT, sq_off, rs_off):
        sqb = [self.abf(sq_off, [GT]), self.abf(sq_off + 1, [GT])]
        rs = self.af32(rs_off, [GT])
        t_sq, t_rs = tiles("nsq", 2), Tile("nrs")
        for c in range(16):
            if c % 2 == 0:
                self.act(sqb[0], self.xT[:, c, :], AF.Square, [self.t_x[c]], [t_sq[0]])
            else:
                self.tt(sqb[1], self.xT[:, c, :], self.xT[:, c, :], ALU.mult, [self.t_x[c]], [t_sq[1]])
            self.mm(self.psum[7][:], self.ones_b[:], sqb[c % 2], c == 0, c == 15, [t_sq[c % 2]], [self.t_ps[7]],
                    join=(c > 0))
        self.ts(rs, self.psum[7][:], 1.0 / D, EPS, ALU.mult, ALU.add, [self.t_ps[7]], [t_rs])
        self.act(rs, rs, AF.Ln, [t_rs], [t_rs])
        self.act(rs, rs, AF.Exp, [t_rs], [t_rs], scale=-0.5)
        for c in range(16):
            self.stt(hT[:, c, :], self.xT[:, c, :], gcol[:, c:c + 1], rs, ALU.mult, ALU.mult,
                     [self.t_x[c], t_rs, self.t_const], [t_hT[c]])

    def ffn(self, k):
        P, din = self.P, self.din
        gu_d, dn_d = din[f"f{k}_gu"], din[f"f{k}_dn"]
        gcol = self.g_f1 if k == 1 else self.g_f2
        hT = self.abf(0, [16, GT]); t_hT = tiles("hT", 16)
        aT = [self.abf(16, [JQ, GT]), self.abf(27, [JQ, GT])]
        t_aT = [tiles("aTa", JQ), tiles("aTb", JQ)]
        wgu = [self.abf(38 + 8 * i, [2 * 16, 128]) for i in range(3)]; t_wgu = tiles("wgu", 3)
        wdn = [self.abf(62 + 2.75 * i, [JQ, 128]) for i in range(4)]; t_wdn = tiles("wdn", 4)
        sg = [self.af32(73, [GT]), self.af32(75, [GT])]; t_sg = tiles("sg", 2)
        self.norm(gcol, hT, t_hT, 77, 79)
        def dma_gu(n):
            P.dma("pool", wgu[n % 3], gu_d[n].rearrange("p a c f -> p (a c) f"), writes=[t_wgu[n % 3]])

        def dma_dn(m):
            P.dma("pool", wdn[m % 4], dn_d[m // 16, m % 16], writes=[t_wdn[m % 4]])

        for n in range(3):
            dma_gu(n)
        for s in range(NQ):
            a = aT[s % 2]
            ta = t_aT[s % 2]
            for m in range(4):
                dma_dn(s * 16 + m)
            for jj in range(JQ):
                n_gu = s * JQ + jj
                sl = n_gu % 3
                bg, bu = n_gu % 2, 2 + n_gu % 2
                for half, b in ((0, bg), (1, bu)):
                    for c in range(16):
                        self.mm(self.psum[b][:], wgu[sl][:, half * 16 + c, :], hT[:, c, :], c == 0, c == 15,
                                [t_wgu[sl], t_hT[c]], [self.t_ps[b]], join=(c > 0))
                self.act(sg[n_gu % 2], self.psum[bg][:], AF.Silu, [self.t_ps[bg]], [t_sg[n_gu % 2]])
                self.tt(a[:, jj, :], sg[n_gu % 2], self.psum[bu][:], ALU.mult, [t_sg[n_gu % 2], self.t_ps[bu]],
                        [ta[jj]])
                if n_gu + 3 < NJ:
                    dma_gu(n_gu + 3)
            for c in range(16):
                n_dn = s * 16 + c
                sl = n_dn % 4
                b = 4 + n_dn % 2
                for jj in range(JQ):
                    self.mm(self.psum[b][:], wdn[sl][:, jj, :], a[:, jj, :], jj == 0, jj == JQ - 1,
                            [t_wdn[sl], ta[jj]], [self.t_ps[b]], join=(jj > 0))
                self.stt(self.xT[:, c, :], self.psum[b][:], 0.5, self.xT[:, c, :], ALU.mult, ALU.add,
                         [self.t_ps[b], self.t_x[c]], [self.t_x[c]])
                if c + 4 < 16:
                    dma_dn(n_dn + 4)
        P.end_stage()

    def carried(self):
        c = {}
        c["gqlo"] = self.abf(0, [4, GT]); c["gqhi"] = self.abf(4, [4, GT])
        c["zgT"] = self.abf(8, [GT])
        c["gk"] = self.af32(9, [4, 512]); c["gv"] = self.abf(17, [4, 1024]); c["szr"] = self.abf(25, [4, 1024])
        c["zkr"] = self.af32(33, [4, 64])
        c["cT"] = self.abf(34, [6, GT])
        c["qTn"] = self.abf(40, [8, GT]); c["qTr"] = self.abf(48, [8, GT])
        c["mixT"] = self.abf(56, [16, GT])
        return c

    def mix_p1(self, g, C, T):
        P, din = self.P, self.din
        hT = self.abf(40, [16, GT]); t_hT = T["hT"]
        wfm = [self.abf(56 + 4 * i, [16, 128]) for i in range(3)]; t_w = tiles("wfm", 3)
        zT = self.af32(68, [6, GT]); t_z = tiles("zT", 6)
        self.norm(self.g_mix, hT, t_hT, 80, 82)
        sqb = [self.abf(80, [GT]), self.abf(81, [GT])]; t_sq = tiles("p1sq", 2)
        rsq, rskv = self.af32(82, [GT]), self.af32(84, [GT]); t_rq, t_rkv = Tile("rsq"), Tile("rskv")
        self.memset("pool", C["gqlo"], 0.0, [], [T["gq0"]])
        self.memset("pool", C["gqhi"], 0.0, [], [T["gq0"]], join=True)
        self.memset("pool", C["zgT"][0:32, :], 1.0, [], [T["zgT"]])
        n = 0
        pending = None
        for ck in range(10):
            sl = n % 3
            P.dma("pool", wfm[sl], din["win_fm"][ck], writes=[t_w[sl]])
            b = n % 2
            n += 1
            for c in range(16):
                self.mm(self.psum[b][:], wfm[sl][:, c, :], hT[:, c, :], c == 0, c == 15, [t_w[sl], t_hT[c]],
                        [self.t_ps[b]], join=(c > 0))
            if pending is not None:
                pending()
                pending = None
            if ck < 6:
                self.copy("act", zT[:, ck, :], self.psum[b][:], [self.t_ps[b]], [t_z[ck]])
                self.act(sqb[ck % 2], self.psum[b][:], AF.Square, [self.t_ps[b]], [t_sq[ck % 2]])
                def _post(ck=ck):
                    if ck < 4:
                        self.mm(self.psum[2][:], self.ones_b[:], sqb[ck % 2], ck == 0, ck == 3, [t_sq[ck % 2]],
                                [self.t_ps[2]], join=(ck > 0))
                    else:
                        self.mm(self.psum[3][:], self.ones_b[:], sqb[ck % 2], ck == 4, ck == 5, [t_sq[ck % 2]],
                                [self.t_ps[3]], join=(ck > 4))
                    if ck == 3:
                        self.ts(rsq, self.psum[2][:], 1.0 / 512, EPS, ALU.mult, ALU.add, [self.t_ps[2]], [t_rq])
                        self.act(rsq, rsq, AF.Ln, [t_rq], [t_rq])
                        self.act(rsq, rsq, AF.Exp, [t_rq], [t_rq], scale=-0.5)
                        for c4 in range(4):
                            self.stt(C["cT"][:, c4, :], zT[:, c4, :], self.g_qa[:, c4:c4 + 1], rsq, ALU.mult, ALU.mult,
                                     [t_z[c4], t_rq, self.t_const], [T["cT"][c4]])
                    if ck == 5:
                        self.ts(rskv, self.psum[3][:], 1.0 / 256, EPS, ALU.mult, ALU.add, [self.t_ps[3]], [t_rkv])
                        self.act(rskv, rskv, AF.Ln, [t_rkv], [t_rkv])
                        self.act(rskv, rskv, AF.Exp, [t_rkv], [t_rkv], scale=-0.5)
                        for c2 in range(2):
                            self.stt(C["cT"][:, 4 + c2, :], zT[:, 4 + c2, :], self.g_kva[:, c2:c2 + 1], rskv, ALU.mult,
                                     ALU.mult, [t_z[4 + c2], t_rkv, self.t_const], [T["cT"][4 + c2]])
                pending = _post
            else:
                h = ck - 6
                pv = self.psum[b][:].rearrange("p (t a w) -> p t a w", a=2, w=64)
                lo = C["gqlo"][:, h, :].rearrange("p (t a w) -> p t a w", a=2, w=64)
                hi = C["gqhi"][:, h, :].rearrange("p (t a w) -> p t a w", a=2, w=64)
                self.act(lo[:, :, 0, :], pv[:, :, 0, :], AF.Copy, [self.t_ps[b], T["gq0"]], [T["gq"]], join=True,
                         scale=128 ** -0.5)
                self.act(hi[:, :, 1, :], pv[:, :, 1, :], AF.Copy, [self.t_ps[b], T["gq0"]], [T["gq"]], join=True,
                         scale=128 ** -0.5)
        wzg = self.abf(56, [16, 16]); t_wzg = t_w[0]
        P.dma("pool", wzg, din["win_zg"], writes=[t_wzg])
        for c in range(16):
            self.mm(self.psum[4][0:16, :], wzg[:, c, :], hT[:, c, :], c == 0, c == 15, [t_wzg, t_hT[c]],
                    [self.t_ps[4]], join=(c > 0))
        self.copy("act", C["zgT"][0:16, :], self.psum[4][0:16, :], [self.t_ps[4], T["zgT"]], [T["zgT"]])
        P.end_stage()

    def mix_p2(self, g, C, T):
        P, din = self.P, self.din
        hT = self.abf(40, [16, GT]); t_hT = T["hT"]
        wtm = [self.abf(56, [16, 256]), self.abf(64, [16, 256])]; t_w = tiles("wtm", 2)
        n = 0
        for cg in range(10):
            sl = cg % 2
            P.dma("pool", wtm[sl], din["win_tm"][cg], writes=[t_w[sl]])
            for tt in range(4):
                b = n % 4
                n += 1
                for c in range(16):
                    self.mm(self.psum[b][:, 0:256], hT[:, c, tt * 128:(tt + 1) * 128], wtm[sl][:, c, :], c == 0,
                            c == 15, [t_w[sl], t_hT[c]], [self.t_ps[b]], join=(c > 0))
                src = self.psum[b][:, 0:256]
                if cg < 2:
                    self.copy("act" if n % 2 else "dve", C["gk"][:, tt, cg * 256:(cg + 1) * 256], src,
                              [self.t_ps[b]], [T["gk"][tt]], join=True)
                elif cg < 6:
                    o = (cg - 2) * 256
                    self.copy("act" if n % 2 else "dve", C["gv"][:, tt, o:o + 256], src, [self.t_ps[b]],
                              [T["gv"][tt]], join=True)
                else:
                    o = (cg - 6) * 256
                    self.act(C["szr"][:, tt, o:o + 256], src, AF.Silu, [self.t_ps[b]], [T["szr"][tt]], join=True)
        wkr = self.abf(72, [16, 64]); t_wkr = Tile("wkr")
        P.dma("pool", wkr, din["win_kr"], writes=[t_wkr])
        for tt in range(4):
            b = 4 + tt % 2
            for c in range(16):
                self.mm(self.psum[b][:, 0:64], hT[:, c, tt * 128:(tt + 1) * 128], wkr[:, c, :], c == 0, c == 15,
                        [t_wkr, t_hT[c]], [self.t_ps[b]], join=(c > 0))
            self.copy("dve", C["zkr"][:, tt, :], self.psum[b][:, 0:64], [self.t_ps[b]], [T["zkr"]], join=True)

    def mix_p3a(self, g, C, T):
        P, din = self.P, self.din
        wkv = self.abf(74, [2, 2048]); t_wkv = Tile("wkv")
        P.dma("pool", wkv, din["wkv_up"], writes=[t_wkv])
        junk = self.af32(82, [128]); t_junk = Tile("junk")
        ssk = self.af32(82.5, [4, 8]); sskr = self.af32(82.625, [4]); t_ssk = Tile("ssk")
        kr = self.af32(82.75, [4, 64]); t_kr = Tile("kr")
        tmp = [self.af32(83.75 + 0.5 * i, [4, 32]) for i in range(2)]; t_tmp = tiles("rtmp", 2)
        krb = self.abf(84.75, [4, 64]); t_krb = Tile("krb")
        va = self.Vaug[:].rearrange("p t (h w) -> p t h w", w=VW)
        cT = C["cT"]
        n = 0
        self.memset("dve", ssk, 0.0, [], [t_ssk])
        self.memset("dve", sskr, 0.0, [], [t_ssk], join=True)
        for tt in range(4):
            Tg = g * 4 + tt
            for cgi in range(4):
                b = n % 4
                n += 1
                for c in range(2):
                    self.mm(self.psum[b][:], cT[:, 4 + c, tt * 128:(tt + 1) * 128], wkv[:, c, cgi * 512:(cgi + 1) * 512],
                            c == 0, c == 1, [t_wkv, T["cT"][4 + c]], [self.t_ps[b]], join=(c > 0))
                for hh in range(2):
                    h = cgi * 2 + hh
                    self.sq_acc(junk, self.psum[b][:, hh * 256:hh * 256 + 128], ssk[:, tt, h:h + 1], [self.t_ps[b]],
                                t_ssk)
                self.copy("act", va[:, Tg, cgi * 2:cgi * 2 + 2, 0:128],
                          self.psum[b][:].rearrange("p (h w) -> p h w", w=256)[:, :, 128:256],
                          [self.t_ps[b]], [self.t_K], join=True)
            self.sq_acc(junk[:, 0:64], C["zkr"][:, tt, :], sskr[:, tt:tt + 1], [T["zkr"]], t_ssk)
        self.tt(ssk, ssk, sskr.unsqueeze(2).to_broadcast([128, 4, 8]), ALU.add, [t_ssk], [t_ssk])
        self.rsqrt_inplace(ssk, t_ssk, 1.0 / 192, EPS)
        self.ts(self.rk[:, g * 4:(g + 1) * 4, :], ssk, 192 ** -0.5, None, ALU.mult, None, [t_ssk], [self.t_K],
                join=True)
        self.tt(kr, C["zkr"], self.gk_b[:, 128:192].unsqueeze(1).to_broadcast([128, 4, 64]), ALU.mult,
                [T["zkr"], self.t_const], [t_kr])
        cs = self.cosT[:, g * 4:(g + 1) * 4, :]
        sn = self.sinT[:, g * 4:(g + 1) * 4, :]
        x1, x2 = kr[:, :, 0:32], kr[:, :, 32:64]
        self.tt(tmp[0], x1, cs, ALU.mult, [t_kr], [t_tmp[0]])
        self.tt(tmp[1], x2, sn, ALU.mult, [t_kr], [t_tmp[1]])
        self.tt(krb[:, :, 0:32], tmp[0], tmp[1], ALU.subtract, [t_tmp[0], t_tmp[1]], [t_krb])
        self.tt(tmp[0], x2, cs, ALU.mult, [t_kr], [t_tmp[0]])
        self.tt(tmp[1], x1, sn, ALU.mult, [t_kr], [t_tmp[1]])
        self.tt(krb[:, :, 32:64], tmp[0], tmp[1], ALU.add, [t_tmp[0], t_tmp[1]], [t_krb], join=True)
        pb = self.psbf(4)
        for tt in range(4):
            self.tr(pb[0:64, tt * 128:(tt + 1) * 128], krb[:, tt, :], self.ident_b[:], [t_krb], [self.t_ps[4]],
                    join=(tt > 0))
        self.copy("act", self.KTr[0:64, g * GT:(g + 1) * GT], pb[0:64, 0:512], [self.t_ps[4]], [self.t_K], join=True)
        for h in range(8):
            b = 5 + h % 3
            for c in range(2):
                self.mm(self.psum[b][:], wkv[:, c, h * 256:h * 256 + 128], cT[:, 4 + c, :], c == 0, c == 1,
                        [t_wkv, T["cT"][4 + c]], [self.t_ps[b]], join=(c > 0))
            self.copy("act" if h % 2 else "dve", self.KTn[:, h, g * GT:(g + 1) * GT], self.psum[b][:],
                      [self.t_ps[b]], [self.t_K], join=True)
        P.end_stage()

    def mix_p3b(self, g, C, T):
        P, din = self.P, self.din
        wq = self.abf(56, [4, 1536]); t_wq = Tile("wq")
        P.dma("pool", wq, din["wq_up"], writes=[t_wq])
        qf = [self.af32(68, [8, 192]), self.af32(74, [8, 192])]; t_qf = tiles("qf", 2)
        qb = self.abf(80, [8, 192]); t_qb = Tile("qb")
        tmp = [self.af32(83, [8, 32]), self.af32(84, [8, 32])]; t_tmp = tiles("qtmp", 2)
        junk = self.af32(33, [192])
        ssq = [self.af32(33.75, [8]), self.af32(33.78125, [8])]; t_ssq = tiles("ssq", 2)
        cT = C["cT"]

        def A_pe(tt):
            for cg in range(3):
                for c in range(4):
                    self.mm(self.psum[cg][:], cT[:, c, tt * 128:(tt + 1) * 128], wq[:, c, cg * 512:(cg + 1) * 512],
                            c == 0, c == 3, [t_wq, T["cT"][c]], [self.t_ps[cg]], join=(c > 0))

        def A_post(tt):
            q, tq, sq, tsq = qf[tt % 2], t_qf[tt % 2], ssq[tt % 2], t_ssq[tt % 2]
            qff = q.rearrange("p h w -> p (h w)")
            for cg in range(3):
                self.copy("act" if cg == 1 else "dve", qff[:, cg * 512:(cg + 1) * 512], self.psum[cg][:],
                          [self.t_ps[cg]], [tq], join=(cg > 0))
            self.memset("dve", sq, 0.0, [], [tsq])
            for h in range(8):
                self.sq_acc(junk, q[:, h, :], sq[:, h:h + 1], [tq], tsq)
            self.ts(sq, sq, 1.0 / 192, EPS, ALU.mult, ALU.add, [tsq], [tsq])
            self.act(sq, sq, AF.Ln, [tsq], [tsq])
            self.act(sq, sq, AF.Exp, [tsq], [tsq], scale=-0.5)

        def B(tt):
            Tg = g * 4 + tt
            q, tq, sq, tsq = qf[tt % 2], t_qf[tt % 2], ssq[tt % 2], t_ssq[tt % 2]
            self.tt(q, q, sq.unsqueeze(2).to_broadcast([128, 8, 192]), ALU.mult, [tq, tsq], [tq])
            self.tt(q, q, self.gqk_b[:].unsqueeze(1).to_broadcast([128, 8, 192]), ALU.mult, [tq, self.t_const], [tq])
            cs = self.cosT[:, Tg, :].unsqueeze(1).to_broadcast([128, 8, 32])
            sn = self.sinT[:, Tg, :].unsqueeze(1).to_broadcast([128, 8, 32])
            x1, x2 = q[:, :, 128:160], q[:, :, 160:192]
            self.copy("act", qb[:, :, 0:128], q[:, :, 0:128], [tq], [t_qb])
            self.tt(tmp[0], x1, cs, ALU.mult, [tq], [t_tmp[0]])
            self.tt(tmp[1], x2, sn, ALU.mult, [tq], [t_tmp[1]])
            self.tt(qb[:, :, 128:160], tmp[0], tmp[1], ALU.subtract, [t_tmp[0], t_tmp[1]], [t_qb], join=True)
            self.tt(tmp[0], x2, cs, ALU.mult, [tq], [t_tmp[0]])
            self.tt(tmp[1], x1, sn, ALU.mult, [tq], [t_tmp[1]])
            self.tt(qb[:, :, 160:192], tmp[0], tmp[1], ALU.add, [t_tmp[0], t_tmp[1]], [t_qb], join=True)

        def Ct(tt):
            for hq in range(2):
                b = 3 + hq
                pb = self.psbf(b)
                for k in range(4):
                    h = hq * 4 + k
                    self.tr(pb[:, k * 128:(k + 1) * 128], qb[:, h, 0:128], self.ident_b[:], [t_qb], [self.t_ps[b]],
                            join=(k > 0))
                self.copy("act" if hq else "dve", C["qTn"][:, hq * 4:(hq + 1) * 4, tt * 128:(tt + 1) * 128],
                          pb[:, 0:512].rearrange("p (k t) -> p k t", t=128), [self.t_ps[b]], [T["qT"]], join=True)
            pb = self.psbf(5)
            for h in range(8):
                self.tr(pb[0:64, h * 128:(h + 1) * 128], qb[:, h, 128:192], self.ident_b[:], [t_qb], [self.t_ps[5]],
                        join=(h > 0))
            self.copy("dve", C["qTr"][0:64, :, tt * 128:(tt + 1) * 128],
                      pb[0:64, :].rearrange("p (k t) -> p k t", t=128), [self.t_ps[5]], [T["qT"]], join=True)

        A_pe(0)
        A_post(0)
        for tt in range(4):
            if tt < 3:
                A_pe(tt + 1)
            B(tt)
            if tt < 3:
                A_post(tt + 1)
            Ct(tt)
        P.end_stage()

    def mla(self, g, C, T):
        P = self.P
        pT = [self.abf(72 + i, [GT]) for i in range(4)]; t_pT = tiles("pT", 4)
        rec = [self.af32(76, [GT]), self.af32(78, [GT])]; t_rec = tiles("rec", 2)
        qTn, qTr = C["qTn"], C["qTr"]
        nkt = 4 * g + 4
        its = [(h, kt) for h in range(8) for kt in range(nkt)]

        def S(i):
            h, kt = its[i]
            off = max(0, kt - 4 * g) * 128
            bs, sl = i % 3, i % 4
            self.mm(self.psum[bs][:, off:512], self.KTn[:, h, kt * 128:(kt + 1) * 128], qTn[:, h, off:512],
                    True, False, [self.t_K, T["qT"]], [self.t_ps[bs]])
            self.mm(self.psum[bs][:, off:512], self.KTr[0:64, kt * 128:(kt + 1) * 128], qTr[0:64, h, off:512],
                    False, True, [self.t_K, T["qT"]], [self.t_ps[bs]], join=True)
            self.act(pT[sl][:, off:512], self.psum[bs][:, off:512], AF.Exp, [self.t_ps[bs], self.t_K],
                     [t_pT[sl]], scale=self.rk[:, kt, h:h + 1])
            if kt >= 4 * g:
                self.memset("pool", pT[sl][64:128, off:off + 64], 0.0, [t_pT[sl]], [t_pT[sl]])

        def V(i):
            h, kt = its[i]
            off = max(0, kt - 4 * g) * 128
            sl = i % 4
            bo, bsum = 4 + 2 * (h % 2), 5 + 2 * (h % 2)
            first, last = kt == 0, kt == nkt - 1
            self.mm(self.psum[bo][:, off:512], self.Vaug[:, kt, h * VW:h * VW + 128], pT[sl][:, off:512],
                    first, last, [t_pT[sl], self.t_K], [self.t_ps[bo]], join=not first)
            self.mm(self.psum[bsum][:, off:512], self.ones_b[:], pT[sl][:, off:512],
                    first, last, [t_pT[sl]], [self.t_ps[bsum]], join=not first)
            if last:
                r = rec[h % 2]
                P.op("dve", lambda e, o=r, i_=self.psum[bsum][:]: e.reciprocal(o, i_),
                     reads=[self.t_ps[bsum]], writes=[t_rec[h % 2]])
                self.tt(C["mixT"][:, h, :], self.psum[bo][:], r, ALU.mult, [self.t_ps[bo], t_rec[h % 2]],
                        [T["mixT"]], join=True)

        n = len(its)
        for i in range(n + 1):
            if i < n:
                S(i)
            if i >= 1:
                V(i - 1)
        P.end_stage()

    def gla(self, g, C, T):
        P = self.P
        la = self.af32(40, [4, 512]); t_la = tiles("la", 4)
        kdec = self.abf(48, [4, 512]); t_kd = tiles("kdec", 4)
        eb = [self.af32(52, [512]), self.af32(54, [512])]; t_eb = tiles("eb", 2)
        Sb = [self.abf(72 + i, [2, 256]) for i in range(4)]; t_Sb = tiles("Sb", 4)
        og = self.af32(76, [4, 256]); t_og = Tile("og")
        ogb = [self.abf(80, [1024]), self.abf(82, [1024])]; t_ogb = tiles("ogb", 2)
        dec = self.af32(84, [4, 4, 2]); t_dec = Tile("dec")
        sso = self.af32(84.25, [4]); t_sso = Tile("sso"); t_sso_h = tiles("ssoh", 4)
        junk = self.af32(85, [256]); t_junk = Tile("junkg")
        gqlo, gqhi, gk, gv, szr, zgT = C["gqlo"], C["gqhi"], C["gk"], C["gv"], C["szr"], C["zgT"]
        for tt in range(4):
            b = tt % 2
            self.mm(self.psum[b][:], zgT[0:17, tt * 128:(tt + 1) * 128], self.wg2[0:17, :], True, True,
                    [T["zgT"], self.t_const], [self.t_ps[b]])
            self.act(eb[b], self.psum[b][:], AF.Exp, [self.t_ps[b]], [t_eb[b]], scale=-1.0)
            self.ts(eb[b], eb[b], 1.0, None, ALU.add, None, [t_eb[b]], [t_eb[b]])
            self.act(la[:, tt, :], eb[b], AF.Ln, [t_eb[b]], [t_la[tt]])
        lhi = [self.abf(34, [512]), self.abf(35, [512])]; llo = [self.abf(36, [512]), self.abf(37, [512])]
        lres = self.af32(38, [512])
        t_lh, t_ll, t_lr = tiles("lhi", 2), tiles("llo", 2), Tile("lres")
        for tt in range(4):
            b = tt % 2
            self.copy("act", lhi[b], la[:, tt, :], [t_la[tt]], [t_lh[b]])
            self.tt(lres, la[:, tt, :], lhi[b], ALU.subtract, [t_la[tt], t_lh[b]], [t_lr])
            self.copy("dve", llo[b], lres, [t_lr], [t_ll[b]])
            self.mm(self.psum[b][:], self.tri_b[:], lhi[b], True, False, [t_lh[b]], [self.t_ps[b]])
            self.mm(self.psum[b][:], self.tri_b[:], llo[b], False, True, [t_ll[b]], [self.t_ps[b]], join=True)
            self.act(eb[b], self.psum[b][:], AF.Exp, [self.t_ps[b]], [t_eb[b]], scale=-1.0 / 16)
            self.tt(kdec[:, tt, :], gk[:, tt, :], eb[b], ALU.mult, [T["gk"][tt], t_eb[b]], [t_kd[tt]])
            for h in range(4):
                o = (tt * 4 + h) * 2
                self.mm(self.psum[2][:, o:o + 2], lhi[b][:, h * 128:(h + 1) * 128], self.ind_b[:], True, False,
                        [t_lh[b]], [self.t_ps[2]], join=(o > 0))
                self.mm(self.psum[2][:, o:o + 2], llo[b][:, h * 128:(h + 1) * 128], self.ind_b[:], False, True,
                        [t_ll[b]], [self.t_ps[2]], join=True)
        self.act(dec.rearrange("p a b c -> p (a b c)"), self.psum[2][:, 0:32], AF.Exp, [self.t_ps[2]], [t_dec],
                 scale=-1.0 / 16)
        Stmp = self.af32(34, [4, 256]); t_St = tiles("Stmp", 4)
        ob_ = [5, 6]

        def U(tt):
            for h in range(4):
                cols = (h % 2) * 256
                for half in range(2):
                    p0 = half * 64
                    bank = (3 if half == 0 else 1) + h // 2
                    self.mm(self.psum[bank][:, cols:cols + 256], kdec[p0:p0 + 64, tt, h * 128:(h + 1) * 128],
                            gv[p0:p0 + 64, tt, h * 256:(h + 1) * 256], True, True, [t_kd[tt], T["gv"][tt]],
                            [self.t_ps[bank]], join=(h % 2 > 0))

        def CH(tt):
            for h in range(4):
                cols = (h % 2) * 256
                b0, b1 = 3 + h // 2, 1 + h // 2
                self.stt(Stmp[:, h, :], self.Sst[:, h, :], dec[:, tt, h, 0:1], self.psum[b0][:, cols:cols + 256],
                         ALU.mult, ALU.add, [self.t_Sh[h], t_dec, self.t_ps[b0]], [t_St[h]])
                self.copy("act", Sb[h][:, 0, :], Stmp[:, h, :], [t_St[h]], [t_Sb[h]])
                self.stt(self.Sst[:, h, :], Stmp[:, h, :], dec[:, tt, h, 1:2], self.psum[b1][:, cols:cols + 256],
                         ALU.mult, ALU.add, [t_St[h], t_dec, self.t_ps[b1]], [self.t_Sh[h]])
                self.copy("act", Sb[h][:, 1, :], self.Sst[:, h, :], [self.t_Sh[h]], [t_Sb[h]], join=True)

        def O(tt):
            for h in range(4):
                bo = ob_[h // 2]
                co = (h % 2) * 256
                self.mm(self.psum[bo][:, co:co + 256], gqlo[:, h, tt * 128:(tt + 1) * 128], Sb[h][:, 0, :], True,
                        False, [T["gq"], t_Sb[h]], [self.t_ps[bo]], join=(h % 2 > 0))
                self.mm(self.psum[bo][:, co:co + 256], gqhi[:, h, tt * 128:(tt + 1) * 128], Sb[h][:, 1, :], False,
                        True, [T["gq"], t_Sb[h]], [self.t_ps[bo]], join=True)

        U(0)
        CH(0)
        dtr_pending = []
        for tt in range(4):
            if tt < 3:
                U(tt + 1)
            O(tt)
            while dtr_pending:
                dtr_pending.pop(0)()
            if tt < 3:
                CH(tt + 1)
            self.memset("dve", sso, 0.0, [], [t_sso])
            for h in range(4):
                bo = ob_[h // 2]
                co = (h % 2) * 256
                self.sq_acc(junk, self.psum[bo][:, co:co + 256], sso[:, h:h + 1], [self.t_ps[bo]], t_sso)
            self.ts(sso, sso, 1.0 / 256, EPS, ALU.mult, ALU.add, [t_sso], [t_sso])
            self.act(sso, sso, AF.Ln, [t_sso], [t_sso])
            self.act(sso, sso, AF.Exp, [t_sso], [t_sso], scale=-0.5)
            for hp in range(2):
                self.tt(og[:, hp * 2:hp * 2 + 2, :], self.psum[ob_[hp]][:].rearrange("p (h w) -> p h w", w=256),
                        sso[:, hp * 2:hp * 2 + 2].unsqueeze(2).to_broadcast([128, 2, 256]), ALU.mult,
                        [self.t_ps[ob_[hp]], t_sso], [t_og], join=(hp > 0))
            self.tt(og, og, self.gout_b[:].unsqueeze(1).to_broadcast([128, 4, 256]), ALU.mult, [t_og, self.t_const],
                    [t_og])
            o_b = ogb[tt % 2]
            self.tt(o_b, og.rearrange("p h w -> p (h w)"), szr[:, tt, :], ALU.mult, [t_og, T["szr"][tt]],
                    [t_ogb[tt % 2]])
            def Dtr(tt=tt, o_b=o_b):
                b = 7 if tt % 2 else 0
                pb = self.psbf(b)
                for c in range(8):
                    self.tr(pb[:, c * 128:(c + 1) * 128], o_b[:, c * 128:(c + 1) * 128], self.ident_b[:],
                            [t_ogb[tt % 2]], [self.t_ps[b]], join=(c > 0))
                self.copy("act", C["mixT"][:, 8:16, tt * 128:(tt + 1) * 128],
                          pb[:, :].rearrange("p (k t) -> p k t", t=128), [self.t_ps[b]], [T["mixT"]], join=True)
            dtr_pending.append(Dtr)
        for f_ in dtr_pending:
            f_()
        P.end_stage()

    def wout(self, g, C, T):
        P, din = self.P, self.din
        wo = [self.abf(4 * i, [16, 128]) for i in range(3)]; t_wo = tiles("wo", 3)
        mixT = C["mixT"]
        for co in range(16):
            sl = co % 3
            P.dma("pool", wo[sl], din["w_out"][co], writes=[t_wo[sl]])
            b = co % 2
            for ci in range(16):
                self.mm(self.psum[b][:], wo[sl][:, ci, :], mixT[:, ci, :], ci == 0, ci == 15, [t_wo[sl], T["mixT"]],
                        [self.t_ps[b]], join=(ci > 0))
            self.tt(self.xT[:, co, :], self.psum[b][:], self.xT[:, co, :], ALU.add, [self.t_ps[b], self.t_x[co]],
                    [self.t_x[co]])
        P.end_stage()

    def cross(self, g):
        P, din = self.P, self.din
        hT = self.abf(0, [16, GT]); t_hT = tiles("hTc", 16)
        wq = [self.abf(16 + 4 * i, [16, 128]) for i in range(3)]; t_wq = tiles("mwq", 3)
        qmz = [self.af32(28, [GT]), self.af32(30, [GT])]; t_qmz = tiles("qmz", 2)
        qmT = self.abf(36, [4, GT]); t_qm = tiles("qmT", 4)
        pT = [self.abf(40 + i, [GT]) for i in range(4)]; t_pT = tiles("pTc", 4)
        rec = [self.af32(44, [GT]), self.af32(46, [GT])]; t_rec = tiles("recc", 2)
        omT = self.abf(48, [4, GT]); t_omT = tiles("omT", 4)
        wmo = [self.abf(52 + i, [4, 128]) for i in range(3)]; t_wmo = tiles("wmo", 3)
        sqb = [self.abf(56, [GT]), self.abf(57, [GT])]; t_sq = tiles("sqc", 2)
        rs = [self.af32(58, [GT]), self.af32(60, [GT])]; t_rs = tiles("rsc", 2)
        for h in range(3):
            P.dma("pool", wq[h], din["mwq"][h], writes=[t_wq[h]])
        for co in range(3):
            P.dma("pool", wmo[co], din["mwo"][co], writes=[t_wmo[co]])
        self.norm(self.g_ma, hT, t_hT, 62, 64)

        def proj(h):
            sl, b = h % 3, h % 2
            for c in range(16):
                self.mm(self.psum[b][:], wq[sl][:, c, :], hT[:, c, :], c == 0, c == 15, [t_wq[sl], t_hT[c]],
                        [self.t_ps[b]], join=(c > 0))
            self.copy("act", qmz[b], self.psum[b][:], [self.t_ps[b]], [t_qmz[b]])
            self.act(sqb[b], self.psum[b][:], AF.Square, [self.t_ps[b]], [t_sq[b]])
            if h == 0:
                P.dma("pool", wq[0], din["mwq"][3], writes=[t_wq[0]])

        def qnorm(h):
            b = h % 2
            self.mm(self.psum[2 + b][:], self.ones_b[:], sqb[b], True, True, [t_sq[b]], [self.t_ps[2 + b]])
            self.ts(rs[b], self.psum[2 + b][:], 1.0 / 128, EPS, ALU.mult, ALU.add, [self.t_ps[2 + b]], [t_rs[b]])
            self.act(rs[b], rs[b], AF.Ln, [t_rs[b]], [t_rs[b]])
            self.act(rs[b], rs[b], AF.Exp, [t_rs[b]], [t_rs[b]], scale=-0.5)
            self.stt(qmT[:, h, :], qmz[b], self.g_mq[:, 0:1], rs[b], ALU.mult, ALU.mult,
                     [t_qmz[b], t_rs[b], self.t_const], [t_qm[h]])

        for h in range(5):
            if h < 4:
                proj(h)
            if h >= 1:
                qnorm(h - 1)

        its = [(h, mt) for h in range(4) for mt in range(2)]

        def S(i):
            h, mt = its[i]
            bs, sl = i % 2, i % 4
            self.mm(self.psum[bs][:], self.mKT[:, h, mt * 128:(mt + 1) * 128], qmT[:, h, :], True, True,
                    [self.t_mem, t_qm[h]], [self.t_ps[bs]])
            self.act(pT[sl], self.psum[bs][:], AF.Exp, [self.t_ps[bs]], [t_pT[sl]], scale=128 ** -0.5)

        def V(i):
            h, mt = its[i]
            sl = i % 4
            bo, bsum = 4 + 2 * (h % 2), 5 + 2 * (h % 2)
            self.mm(self.psum[bo][:], self.mV[:, mt, h * VW:h * VW + 128], pT[sl], mt == 0, mt == 1,
                    [t_pT[sl], self.t_mem], [self.t_ps[bo]], join=(mt > 0))
            self.mm(self.psum[bsum][:], self.ones_b[:], pT[sl], mt == 0, mt == 1, [t_pT[sl]], [self.t_ps[bsum]],
                    join=(mt > 0))
            if mt == 1:
                r = rec[h % 2]
                P.op("dve", lambda e, o=r, i_=self.psum[bsum][:]: e.reciprocal(o, i_),
                     reads=[self.t_ps[bsum]], writes=[t_rec[h % 2]])
                self.tt(omT[:, h, :], self.psum[bo][:], r, ALU.mult, [self.t_ps[bo], t_rec[h % 2]], [t_omT[h]])

        for i in range(len(its) + 1):
            if i < len(its):
                S(i)
            if i >= 1:
                V(i - 1)
        for co in range(16):
            sl = co % 3
            b = 2 + co % 2
            for ci in range(4):
                self.mm(self.psum[b][:], wmo[sl][:, ci, :], omT[:, ci, :], ci == 0, ci == 3, [t_wmo[sl], t_omT[ci]],
                        [self.t_ps[b]], join=(ci > 0))
            self.tt(self.xT[:, co, :], self.psum[b][:], self.xT[:, co, :], ALU.add, [self.t_ps[b], self.t_x[co]],
                    [self.t_x[co]])
            if co + 3 < 16:
                P.dma("pool", wmo[sl], din["mwo"][co + 3], writes=[t_wmo[sl]])
        P.end_stage()

    def build(self):
        self.prologue()
        order = ["x", "ffn1", "mix", "cross", "ffn2"]
        lvl = order.index(self.upto) - 1
        for g in range(self.ngroups):
            if g == 0:
                self.load_x(g)
            if lvl >= 0:
                self.ffn(1)
            if lvl >= 1:
                C = self.carried()
                T = {"hT": tiles("hTm", 16), "gq": Tile("gq"), "gq0": Tile("gq0"), "zgT": Tile("zgT"), "gk": tiles("gk", 4),
                     "gv": tiles("gv", 4), "szr": tiles("szr", 4), "zkr": Tile("zkr"), "cT": tiles("cT", 6),
                     "qT": Tile("qT"), "mixT": Tile("mixT")}
                sub = getattr(self, "sub", 99)
                self.mix_p1(g, C, T)
                self.mix_p2(g, C, T)
                self.mix_p3a(g, C, T)
                if sub >= 4:
                    self.mix_p3b(g, C, T)
                if sub >= 5:
                    self.mla(g, C, T)
                if sub >= 6:
                    self.gla(g, C, T)
                if sub >= 7:
                    self.wout(g, C, T)
            if lvl >= 2:
                self.cross(g)
            if lvl >= 3:
                self.ffn(2)
            if g + 1 < self.ngroups:
                self.store_x(g, end=False)
                self.load_x(g + 1, base=16, end=True)
            else:
                self.store_x(g)
        self.P.close()
        return self.nc


def _chunk_rows(w, ncols_chunk):
    Kd, N = w.shape
    return np.ascontiguousarray(w.reshape(Kd // 128, 128, N // ncols_chunk, ncols_chunk).transpose(2, 1, 0, 3))


def _rows(w):
    Kd, N = w.shape
    return np.ascontiguousarray(w.reshape(Kd // 128, 128, N).transpose(1, 0, 2))


def _col(g):
    return np.ascontiguousarray(g.reshape(-1, 128).T)


def prep_shared(inp):
    f32 = np.float32
    sh = {}
    for k in (1, 2):
        wg = _chunk_rows(np.asarray(inp[f"ffn{k}_w_gate"][0], f32), 128)
        wu = _chunk_rows(np.asarray(inp[f"ffn{k}_w_up"][0], f32), 128)
        sh[f"f{k}_gu"] = np.ascontiguousarray(np.stack([wg, wu], axis=2))
        wd = np.asarray(inp[f"ffn{k}_w_down"][0], f32)
        sh[f"f{k}_dn"] = np.ascontiguousarray(wd.reshape(NQ, JQ, 128, 16, 128).transpose(0, 3, 2, 1, 4))
        sh[f"f{k}_g"] = _col(np.asarray(inp[f"ffn{k}_norm"][0], f32))
    win = np.asarray(inp["w_in"][0], f32)
    fm_cols = np.concatenate([win[:, O_ZQ:O_ZQ + 512], win[:, O_ZKV:O_ZKV + 256], win[:, O_GQ:O_GQ + 512]], axis=1)
    sh["win_fm"] = _chunk_rows(fm_cols, 128)
    sh["win_zg"] = _rows(win[:, O_ZG:O_ZG + 16])
    tm_cols = np.concatenate([win[:, O_GK:O_GK + 512], win[:, O_GV:O_GV + 1024], win[:, O_ZR:O_ZR + 1024]], axis=1)
    sh["win_tm"] = _chunk_rows(tm_cols, 256)
    sh["win_kr"] = _rows(win[:, O_ZKR:O_ZKR + 64])
    sh["mix_g"] = _col(np.asarray(inp["mix_norm"][0], f32))
    sh["qa_g"] = _col(np.asarray(inp["q_a_norm"][0], f32))
    sh["kva_g"] = _col(np.asarray(inp["kv_a_norm"][0], f32))
    sh["wq_up"] = _rows(np.asarray(inp["w_q_up"][0], f32))
    sh["wkv_up"] = _rows(np.asarray(inp["w_kv_up"][0], f32))
    sh["gq_row"] = np.asarray(inp["mla_q_norm"][0], f32).reshape(1, 192).copy()
    sh["gk_row"] = np.asarray(inp["mla_k_norm"][0], f32).reshape(1, 192).copy()
    sh["wg2"] = np.ascontiguousarray(np.concatenate([np.asarray(inp["gla_w_gate2"][0], f32),
                                                     np.asarray(inp["gla_b_gate"][0], f32).reshape(1, 512)], axis=0))
    sh["gout_row"] = np.asarray(inp["gla_out_norm"][0], f32).reshape(1, 256).copy()
    sh["w_out"] = _chunk_rows(np.asarray(inp["w_out"][0], f32), 128)
    sh["ma_g"] = _col(np.asarray(inp["mem_attn_norm"][0], f32))
    sh["mn_g"] = _col(np.asarray(inp["mem_norm"][0], f32))
    sh["mwq"] = _chunk_rows(np.asarray(inp["mem_w_q"][0], f32), 128)
    sh["mwk"] = _chunk_rows(np.asarray(inp["mem_w_k"][0], f32), 128)
    sh["mwv"] = _rows(np.asarray(inp["mem_w_v"][0], f32))
    sh["mwo"] = _chunk_rows(np.asarray(inp["mem_w_o"][0], f32), 128)
    sh["mq_g"] = np.asarray(inp["mem_q_norm"][0], f32).reshape(128, 1).copy()
    sh["mk_g"] = np.asarray(inp["mem_k_norm"][0], f32).reshape(128, 1).copy()
    half = 32
    sh["invf"] = (np.float32(10000.0) ** (-np.arange(half, dtype=np.float32) / np.float32(half))).astype(f32).reshape(1, 32)
    return sh


def core_inputs(inp, b, sh):
    m = dict(sh)
    m["x"] = np.ascontiguousarray(np.asarray(inp["x"][b], np.float32))
    m["mem"] = np.ascontiguousarray(np.asarray(inp["mem"][b], np.float32))
    m["pos"] = np.ascontiguousarray(np.asarray(inp["positions"][b], np.int32).reshape(16, 128).T)
    return m


def kernel(**inputs):
    B = inputs["x"].shape[0]
    sh = prep_shared(inputs)
    nc = bass.Bass("TRN2", target_bir_lowering=False)
    K(nc).build()
    in_maps = [core_inputs(inputs, b, sh) for b in range(B)]
    res = run_bass_kernel_spmd(nc, in_maps, core_ids=list(range(B)))
    return np.stack([r["y"] for r in res.results], axis=0).astype(np.float32)
```
